# Optimizing a Trainium2 kernel written in Bass

```python
import math
import jax, jax.numpy as jnp
from jax import lax
import numpy as np

D_MODEL = 1024
BATCH = 8
SEQ = 8192
DEPTH = 1

HEAD_DIM = 64
ATTN_WIDTH = D_MODEL // 2
ATTN_HEADS = ATTN_WIDTH // HEAD_DIM
KV_HEADS = 2
GQA_GROUP = ATTN_HEADS // KV_HEADS
ROT_DIM = HEAD_DIM // 4
ROPE_THETA = 500000.0
WINDOW = 128
BLOCK = 128
SSM_INNER = D_MODEL - ATTN_WIDTH
SSM_HEAD_DIM = 64
SSM_HEADS = SSM_INNER // SSM_HEAD_DIM
SSM_GROUPS = 2
SSM_HPG = SSM_HEADS // SSM_GROUPS
D_STATE = 128
SSM_CONV = 5
CHUNK = 128
D_FF = ((8 * D_MODEL // 3 + 127) // 128) * 128
FFN_CONV = 3
EPS = 1e-5
NEG = -1e30

Q_COLS = ATTN_HEADS * HEAD_DIM
KV_COLS = KV_HEADS * HEAD_DIM
XBC_COLS = SSM_INNER + 2 * SSM_GROUPS * D_STATE
DT_COLS = 2 * SSM_HEADS
IN_COLS = Q_COLS + 2 * KV_COLS + SSM_INNER + XBC_COLS + DT_COLS
IN_SPLITS = (Q_COLS, Q_COLS + KV_COLS, Q_COLS + 2 * KV_COLS,
             Q_COLS + 2 * KV_COLS + SSM_INNER,
             Q_COLS + 2 * KV_COLS + SSM_INNER + XBC_COLS)

kernel_name = "hymba_ssd_swa_convffn_encoder"


def rmsnorm(x, g):
    xf = x.astype(jnp.float32)
    y = xf * lax.rsqrt(jnp.mean(xf * xf, axis=-1, keepdims=True) + EPS)
    return (y * g.astype(jnp.float32)).astype(x.dtype)


def dwconv(u, w, b):
    K, C = w.shape
    y = lax.conv_general_dilated(
        u, w[:, None, :].astype(u.dtype), window_strides=(1,),
        padding=[(K // 2, K // 2)], dimension_numbers=("NWC", "WIO", "NWC"),
        feature_group_count=C)
    return y + b.astype(u.dtype)


def rope_tables(S):
    pos = jnp.arange(S, dtype=jnp.float32)
    inv = ROPE_THETA ** (-jnp.arange(0, ROT_DIM, 2, dtype=jnp.float32) / ROT_DIM)
    ang = pos[:, None] * inv[None, :]
    return jnp.cos(ang), jnp.sin(ang)


def apply_partial_rope(t, cos, sin):
    half = ROT_DIM // 2
    c = cos[None, :, None, :].astype(t.dtype)
    s = sin[None, :, None, :].astype(t.dtype)
    t1, t2, rest = t[..., :half], t[..., half:ROT_DIM], t[..., ROT_DIM:]
    return jnp.concatenate([t1 * c - t2 * s, t2 * c + t1 * s, rest], axis=-1)


def windowed_attention(q, k, v, sink, cos, sin):
    Bsz, S, _ = q.shape
    nb = S // BLOCK
    q = apply_partial_rope(q.reshape(Bsz, S, ATTN_HEADS, HEAD_DIM), cos, sin)
    k = apply_partial_rope(k.reshape(Bsz, S, KV_HEADS, HEAD_DIM), cos, sin)
    v = v.reshape(Bsz, S, KV_HEADS, HEAD_DIM)
    qb = q.reshape(Bsz, nb, BLOCK, KV_HEADS, GQA_GROUP, HEAD_DIM)

    def band(t):
        tp = jnp.pad(t, ((0, 0), (BLOCK, BLOCK), (0, 0), (0, 0)))
        tp = tp.reshape(Bsz, nb + 2, BLOCK, KV_HEADS, HEAD_DIM)
        return jnp.concatenate([tp[:, :-2], tp[:, 1:-1], tp[:, 2:]], axis=2)

    kw, vw = band(k), band(v)
    scale = 1.0 / math.sqrt(HEAD_DIM)
    scores = jnp.einsum("bnqkgd,bnskd->bnkgqs", qb, kw).astype(jnp.float32) * scale
    blk = jnp.arange(nb)[:, None, None]
    qpos = blk * BLOCK + jnp.arange(BLOCK)[None, :, None]
    kpos = (blk - 1) * BLOCK + jnp.arange(3 * BLOCK)[None, None, :]
    valid = (jnp.abs(kpos - qpos) <= WINDOW) & (kpos >= 0) & (kpos < S)
    scores = jnp.where(valid[None, :, None, None], scores, NEG)
    sink_l = sink.astype(jnp.float32).reshape(KV_HEADS, GQA_GROUP)[None, None, :, :, None, None]
    m = jnp.maximum(jnp.max(scores, axis=-1, keepdims=True), sink_l)
    p = jnp.exp(scores - m)
    denom = jnp.sum(p, axis=-1, keepdims=True) + jnp.exp(sink_l - m)
    probs = (p / denom).astype(v.dtype)
    o = jnp.einsum("bnkgqs,bnskd->bnqkgd", probs, vw)
    return o.reshape(Bsz, S, ATTN_WIDTH)


def ssd_chunked(xs, dt, a_log, bm, cm):
    Bsz, S = xs.shape[:2]
    nc = S // CHUNK
    A = -jnp.exp(a_log.astype(jnp.float32)).reshape(SSM_GROUPS, SSM_HPG)
    a = (dt * A).reshape(Bsz, nc, CHUNK, SSM_GROUPS, SSM_HPG)
    X = (xs * dt[..., None]).reshape(Bsz, nc, CHUNK, SSM_GROUPS, SSM_HPG, SSM_HEAD_DIM)
    Bc = bm.reshape(Bsz, nc, CHUNK, SSM_GROUPS, D_STATE)
    Cc = cm.reshape(Bsz, nc, CHUNK, SSM_GROUPS, D_STATE)
    acs = jnp.cumsum(a, axis=2)
    idx = jnp.arange(CHUNK)
    lower = (idx[:, None] >= idx[None, :])[None, None, :, :, None, None]
    seg = acs[:, :, :, None] - acs[:, :, None, :]
    L = jnp.exp(jnp.where(lower, seg, -jnp.inf))
    cb = jnp.einsum("bclgn,bcsgn->bclsg", Cc, Bc)
    y_diag = jnp.einsum("bclsgr,bcsgrp->bclgrp", cb[..., None] * L, X)
    decay_states = jnp.exp(acs[:, :, -1:] - acs)
    states = jnp.einsum("bclgn,bclgrp->bcgrpn", Bc, X * decay_states[..., None])
    chunk_decay = jnp.exp(acs[:, :, -1])

    def step(h, inp):
        dec, st = inp
        return h * dec[..., None, None] + st, h

    h0 = jnp.zeros((Bsz, SSM_GROUPS, SSM_HPG, SSM_HEAD_DIM, D_STATE), jnp.float32)
    _, prev = lax.scan(step, h0, (jnp.moveaxis(chunk_decay, 1, 0), jnp.moveaxis(states, 1, 0)))
    prev = jnp.moveaxis(prev, 0, 1)
    y_off = jnp.einsum("bclgn,bcgrpn->bclgrp", Cc, prev) * jnp.exp(acs)[..., None]
    return (y_diag + y_off).reshape(Bsz, S, SSM_GROUPS, SSM_HPG, SSM_HEAD_DIM)


def ssd_mixer(z, xbc, dt_raw, conv_w, conv_b, dt_bias_f, dt_bias_b, a_log_f, a_log_b, d_skip, norm_g):
    Bsz, S, _ = z.shape
    xbc = jax.nn.silu(dwconv(xbc, conv_w, conv_b))
    xs, bm, cm = jnp.split(xbc, [SSM_INNER, SSM_INNER + SSM_GROUPS * D_STATE], axis=-1)
    xs = xs.astype(jnp.float32).reshape(Bsz, S, SSM_GROUPS, SSM_HPG, SSM_HEAD_DIM)
    bm = bm.astype(jnp.float32).reshape(Bsz, S, SSM_GROUPS, D_STATE)
    cm = cm.astype(jnp.float32).reshape(Bsz, S, SSM_GROUPS, D_STATE)
    dt_raw = dt_raw.astype(jnp.float32)
    dt_f = jax.nn.softplus(dt_raw[..., :SSM_HEADS] + dt_bias_f.astype(jnp.float32))
    dt_b = jax.nn.softplus(dt_raw[..., SSM_HEADS:] + dt_bias_b.astype(jnp.float32))
    dt_f = dt_f.reshape(Bsz, S, SSM_GROUPS, SSM_HPG)
    dt_b = dt_b.reshape(Bsz, S, SSM_GROUPS, SSM_HPG)
    flip = lambda t: jnp.flip(t, axis=1)
    y_f = ssd_chunked(xs, dt_f, a_log_f, bm, cm)
    y_b = flip(ssd_chunked(flip(xs), flip(dt_b), a_log_b, flip(bm), flip(cm)))
    y = y_f + y_b + xs * d_skip.astype(jnp.float32).reshape(SSM_GROUPS, SSM_HPG, 1)
    y = y.reshape(Bsz, S, SSM_INNER) * jax.nn.silu(z.astype(jnp.float32))
    yg = y.reshape(Bsz, S, SSM_GROUPS, SSM_INNER // SSM_GROUPS)
    yg = yg * lax.rsqrt(jnp.mean(yg * yg, axis=-1, keepdims=True) + EPS)
    y = yg.reshape(Bsz, S, SSM_INNER) * norm_g.astype(jnp.float32)
    return y.astype(z.dtype)


def setup_inputs(seed: int = 0) -> dict:
    key = jax.random.key(seed)
    ks = jax.random.split(key, 20)
    f32 = jnp.float32

    def nrm(k, shape, scale):
        return jax.random.normal(k, shape, f32) * scale

    def gain(k, shape):
        return 1.0 + 0.02 * jax.random.normal(k, shape, f32)

    def dt_bias(k):
        dt0 = jnp.exp(jax.random.uniform(k, (DEPTH, SSM_HEADS), f32, math.log(1e-3), math.log(1e-1)))
        return dt0 + jnp.log(-jnp.expm1(-dt0))

    def a_log(k):
        return jnp.log(jax.random.uniform(k, (DEPTH, SSM_HEADS), f32, 1.0, 16.0))

    return {
        "x": jax.random.normal(ks[0], (BATCH, SEQ, D_MODEL), f32),
        "norm1_g": gain(ks[1], (DEPTH, D_MODEL)),
        "w_in": nrm(ks[2], (DEPTH, D_MODEL, IN_COLS), D_MODEL ** -0.5),
        "attn_sink": nrm(ks[3], (DEPTH, ATTN_HEADS), 0.5),
        "attn_out_g": gain(ks[4], (DEPTH, ATTN_WIDTH)),
        "ssd_conv_w": nrm(ks[5], (DEPTH, SSM_CONV, XBC_COLS), SSM_CONV ** -0.5),
        "ssd_conv_b": nrm(ks[6], (DEPTH, XBC_COLS), 0.02),
        "ssd_dt_bias_fwd": dt_bias(ks[7]),
        "ssd_dt_bias_bwd": dt_bias(ks[8]),
        "ssd_a_log_fwd": a_log(ks[9]),
        "ssd_a_log_bwd": a_log(ks[10]),
        "ssd_d": gain(ks[11], (DEPTH, SSM_HEADS)),
        "ssd_norm_g": gain(ks[12], (DEPTH, SSM_INNER)),
        "w_out": nrm(ks[13], (DEPTH, ATTN_WIDTH + SSM_INNER, D_MODEL), (ATTN_WIDTH + SSM_INNER) ** -0.5),
        "norm2_g": gain(ks[14], (DEPTH, D_MODEL)),
        "w_up": nrm(ks[15], (DEPTH, D_MODEL, 2 * D_FF), D_MODEL ** -0.5),
        "ffn_conv_w": nrm(ks[16], (DEPTH, FFN_CONV, 2 * D_FF), FFN_CONV ** -0.5),
        "ffn_conv_b": nrm(ks[17], (DEPTH, 2 * D_FF), 0.02),
        "w_down": nrm(ks[18], (DEPTH, D_FF, D_MODEL), D_FF ** -0.5),
        "final_norm_g": gain(ks[19], (D_MODEL,)),
    }


def reference(x, norm1_g, w_in, attn_sink, attn_out_g, ssd_conv_w, ssd_conv_b,
              ssd_dt_bias_fwd, ssd_dt_bias_bwd, ssd_a_log_fwd, ssd_a_log_bwd, ssd_d,
              ssd_norm_g, w_out, norm2_g, w_up, ffn_conv_w, ffn_conv_b, w_down, final_norm_g):
    S = x.shape[1]
    cos, sin = rope_tables(S)
    for l in range(DEPTH):
        h = rmsnorm(x, norm1_g[l])
        proj = h @ w_in[l].astype(h.dtype)
        q, k, v, z, xbc, dt_raw = jnp.split(proj, IN_SPLITS, axis=-1)
        attn = rmsnorm(windowed_attention(q, k, v, attn_sink[l], cos, sin), attn_out_g[l])
        ssd = ssd_mixer(z, xbc, dt_raw, ssd_conv_w[l], ssd_conv_b[l], ssd_dt_bias_fwd[l],
                        ssd_dt_bias_bwd[l], ssd_a_log_fwd[l], ssd_a_log_bwd[l], ssd_d[l], ssd_norm_g[l])
        mixed = jnp.concatenate([attn, ssd], axis=-1)
        x = x + mixed @ w_out[l].astype(mixed.dtype)
        h = rmsnorm(x, norm2_g[l])
        u = dwconv(h @ w_up[l].astype(h.dtype), ffn_conv_w[l], ffn_conv_b[l])
        gate, val = jnp.split(u, 2, axis=-1)
        x = x + (jax.nn.silu(gate) * val) @ w_down[l].astype(h.dtype)
    return rmsnorm(x, final_norm_g)
```

```python
import numpy as np
import ml_dtypes
from contextlib import ExitStack
import concourse.bass as bass
import concourse.mybir as mybir
from concourse.bass_utils import run_bass_kernel_spmd

F32 = mybir.dt.float32
BF16 = mybir.dt.bfloat16
AF = mybir.ActivationFunctionType
ALU = mybir.AluOpType

D = 1024
SEQ = 8192
NCORES = 8
EPS = 1e-5
NEGM = -30000.0
IN_COLS = 2320
DFF = 2816
FT = 382

_c = {}
_o = 0
for _n, _w in [("g1", 8), ("g2", 8), ("ga", 4), ("gs", 4), ("scw", 40), ("scb", 8), ("fcw", 132), ("fcb", 44),
               ("dtb", 1), ("alog", 1), ("sel", 3), ("dskip", 8), ("sink", 8), ("identf", 128)]:
    _c[_n] = (_o, _o + _w)
    _o += _w
CST_COLS = _o
_b = {"identb": (0, 128), "mle": (128, 640), "mge": (640, 1152), "ind": (1152, 1152 + 2048)}
CBF_COLS = 1152 + 2048


class Res:
    __slots__ = ("w", "r", "sem", "cnt")

    def __init__(self):
        self.w = None
        self.r = {}
        self.sem = None
        self.cnt = 0


class Eng:
    def __init__(self, kb, e, sem, is_pe=False):
        self.kb = kb
        self.e = e
        self.sem = sem
        self.cnt = 0
        self.seen = {}
        self.is_pe = is_pe
        self.pend_r = []
        self.pend_w = []

    def wait(self, tok, raw=True):
        if tok is None:
            return
        sem, val = tok
        if self.seen.get(sem, 0) >= val:
            return
        if sem is self.sem and (self.is_pe or not raw):
            return
        self.e.wait_ge(sem, val)
        self.seen[sem] = val

    def deps(self, reads, writes):
        for r in reads:
            self.wait(r.w, raw=True)
        for w in writes:
            self.wait(w.w, raw=True)
            for s, v in w.r.items():
                self.wait((s, v), raw=True)

    @staticmethod
    def mark(tok, reads, writes):
        for r in reads:
            if r.r.get(tok[0], 0) < tok[1]:
                r.r[tok[0]] = tok[1]
        for w in writes:
            w.w = tok
            w.r = {}

    def op(self, fn, r=(), w=(), inc=True):
        self.deps(r, w)
        ins = fn(self.e)
        if not inc:
            self.pend_r += list(r)
            self.pend_w += list(w)
            return None
        self.cnt += 1
        ins.then_inc(self.sem, 1)
        tok = (self.sem, self.cnt)
        self.mark(tok, list(r) + self.pend_r, list(w) + self.pend_w)
        self.pend_r = []
        self.pend_w = []
        return tok

    def mm_group(self, fns, r=(), w=()):
        tok = None
        for i, fn in enumerate(fns):
            tok = self.op(fn, r=r, w=w, inc=(i == len(fns) - 1))
        return tok

    def dma(self, out, in_, semres, r=(), w=(), slow=False):
        self.deps(r, w)
        if semres.sem is None:
            semres.sem = self.kb.new_sem()
        ins = self.e.dma_start(out=out, in_=in_, allow_slow_non_contiguous=True) if slow else self.e.dma_start(out=out, in_=in_)
        semres.cnt += 16
        ins.then_inc(semres.sem, 16)
        tok = (semres.sem, semres.cnt)
        self.mark(tok, r, w)
        return tok


class KB:
    def __init__(self, S, dbg=False, phases="A1,D,AB,C,F"):
        self.S = S
        self.nb = S // 128
        self.dbg = dbg
        self.phases = phases.split(",")
        self.nc = bass.Bass("TRN2", target_bir_lowering=False)
        self.es = ExitStack()
        self.nsem = 0
        self.alltok_res = []

    def new_sem(self):
        self.nsem += 1
        return self.es.enter_context(self.nc.semaphore("s%d" % self.nsem))

    def tile(self, es, name, shape, dt):
        t = es.enter_context(self.nc.sbuf_tensor("t_" + name, list(shape), dt))
        r = Res()
        self.alltok_res.append(r)
        return t, r

    def dram(self, name, shape, dt, kind="Internal"):
        return self.nc.dram_tensor(name, list(shape), dt, kind=kind).ap()

    def barrier(self):
        toks = [(e.sem, e.cnt) for e in self.engs if e.cnt > 0]
        for r in self.alltok_res + self.dram_res:
            if r.sem is not None and r.cnt > 0:
                toks.append((r.sem, r.cnt))
        for e in self.engs:
            for t in toks:
                e.wait(t)

    def build(self):
        nc, S, nb = self.nc, self.S, self.nb
        es = self.es
        okind = "ExternalOutput" if self.dbg else "Internal"
        self.x = self.dram("x", [S, D], F32, "ExternalInput")
        self.w_in = self.dram("w_in", [D, IN_COLS], F32, "ExternalInput")
        self.w_out = self.dram("w_out", [D, D], F32, "ExternalInput")
        self.w_up = self.dram("w_up", [D, 2 * DFF], F32, "ExternalInput")
        self.w_down = self.dram("w_down", [DFF, D], F32, "ExternalInput")
        self.cst_d = self.dram("cst", [128, CST_COLS], F32, "ExternalInput")
        self.cbf_d = self.dram("cbf", [128, CBF_COLS], BF16, "ExternalInput")
        self.rope_d = self.dram("rope", [128, nb * 16], F32, "ExternalInput")
        self.gf_d = self.dram("gfc", [128, D], F32, "ExternalInput")
        self.out = self.dram("out", [S, D], F32, "ExternalOutput")
        self.h1T = self.dram("h1T", [128, 8, S + 4], BF16, okind)
        self.xbcT = self.dram("xbcT", [nb, 128, 1024], BF16, okind)
        self.prevf = self.dram("prevf", [nb, 128, 512], BF16, okind)
        self.x1s = self.dram("x1s", [S, D], F32, okind)
        self.h2T = self.dram("h2T", [128, 8, S + 2], BF16, okind)
        self.wq_s = self.dram("wq_s", [128, 8, 1280], BF16)
        self.wo_s = self.dram("wo_s", [128, 8, 1024], BF16)
        self.wu_s = self.dram("wu_s", [128, 8, 2 * DFF], BF16)
        self.wd_s = self.dram("wd_s", [128, DFF // 128, D], BF16)
        self.R_wscr = Res()
        if self.dbg:
            self.dbgD = self.dram("dbgD", [128, nb * 32 + nb * 16], F32, "ExternalOutput")
        self.R_h1T = [Res() for _ in range(nb)]
        self.R_h1pad = Res()
        self.R_xbcT = [Res() for _ in range(nb)]
        self.R_prevf = [Res() for _ in range(nb)]
        self.R_x1 = [Res() for _ in range(nb)]
        self.R_h2T = [Res() for _ in range(nb)]
        self.R_h2pad = Res()
        self.R_out = Res()
        self.dram_res = self.R_h1T + self.R_xbcT + self.R_prevf + self.R_x1 + self.R_h2T + [self.R_h1pad, self.R_h2pad, self.R_out, self.R_wscr]

        self.pe = Eng(self, nc.tensor, self.new_sem(), is_pe=True)
        self.act = Eng(self, nc.scalar, self.new_sem())
        self.dve = Eng(self, nc.vector, self.new_sem())
        self.pool = Eng(self, nc.gpsimd, self.new_sem())
        self.sp = Eng(self, nc.sync, self.new_sem())
        self.engs = [self.pe, self.act, self.dve, self.pool, self.sp]

        self.bank = []
        self.R_bank = []
        for i in range(8):
            t = es.enter_context(nc.psum_tensor("bank%d" % i, [128, 512], F32))
            self.bank.append(t)
            self.R_bank.append(Res())

        self.cst, self.R_cst = self.tile(es, "cst", [128, CST_COLS], F32)
        self.cbf, self.R_cbf = self.tile(es, "cbf", [128, CBF_COLS], BF16)
        self.mh, self.R_mh = self.tile(es, "mh", [128, 8], F32)
        self.zero, self.R_zero = self.tile(es, "zero", [128, 16], BF16)
        self.sp.dma(self.cst[:], self.cst_d, self.R_cst, w=[self.R_cst])
        self.sp.dma(self.cbf[:], self.cbf_d, self.R_cbf, w=[self.R_cbf])
        self.epsc, self.R_epsc = self.tile(es, "epsc", [128, 2], F32)
        self.pool.op(lambda e: e.memset(self.epsc[:], EPS), w=[self.R_epsc])
        self.pool.op(lambda e: e.memset(self.mh[:], -0.5), w=[self.R_mh])
        self.pool.op(lambda e: e.memset(self.zero[:], 0.0), w=[self.R_zero])

        with ExitStack() as es_ssd:
            self.PHI3, self.R_PHI3 = self.tile(es_ssd, "PHI3", [128, S], BF16)
            self.PSI3, self.R_PSI3 = self.tile(es_ssd, "PSI3", [128, S], BF16)
            self.TM, self.R_TM = self.tile(es_ssd, "TM", [128, nb, 32], F32)
            self.DECB, self.R_DECB = self.tile(es_ssd, "DECB", [128, nb, 16], F32)
            with ExitStack() as es_x:
                self.X = []
                self.R_X = []
                for i in range(4):
                    t, r = self.tile(es_x, "X%d" % i, [128, S], F32)
                    self.X.append(t)
                    self.R_X.append(r)
                if "A1" in self.phases:
                    self.phase_a1()
                    self.barrier()
                if "D" in self.phases:
                    self.phase_d()
                    self.barrier()
            if "AB" in self.phases:
                self.phase_ab()
                self.barrier()
            if "C" in self.phases:
                self.phase_c()
                self.barrier()
        if "F" in self.phases:
            self.phase_f()
            self.barrier()
        self.barrier()
        self.es.close()
        return nc

    def c_(self, name, lo=None, hi=None):
        a, b = _c[name]
        if lo is not None:
            a, b = a + lo, a + hi
        return self.cst[:, a:b]

    def rstd(self, ss_ap, n, out_ap, rr, rw):
        k = out_ap.shape[1]
        self.pool.op(lambda e: e.tensor_scalar(out=out_ap, in0=ss_ap, scalar1=1.0 / n, scalar2=EPS,
                                               op0=ALU.mult, op1=ALU.add), r=rr, w=rw)
        self.pool.op(lambda e: e.tensor_tensor(out=out_ap, in0=out_ap, in1=self.mh[0:out_ap.shape[0], 0:k], op=ALU.pow),
                     r=[self.R_mh], w=rw)

    def prep_weight(self, es, src, ncols, emit, nk=8, tag="w"):
        stg = []
        for i in range(2):
            stg.append(self.tile(es, "stg_%s%d" % (tag, i), [128, ncols], F32))
        for k in range(nk):
            t, r = stg[k % 2]
            self.sp.dma(t[:], src[k * 128:(k + 1) * 128, :], r, w=[r])
            emit(k, t, r)

    def phase_a1(self):
        nc, S, nb = self.nc, self.S, self.nb
        pe, act, dve, pool, sp = self.pe, self.act, self.dve, self.pool, self.sp
        with ExitStack() as es:
            xa = [self.tile(es, "xa%d" % i, [128, D], F32) for i in range(3)]
            junk, R_junk = self.tile(es, "junkA", [128, D], BF16)
            hb = [self.tile(es, "hb%d" % i, [128, D], BF16) for i in range(2)]
            h1 = [self.tile(es, "h1_%d" % i, [128, 8, 128], BF16) for i in range(2)]
            ss, _ = self.tile(es, "ssA", [128, 4], F32)
            rs, _ = self.tile(es, "rsA", [128, 4], F32)
            R_ssl = [Res() for _ in range(4)]
            R_rsl = [Res() for _ in range(4)]
            wdt, R_wdt = self.tile(es, "wdt3", [128, 8, 96], BF16)
            wst, R_wst = self.tile(es, "wdtst", [128, 8, 16], F32)
            sp.dma(wst[:], self.w_in[:, 2304:2320].rearrange("(k p) c -> p k c", p=128), R_wst, w=[R_wst])
            dve.op(lambda e: e.memset(wdt[:], 0.0), w=[R_wdt])
            g1b = self.c_("g1").unsqueeze(2).to_broadcast([128, 8, 16])
            for grp in range(3):
                dve.op(lambda e: e.tensor_tensor(out=wdt[:, :, grp * 32:grp * 32 + 16], in0=wst[:], in1=g1b,
                                                 op=ALU.mult), r=[R_wst, self.R_cst], w=[R_wdt])
            sp.dma(self.h1T[:, :, 0:2], self.zero[:, 0:16].rearrange("p (k c) -> p k c", k=8), self.R_zero,
                   r=[self.R_zero], w=[self.R_h1pad])
            sp.dma(self.h1T[:, :, S + 2:S + 4], self.zero[:, 0:16].rearrange("p (k c) -> p k c", k=8), self.R_zero,
                   r=[self.R_zero], w=[self.R_h1pad])
            identb = self.cbf[:, 0:128]
            tbank = (3, 5)
            dbank = (4, 6)

            def s0(b):
                xt, R_x = xa[b % 3]
                sp.dma(xt[:], self.x[b * 128:(b + 1) * 128, :], R_x, w=[R_x])

            def s1(b):
                xt, R_x = xa[b % 3]
                q = b % 4
                act.op(lambda e: e.activation(out=junk[:], in_=xt[:], func=AF.Square, accum_out=ss[:, q:q + 1]),
                       r=[R_x], w=[R_junk, R_ssl[q]])
                act.op(lambda e: e.activation(out=rs[:, q:q + 1], in_=ss[:, q:q + 1], func=AF.Ln, scale=1.0 / D,
                                              bias=self.epsc[:, 0:1]), r=[R_ssl[q], self.R_epsc], w=[R_rsl[q]])
                act.op(lambda e: e.activation(out=rs[:, q:q + 1], in_=rs[:, q:q + 1], func=AF.Exp, scale=-0.5),
                       r=[], w=[R_rsl[q]])

            def s2(b):
                xt, R_x = xa[b % 3]
                q = b % 4
                hbt, R_hb = hb[b % 2]
                dve.op(lambda e: e.tensor_scalar(out=hbt[:], in0=xt[:], scalar1=rs[:, q:q + 1], scalar2=None,
                                                 op0=ALU.mult), r=[R_x, R_rsl[q]], w=[R_hb])

            def s3(b):
                hbt, R_hb = hb[b % 2]
                bi = tbank[b % 2]
                tb = self.bank[bi][:].bitcast(BF16)
                pe.mm_group([(lambda e, k=k: e.transpose(out=tb[:, k * 128:(k + 1) * 128],
                                                         in_=hbt[:, k * 128:(k + 1) * 128], identity=identb))
                             for k in range(8)], r=[R_hb, self.R_cbf], w=[self.R_bank[bi]])

            def s4(b):
                h1t, R_h1 = h1[b % 2]
                bi = tbank[b % 2]
                tb = self.bank[bi][:].bitcast(BF16)
                dve.op(lambda e: e.tensor_copy(out=h1t[:].rearrange("p k t -> p (k t)"), in_=tb[:, :]),
                       r=[self.R_bank[bi]], w=[R_h1])
                sp.dma(self.h1T[:, :, 2 + b * 128:2 + (b + 1) * 128], h1t[:], R_h1, r=[R_h1], w=[self.R_h1T[b]])

            def s5(b):
                h1t, R_h1 = h1[b % 2]
                bi = dbank[b % 2]
                pe.mm_group([(lambda e, k=k: e.matmul(self.bank[bi][0:96, 0:128], lhsT=wdt[:, k, :], rhs=h1t[:, k, :],
                                                      start=(k == 0), stop=(k == 7))) for k in range(8)],
                            r=[R_wdt, R_h1], w=[self.R_bank[bi]])

            def s6(b):
                bi = dbank[b % 2]
                act.op(lambda e: e.activation(out=self.X[0][0:96, b * 128:(b + 1) * 128], in_=self.bank[bi][0:96, 0:128],
                                              func=AF.Exp, bias=self.c_("dtb")[0:96, :]),
                       r=[self.R_bank[bi], self.R_cst], w=[self.R_X[0]])

            stages = [s0, s1, s2, s3, s4, s5, s6]
            for t in range(nb + len(stages) - 1):
                for si in range(len(stages) - 1, -1, -1):
                    b = t - si
                    if 0 <= b < nb:
                        stages[si](b)

    def phase_d(self):
        nc, S, nb = self.nc, self.S, self.nb
        pe, act, dve, pool, sp = self.pe, self.act, self.dve, self.pool, self.sp
        X, RX = self.X, self.R_X
        P = slice(0, 96)
        sgn = self.c_("sel", 0, 1)[P]
        bsel = self.c_("sel", 1, 2)[P]
        fsel = self.c_("sel", 2, 3)[P]
        with ExitStack() as es:
            negA, R_negA = self.tile(es, "negA", [128, 1], F32)
            CSE, R_CSE = self.tile(es, "CSE", [128, nb], F32)
            c1, R_c1 = self.tile(es, "c1", [128, nb], F32)
            c2, R_c2 = self.tile(es, "c2", [128, nb], F32)
            DEC, R_DEC = self.tile(es, "DEC", [128, nb], F32)
            Dg, R_Dg = self.tile(es, "Dg", [16, nb, 16], F32)
            ones, R_ones = self.tile(es, "onesD", [16, 128], F32)
            act.op(lambda e: e.activation(out=X[0][P], in_=X[0][P], func=AF.Ln, bias=1.0), r=[], w=[RX[0]])
            act.op(lambda e: e.activation(out=X[1][P], in_=X[0][P], func=AF.Ln), r=[RX[0]], w=[RX[1]])
            act.op(lambda e: e.activation(out=negA[P], in_=self.c_("alog")[P], func=AF.Exp), r=[self.R_cst], w=[R_negA])
            dve.op(lambda e: e.tensor_scalar(out=negA[P], in0=negA[P], scalar1=-1.0, scalar2=None, op0=ALU.mult),
                   r=[], w=[R_negA])
            dve.op(lambda e: e.tensor_scalar(out=X[2][P], in0=X[0][P], scalar1=negA[P], scalar2=None, op0=ALU.mult),
                   r=[RX[0], R_negA], w=[RX[2]])
            dve.op(lambda e: e.memset(self.PHI3[P], 1.0), w=[self.R_PHI3])
            dve.op(lambda e: e.memset(self.PHI3[P].rearrange("p (c j) -> p c j", j=128)[:, :, 0:1], 0.0),
                   w=[self.R_PHI3])
            dve.op(lambda e: e.tensor_tensor_scan(out=X[3][P], data0=self.PHI3[P], data1=X[2][P], initial=0.0,
                                                  op0=ALU.mult, op1=ALU.add), r=[self.R_PHI3, RX[2]], w=[RX[3]])
            dve.op(lambda e: e.tensor_copy(out=CSE[P], in_=X[3][P].rearrange("p (c j) -> p c j", j=128)[:, :, 127]),
                   r=[RX[3]], w=[R_CSE])
            dve.op(lambda e: e.tensor_scalar(out=X[2][P], in0=X[2][P], scalar1=bsel, scalar2=None, op0=ALU.mult),
                   r=[self.R_cst], w=[RX[2]])
            dve.op(lambda e: e.scalar_tensor_tensor(out=X[3][P], in0=X[3][P], scalar=sgn, in1=X[2][P],
                                                    op0=ALU.mult, op1=ALU.add), r=[RX[2], self.R_cst], w=[RX[3]])
            dve.op(lambda e: e.tensor_tensor(out=X[1][P], in0=X[1][P], in1=X[3][P], op=ALU.subtract),
                   r=[RX[3]], w=[RX[1]])
            dve.op(lambda e: e.tensor_scalar(out=c1[P], in0=CSE[P], scalar1=bsel, scalar2=None, op0=ALU.mult),
                   r=[R_CSE], w=[R_c1])
            dve.op(lambda e: e.tensor_scalar(out=c2[P], in0=CSE[P], scalar1=fsel, scalar2=None, op0=ALU.mult),
                   r=[R_CSE], w=[R_c2])
            v3 = lambda t: t[P].rearrange("p (c j) -> p c j", j=128)
            bc = lambda t: t[P].unsqueeze(2).to_broadcast([96, nb, 128])
            dve.op(lambda e: e.tensor_tensor(out=v3(X[2]), in0=v3(X[3]), in1=bc(c1), op=ALU.add),
                   r=[RX[3], R_c1], w=[RX[2]])
            act.op(lambda e: e.activation(out=X[2][P], in_=X[2][P], func=AF.Exp), r=[], w=[RX[2]])
            dve.op(lambda e: e.tensor_tensor(out=v3(X[0]), in0=v3(X[1]), in1=bc(c2), op=ALU.add),
                   r=[RX[1], R_c2], w=[RX[0]])
            act.op(lambda e: e.activation(out=X[0][P], in_=X[0][P], func=AF.Exp), r=[], w=[RX[0]])
            act.op(lambda e: e.activation(out=DEC[P], in_=CSE[P], func=AF.Exp), r=[R_CSE], w=[R_DEC])
            identf = self.c_("identf")
            for g0 in range(0, nb, 16):
                gn = min(16, nb - g0)
                bk = self.bank[0][:].rearrange("p (b c) -> p b c", c=32)
                for bi in range(gn):
                    b = g0 + bi
                    pe.op(lambda e: e.transpose(out=bk[:, bi, 0:16], in_=X[2][0:16, b * 128:(b + 1) * 128],
                                                identity=identf[0:16, 0:16]), r=[RX[2], self.R_cst], w=[self.R_bank[0]])
                    pe.op(lambda e: e.transpose(out=bk[:, bi, 16:32], in_=X[0][0:16, b * 128:(b + 1) * 128],
                                                identity=identf[0:16, 0:16]), r=[RX[0], self.R_cst], w=[self.R_bank[0]])
                dve.op(lambda e: e.tensor_copy(out=self.TM[:, g0:g0 + gn, :], in_=bk[:, 0:gn, :]),
                       r=[self.R_bank[0]], w=[self.R_TM])
            dve.op(lambda e: e.memset(ones[:], 1.0), w=[R_ones])
            dve.op(lambda e: e.tensor_tensor(out=Dg[:], in0=DEC[0:16].unsqueeze(2).to_broadcast([16, nb, 16]),
                                             in1=identf[0:16, 0:16].unsqueeze(1).to_broadcast([16, nb, 16]),
                                             op=ALU.mult), r=[R_DEC, self.R_cst], w=[R_Dg])
            for c0 in range(0, nb, 32):
                cn = min(32, nb - c0)
                pe.op(lambda e: e.matmul(self.bank[1][:, 0:cn * 16], lhsT=ones[:],
                                         rhs=Dg[:, c0:c0 + cn, :].rearrange("p b h -> p (b h)"), start=True, stop=True),
                      r=[R_ones, R_Dg], w=[self.R_bank[1]])
                dve.op(lambda e: e.tensor_copy(out=self.DECB[:, c0:c0 + cn, :].rearrange("p b h -> p (b h)"),
                                               in_=self.bank[1][:, 0:cn * 16]), r=[self.R_bank[1]], w=[self.R_DECB])
            if self.dbg:
                sp.dma(self.dbgD[:, 0:nb * 32], self.TM[:].rearrange("p b c -> p (b c)"), self.R_TM, r=[self.R_TM])
                sp.dma(self.dbgD[:, nb * 32:nb * 48], self.DECB[:].rearrange("p b c -> p (b c)"), self.R_DECB,
                       r=[self.R_DECB])
            for src, R_src, scr, R_scr, dst, R_dst in ((X[3], RX[3], X[2], RX[2], self.PHI3, self.R_PHI3),
                                                       (X[1], RX[1], X[0], RX[0], self.PSI3, self.R_PSI3)):
                dve.op(lambda e: e.tensor_copy(out=dst[0:96], in_=src[0:96]), r=[R_src], w=[R_dst])
                for (a, b) in ((32, 64), (64, 96)):
                    dve.op(lambda e: e.tensor_tensor(out=scr[a:b], in0=src[a:b], in1=dst[a:b], op=ALU.subtract),
                           r=[R_src, R_dst], w=[R_scr])
                for (a, b) in ((32, 64), (64, 96)):
                    dve.op(lambda e: e.tensor_copy(out=dst[a:b], in_=scr[a:b]), r=[R_scr], w=[R_dst])
                dve.op(lambda e: e.tensor_tensor(out=scr[64:96], in0=scr[64:96], in1=dst[64:96], op=ALU.subtract),
                       r=[R_dst], w=[R_scr])
                dve.op(lambda e: e.tensor_copy(out=dst[64:96], in_=scr[64:96]), r=[R_scr], w=[R_dst])

    def tok_transposes(self, xbt, R_xb, tokt, R_tok, bankid=3):
        tb = self.bank[bankid][:].bitcast(BF16)
        identb = self.cbf[:, 0:128]
        for j in range(6):
            self.pe.op(lambda e: e.transpose(out=tb[:, j * 128:(j + 1) * 128], in_=xbt[:, j, :], identity=identb),
                       r=[R_xb, self.R_cbf], w=[self.R_bank[bankid]])
        self.dve.op(lambda e: e.tensor_copy(out=tokt[:], in_=tb[:, 0:768]), r=[self.R_bank[bankid]], w=[R_tok])

    def state_update(self, tokt, R_tok, xdec, R_xdec, H, R_H, b, d, bankid):
        dve, pe, pool = self.dve, self.pe, self.pool
        sd = self.TM[:, b, 16 + 8 * d:24 + 8 * d].unsqueeze(2).to_broadcast([128, 8, 64])
        dve.op(lambda e: e.tensor_tensor(out=xdec[:].rearrange("p (h q) -> p h q", q=64),
                                         in0=tokt[:, 0:512].rearrange("p (h q) -> p h q", q=64), in1=sd, op=ALU.mult),
               r=[R_tok, self.R_TM], w=[R_xdec])
        for g in range(2):
            pe.op(lambda e: e.matmul(self.bank[bankid][:, g * 256:(g + 1) * 256],
                                     lhsT=tokt[:, 512 + g * 128:512 + (g + 1) * 128],
                                     rhs=xdec[:, g * 256:(g + 1) * 256], start=(g == 0), stop=(g == 1)),
                  r=[R_tok, R_xdec], w=[self.R_bank[bankid]])
        dec = self.DECB[:, b, 8 * d:8 * d + 8].unsqueeze(2).to_broadcast([128, 8, 64])
        dve.op(lambda e: e.tensor_tensor(out=H[:].rearrange("p (h q) -> p h q", q=64),
                                         in0=H[:].rearrange("p (h q) -> p h q", q=64), in1=dec, op=ALU.mult),
               r=[self.R_DECB], w=[R_H])
        dve.op(lambda e: e.tensor_tensor(out=H[:], in0=H[:], in1=self.bank[bankid][:, :], op=ALU.add),
               r=[self.R_bank[bankid]], w=[R_H])

    def conv_jobs(self):
        jobs = []
        for k in range(8):
            rows = slice(k * 128, (k + 1) * 128)
            jobs.append((self.w_in[rows, 0:512], self.wq_s[:, k, 0:512], "qperm", ("g1", k), 512))
            jobs.append((self.w_in[rows, 512:768], self.wq_s[:, k, 512:768], "scale", ("g1", k), 256))
            jobs.append((self.w_in[rows, 768:1280], self.wq_s[:, k, 768:1280], "scale", ("g1h", k), 512))
            jobs.append((self.w_out[rows, :], self.wo_s[:, k, :], "scale", ("ga", k) if k < 4 else ("gs", k - 4), 1024))
        for k in range(8):
            rows = slice(k * 128, (k + 1) * 128)
            for c0 in range(0, 2 * DFF, 1408):
                jobs.append((self.w_up[rows, c0:c0 + 1408], self.wu_s[:, k, c0:c0 + 1408], "scale", ("g2", k), 1408))
        for k in range(DFF // 128):
            jobs.append((self.w_down[k * 128:(k + 1) * 128, :], self.wd_s[:, k, :], "copy", None, 1024))
        return jobs

    def phase_ab(self):
        nc, S, nb = self.nc, self.S, self.nb
        pe, act, dve, pool, sp = self.pe, self.act, self.dve, self.pool, self.sp
        RB, bank = self.R_bank, self.bank
        tiles = [(b0, min(2, nb - b0)) for b0 in range(0, nb, 2)]
        nt = len(tiles)
        with ExitStack() as es:
            wx, R_wx = self.tile(es, "wxbc", [128, 8, 1024], BF16)
            g1h, R_g1h = self.tile(es, "g1h", [128, 8], F32)
            dve.op(lambda e: e.tensor_scalar(out=g1h[:], in0=self.c_("g1"), scalar1=0.5, scalar2=None, op0=ALU.mult),
                   r=[self.R_cst], w=[R_g1h])
            wst = [self.tile(es, "wxst%d" % i, [128, 1024], F32) for i in range(2)]
            for k in range(8):
                t, r = wst[k % 2]
                sp.dma(t[:], self.w_in[k * 128:(k + 1) * 128, 1280:2304], r, w=[r])
                if k % 2 == 0:
                    dve.op(lambda e: e.tensor_scalar(out=wx[:, k, :], in0=t[:], scalar1=self.c_("g1", k, k + 1),
                                                     scalar2=None, op0=ALU.mult), r=[r, self.R_cst], w=[R_wx])
                else:
                    act.op(lambda e: e.activation(out=wx[:, k, :], in_=t[:], func=AF.Copy,
                                                  scale=self.c_("g1", k, k + 1)), r=[r, self.R_cst], w=[R_wx])
            scw = self.c_("scw")
            scb = self.c_("scb")
            hw = [self.tile(es, "hw%d" % i, [128, 8, 260], BF16) for i in range(3)]
            acc = [self.tile(es, "accAB%d" % i, [128, 256], F32) for i in range(4)]
            xb2 = [self.tile(es, "xb2_%d" % i, [128, 8, 256], BF16) for i in range(2)]
            tok = [self.tile(es, "tokAB%d" % i, [128, 768], BF16) for i in range(2)]
            xdec = [self.tile(es, "xdecAB%d" % i, [128, 512], BF16) for i in range(2)]
            H, R_H = self.tile(es, "HfAB", [128, 512], F32)
            pv = [self.tile(es, "pvAB%d" % i, [128, 512], BF16) for i in range(2)]
            stg = [self.tile(es, "cvs%d" % i, [128, 1408], F32) for i in range(3)]
            cvt = [self.tile(es, "cvo%d" % i, [128, 1408], BF16) for i in range(2)]
            dve.op(lambda e: e.memset(H[:], 0.0), w=[R_H])
            identb = self.cbf[:, 0:128]
            xpb = [0, 1, 2, 3]
            trb = [4, 5]
            stb = [6, 7]
            jobs = self.conv_jobs()
            gtab = {"g1": self.c_("g1"), "g2": self.c_("g2"), "ga": self.c_("ga"), "gs": self.c_("gs"), "g1h": g1h[:]}
            gres = {"g1": self.R_cst, "g2": self.R_cst, "ga": self.R_cst, "gs": self.R_cst, "g1h": R_g1h}
            jstate = {"loaded": 0, "done": 0}

            def job_load(i):
                if i >= len(jobs):
                    return
                src, dst, kind, g, n = jobs[i]
                t, r = stg[i % 3]
                act.dma(t[:, 0:n], src, r, w=[r])

            def job_run(i):
                if i >= len(jobs):
                    return
                src, dst, kind, g, n = jobs[i]
                t, r = stg[i % 3]
                o, ro = cvt[i % 2]
                if kind == "copy":
                    act.op(lambda e: e.activation(out=o[:, 0:n], in_=t[:, 0:n], func=AF.Copy), r=[r], w=[ro])
                else:
                    gcol = gtab[g[0]][:, g[1]:g[1] + 1]
                    if kind == "qperm":
                        for hh in range(2):
                            act.op(lambda e: e.activation(
                                out=o[:, 0:512].rearrange("p (i hh d) -> p hh i d", i=4, hh=2)[:, hh],
                                in_=t[:, hh * 256:(hh + 1) * 256].rearrange("p (i d) -> p i d", i=4),
                                func=AF.Copy, scale=gcol), r=[r, gres[g[0]]], w=[ro])
                    else:
                        act.op(lambda e: e.activation(out=o[:, 0:n], in_=t[:, 0:n], func=AF.Copy, scale=gcol),
                               r=[r, gres[g[0]]], w=[ro])
                act.dma(dst, o[:, 0:n], ro, r=[ro])

            def do_jobs(cnt):
                for _ in range(cnt):
                    i = jstate["done"]
                    if i >= len(jobs):
                        return
                    while jstate["loaded"] < min(len(jobs), i + 3):
                        job_load(jstate["loaded"])
                        jstate["loaded"] += 1
                    job_run(i)
                    jstate["done"] += 1

            per_iter = -(-len(jobs) // nt)
            cc = {"n": 0}

            def load(T):
                b0, n = tiles[T]
                W = n * 128
                hwt, R_hw = hw[T % 3]
                rd = [self.R_h1pad] + [self.R_h1T[i] for i in range(b0 - 1, b0 + n + 1) if 0 <= i < nb]
                sp.dma(hwt[:, :, 0:W + 4], self.h1T[:, :, b0 * 128:b0 * 128 + W + 4], R_hw, r=rd, w=[R_hw])

            def chunk_pair(T, jp):
                b0, n = tiles[T]
                W = n * 128
                hwt, R_hw = hw[T % 3]
                xbt, R_xb = xb2[T % 2]
                info = []
                for j in (2 * jp, 2 * jp + 1):
                    c = cc["n"]
                    cc["n"] += 1
                    bi = xpb[c % 4]
                    at, R_a = acc[c % 4]
                    pe.mm_group([(lambda e, k=k: e.matmul(bank[bi][:, 0:W + 4], lhsT=wx[:, k, j * 128:(j + 1) * 128],
                                                          rhs=hwt[:, k, 0:W + 4], start=(k == 0), stop=(k == 7)))
                                 for k in range(8)], r=[R_wx, R_hw], w=[RB[bi]])
                    act.op(lambda e: e.activation(out=at[:, 0:W], in_=bank[bi][:, 0:W], func=AF.Identity,
                                                  scale=scw[:, j * 5:j * 5 + 1], bias=scb[:, j:j + 1]),
                           r=[RB[bi], self.R_cst], w=[R_a])
                    info.append((j, bi, at, R_a))
                for t in range(1, 5):
                    for (j, bi, at, R_a) in info:
                        dve.op(lambda e: e.scalar_tensor_tensor(out=at[:, 0:W], in0=bank[bi][:, t:t + W],
                                                                scalar=scw[:, j * 5 + t:j * 5 + t + 1], in1=at[:, 0:W],
                                                                op0=ALU.mult, op1=ALU.add),
                               r=[RB[bi], self.R_cst], w=[R_a])
                for (j, bi, at, R_a) in info:
                    act.op(lambda e: e.activation(out=xbt[:, j, 0:W], in_=at[:, 0:W], func=AF.Silu), r=[R_a], w=[R_xb])

            def store(T):
                b0, n = tiles[T]
                xbt, R_xb = xb2[T % 2]
                for bi_ in range(n):
                    b = b0 + bi_
                    sp.dma(self.xbcT[b].rearrange("p (k t) -> p k t", k=8), xbt[:, :, bi_ * 128:(bi_ + 1) * 128], R_xb,
                           r=[R_xb], w=[self.R_xbcT[b]])

            def tr(T):
                b0, n = tiles[T]
                xbt, R_xb = xb2[T % 2]
                for bi_ in range(n):
                    b = b0 + bi_
                    tbk = trb[bi_]
                    tb = bank[tbk][:].bitcast(BF16)
                    pe.mm_group([(lambda e, j=j: e.transpose(out=tb[:, j * 128:(j + 1) * 128],
                                                             in_=xbt[:, j, bi_ * 128:(bi_ + 1) * 128], identity=identb))
                                 for j in range(6)], r=[R_xb, self.R_cbf], w=[RB[tbk]])

            def evac(T):
                b0, n = tiles[T]
                for bi_ in range(n):
                    b = b0 + bi_
                    tbk = trb[bi_]
                    tb = bank[tbk][:].bitcast(BF16)
                    tokt, R_tok = tok[b % 2]
                    xd, R_xd = xdec[b % 2]
                    dve.op(lambda e: e.tensor_copy(out=tokt[:], in_=tb[:, 0:768]), r=[RB[tbk]], w=[R_tok])
                    sd = self.TM[:, b, 16:24].unsqueeze(2).to_broadcast([128, 8, 64])
                    dve.op(lambda e: e.tensor_tensor(out=xd[:].rearrange("p (h q) -> p h q", q=64),
                                                     in0=tokt[:, 0:512].rearrange("p (h q) -> p h q", q=64), in1=sd,
                                                     op=ALU.mult), r=[R_tok, self.R_TM], w=[R_xd])

            def states(T):
                b0, n = tiles[T]
                for bi_ in range(n):
                    b = b0 + bi_
                    tokt, R_tok = tok[b % 2]
                    xd, R_xd = xdec[b % 2]
                    sb = stb[bi_]
                    pe.mm_group([(lambda e, g=g: e.matmul(bank[sb][:, g * 256:(g + 1) * 256],
                                                          lhsT=tokt[:, 512 + g * 128:512 + (g + 1) * 128],
                                                          rhs=xd[:, g * 256:(g + 1) * 256], start=(g == 0), stop=(g == 1)))
                                 for g in range(2)], r=[R_tok, R_xd], w=[RB[sb]])

            def scan(T, bi_):
                b0, n = tiles[T]
                if bi_ >= n:
                    return
                b = b0 + bi_
                sb = stb[bi_]
                pvt, R_pv = pv[b % 2]
                act.op(lambda e: e.activation(out=pvt[:], in_=H[:], func=AF.Copy), r=[R_H], w=[R_pv])
                sp.dma(self.prevf[b], pvt[:], R_pv, r=[R_pv], w=[self.R_prevf[b]])
                dec = self.DECB[:, b, 0:8].unsqueeze(2).to_broadcast([128, 8, 64])
                dve.op(lambda e: e.tensor_tensor(out=H[:].rearrange("p (h q) -> p h q", q=64),
                                                 in0=H[:].rearrange("p (h q) -> p h q", q=64), in1=dec, op=ALU.mult),
                       r=[self.R_DECB], w=[R_H])
                dve.op(lambda e: e.tensor_tensor(out=H[:], in0=H[:], in1=bank[sb][:, :], op=ALU.add),
                       r=[RB[sb]], w=[R_H])

            load(0)
            if nt > 1:
                load(1)
            for T in range(nt + 1):
                P = T - 1
                if T + 2 < nt:
                    load(T + 2)
                if P >= 0:
                    tr(P)
                if T < nt:
                    chunk_pair(T, 0)
                if P >= 0:
                    evac(P)
                    states(P)
                if T < nt:
                    chunk_pair(T, 1)
                if P >= 0:
                    scan(P, 0)
                if T < nt:
                    chunk_pair(T, 2)
                do_jobs(per_iter)
                if T < nt:
                    chunk_pair(T, 3)
                if P >= 0:
                    scan(P, 1)
                if T < nt:
                    store(T)
            do_jobs(len(jobs))

    def phase_c(self):
        nc, S, nb = self.nc, self.S, self.nb
        pe, act, dve, pool, sp = self.pe, self.act, self.dve, self.pool, self.sp
        RB = self.R_bank
        bank = self.bank
        identb = self.cbf[:, 0:128]
        mle = self.cbf[:, _b["mle"][0]:_b["mle"][1]]
        mge = self.cbf[:, _b["mge"][0]:_b["mge"][1]]
        ind = self.cbf[:, _b["ind"][0]:_b["ind"][1]].rearrange("p (h t) -> p h t", t=128)
        with ExitStack() as es:
            wq, R_wq = self.tile(es, "wqkvz", [128, 8, 1280], BF16)
            wo, R_wo = self.tile(es, "wout", [128, 8, 1024], BF16)
            rope, R_rope = self.tile(es, "rope", [128, nb, 16], F32)
            esink, R_esink = self.tile(es, "esink", [128, 8], F32)
            sp.dma(rope[:].rearrange("p b c -> p (b c)"), self.rope_d, R_rope, w=[R_rope])
            act.op(lambda e: e.activation(out=esink[:], in_=self.c_("sink"), func=AF.Exp), r=[self.R_cst], w=[R_esink])
            sp.dma(wq[:], self.wq_s, R_wq, w=[R_wq])
            sp.dma(wo[:], self.wo_s, R_wo, w=[R_wo])
            NR = 3
            hc = [self.tile(es, "hc%d" % i, [128, 8, 128], BF16) for i in range(2)]
            qk = self.tile(es, "qktm", [128, 640], BF16)
            rtmp = [self.tile(es, "rtmp%d" % i, [128, 10, 8], F32) for i in range(4)]
            qT = [self.tile(es, "qT%d" % i, [128, 4, 128], BF16) for i in range(2)]
            KP = [[self.tile(es, "KP%d_%d" % (i, h), [128, 128], BF16) for h in range(2)] for i in range(NR)]
            V1 = [self.tile(es, "V1_%d" % i, [128, 2, 65], BF16) for i in range(NR)]
            gate = [self.tile(es, "gate%d" % i, [128, 512], F32) for i in range(2)]
            thz = self.tile(es, "thz", [128, 512], F32)
            xb = [self.tile(es, "xbC%d" % i, [128, 8, 128], BF16) for i in range(2)]
            pf = [self.tile(es, "pfC%d" % i, [128, 512], BF16) for i in range(2)]
            tok = [self.tile(es, "tokC%d" % i, [128, 768], BF16) for i in range(2)]
            PT = [self.tile(es, "PT%d" % i, [128, 512], BF16) for i in range(3)]
            den = self.tile(es, "den", [128, 8], F32)
            att = self.tile(es, "att", [128, 512], F32)
            junk = self.tile(es, "junkC", [128, 1024], F32)
            ssn = self.tile(es, "ssn", [128, 4], F32)
            rsn = self.tile(es, "rsn", [128, 4], F32)
            R_ss = [Res() for _ in range(4)]
            R_rs = [Res() for _ in range(4)]
            mixed = self.tile(es, "mixed", [128, 1024], BF16)
            R_mixa, R_mixs = Res(), Res()
            LT = [self.tile(es, "LT%d" % i, [128, 8, 128], F32) for i in range(2)]
            WT = [self.tile(es, "WT%d" % i, [128, 8, 128], BF16) for i in range(2)]
            ytmp = [self.tile(es, "ytmp%d" % i, [128, 512], F32) for i in range(4)]
            xdec = self.tile(es, "xdecC", [128, 512], BF16)
            Hb = self.tile(es, "HbC", [128, 512], F32)
            pvb = self.tile(es, "pvbC", [128, 512], BF16)
            mT = self.tile(es, "mT", [128, 8, 128], BF16)
            xr = [self.tile(es, "xr%d" % i, [128, D], F32) for i in range(2)]
            x1 = [self.tile(es, "x1_%d" % i, [128, D], F32) for i in range(2)]
            h2 = self.tile(es, "h2", [128, D], BF16)
            h2T = [self.tile(es, "h2T%d" % i, [128, 8, 128], BF16) for i in range(2)]
            dve.op(lambda e: e.memset(Hb[0][:], 0.0), w=[Hb[1]])
            for i in range(NR):
                for h in range(2):
                    dve.op(lambda e: e.memset(KP[i][h][0][:], 0.0), w=[KP[i][h][1]])
                dve.op(lambda e: e.memset(V1[i][0][:], 1.0), w=[V1[i][1]])
            sp.dma(self.h2T[:, :, 0:1], self.zero[:, 0:8].rearrange("p (k c) -> p k c", k=8), self.R_zero,
                   r=[self.R_zero], w=[self.R_h2pad], slow=True)
            sp.dma(self.h2T[:, :, S + 1:S + 2], self.zero[:, 0:8].rearrange("p (k c) -> p k c", k=8), self.R_zero,
                   r=[self.R_zero], w=[self.R_h2pad], slow=True)
            tb3 = bank[3][:].bitcast(BF16)

            def stage1(b):
                s = b % 2
                hct, R_hc = hc[s]
                sp.dma(hct[:], self.h1T[:, :, 2 + b * 128:2 + (b + 1) * 128], R_hc, r=[self.R_h1T[b]], w=[R_hc])
                sp.dma(xb[s][0][:].rearrange("p k t -> p (k t)"), self.xbcT[b], xb[s][1], r=[self.R_xbcT[b]], w=[xb[s][1]])
                sp.dma(pf[s][0][:], self.prevf[b], pf[s][1], r=[self.R_prevf[b]], w=[pf[s][1]])
                sp.dma(xr[s][0][:], self.x[b * 128:(b + 1) * 128, :], xr[s][1], w=[xr[s][1]])
                for (c0, cn, bi) in ((0, 512, 0), (512, 512, 1), (1024, 256, 2)):
                    for k in range(8):
                        pe.op(lambda e: e.matmul(bank[bi][:, 0:cn], lhsT=hct[:, k, :], rhs=wq[:, k, c0:c0 + cn],
                                                 start=(k == 0), stop=(k == 7)), r=[R_hc, R_wq], w=[RB[bi]])
                cosb = rope[:, b, 0:8]
                sinb = rope[:, b, 8:16]
                for (bi, c0, nh, o0) in ((0, 0, 8, 0), (1, 0, 2, 512)):
                    src = bank[bi][:, c0:c0 + nh * 64].rearrange("p (h d) -> p h d", d=64)
                    dst = qk[0][:, o0:o0 + nh * 64].rearrange("p (h d) -> p h d", d=64)
                    cb_ = cosb.unsqueeze(1).to_broadcast([128, nh, 8])
                    sb_ = sinb.unsqueeze(1).to_broadcast([128, nh, 8])
                    t0, t1, t2, t3 = [rtmp[i][0][:, 0:nh, :] for i in range(4)]
                    rr = [RB[bi], R_rope]
                    dve.op(lambda e: e.tensor_tensor(out=t0, in0=src[:, :, 0:8], in1=cb_, op=ALU.mult), r=rr, w=[rtmp[0][1]])
                    dve.op(lambda e: e.tensor_tensor(out=t1, in0=src[:, :, 8:16], in1=sb_, op=ALU.mult), r=rr, w=[rtmp[1][1]])
                    dve.op(lambda e: e.tensor_tensor(out=t2, in0=src[:, :, 8:16], in1=cb_, op=ALU.mult), r=rr, w=[rtmp[2][1]])
                    dve.op(lambda e: e.tensor_tensor(out=t3, in0=src[:, :, 0:8], in1=sb_, op=ALU.mult), r=rr, w=[rtmp[3][1]])
                    dve.op(lambda e: e.tensor_tensor(out=dst[:, :, 0:8], in0=t0, in1=t1, op=ALU.subtract),
                           r=[rtmp[0][1], rtmp[1][1]], w=[qk[1]])
                    dve.op(lambda e: e.tensor_tensor(out=dst[:, :, 8:16], in0=t2, in1=t3, op=ALU.add),
                           r=[rtmp[2][1], rtmp[3][1]], w=[qk[1]])
                    act.op(lambda e: e.activation(out=dst[:, :, 16:64], in_=src[:, :, 16:64], func=AF.Copy),
                           r=[RB[bi]], w=[qk[1]])
                v1t, R_v1 = V1[b % NR]
                act.op(lambda e: e.activation(out=v1t[:, :, 0:64],
                                              in_=bank[1][:, 128:256].rearrange("p (h d) -> p h d", d=64), func=AF.Copy),
                       r=[RB[1]], w=[R_v1])
                zsrc = lambda: [bank[1][:, 256:512], bank[2][:, 0:256]]
                for zi in range(2):
                    zs = zsrc()[zi]
                    act.op(lambda e: e.activation(out=thz[0][:, zi * 256:(zi + 1) * 256], in_=zs, func=AF.Tanh),
                           r=[RB[1 + zi]], w=[thz[1]])
                    dve.op(lambda e: e.scalar_tensor_tensor(out=gate[s][0][:, zi * 256:(zi + 1) * 256],
                                                            in0=thz[0][:, zi * 256:(zi + 1) * 256], scalar=1.0, in1=zs,
                                                            op0=ALU.add, op1=ALU.mult),
                           r=[thz[1], RB[1 + zi]], w=[gate[s][1]])
                for j in range(5):
                    pe.op(lambda e: e.transpose(out=tb3[:, j * 128:(j + 1) * 128], in_=qk[0][:, j * 128:(j + 1) * 128],
                                                identity=identb), r=[qk[1], self.R_cbf], w=[RB[3]])
                dve.op(lambda e: e.tensor_copy(out=qT[s][0][:].rearrange("p c t -> p (c t)"), in_=tb3[:, 0:512]),
                       r=[RB[3]], w=[qT[s][1]])
                for h in range(2):
                    kpt, R_kp = KP[b % NR][h]
                    dve.op(lambda e: e.tensor_copy(out=kpt[h * 64:(h + 1) * 64, :], in_=tb3[h * 64:(h + 1) * 64, 512:640]),
                           r=[RB[3]], w=[R_kp])
                self.tok_transposes(xb[s][0], xb[s][1], tok[s][0], tok[s][1], bankid=3)

            def stage2(c):
                s = c % 2
                kbs = [kb for kb in (c - 1, c, c + 1) if 0 <= kb < nb]
                for half in range(2):
                    for ki, kb in enumerate(kbs):
                        bi = 4 + ki
                        last = (kb == c)
                        pe.op(lambda e: e.matmul(bank[bi][:, :], lhsT=KP[kb % NR][half][0][:, :],
                                                 rhs=qT[s][0][:].rearrange("p c t -> p (c t)"), start=True, stop=last),
                              r=[KP[kb % NR][half][1], qT[s][1]], w=[RB[bi]])
                        if not last:
                            m = mge if kb < c else mle
                            pe.op(lambda e: e.matmul(bank[bi][:, :], lhsT=identb, rhs=m, start=False, stop=True),
                                  r=[self.R_cbf], w=[RB[bi]])
                        act.op(lambda e: e.activation(out=PT[ki][0][:], in_=bank[bi][:, :], func=AF.Exp, scale=0.125),
                               r=[RB[bi]], w=[PT[ki][1]])
                    po = bank[7][:, 0:260].rearrange("p (h d) -> p h d", d=65)
                    for i in range(4):
                        for ki, kb in enumerate(kbs):
                            pe.op(lambda e: e.matmul(po[:, i, :], lhsT=PT[ki][0][:, i * 128:(i + 1) * 128],
                                                     rhs=V1[kb % NR][0][:, half, :], start=(ki == 0), stop=(ki == len(kbs) - 1)),
                                  r=[PT[ki][1], V1[kb % NR][1]], w=[RB[7]])
                    hs = slice(half * 4, half * 4 + 4)
                    dve.op(lambda e: e.tensor_tensor(out=den[0][:, hs], in0=po[:, :, 64], in1=esink[:, hs], op=ALU.add),
                           r=[RB[7], R_esink], w=[den[1]])
                    dve.op(lambda e: e.reciprocal(out=den[0][:, hs], in_=den[0][:, hs]), r=[], w=[den[1]])
                    dve.op(lambda e: e.tensor_tensor(
                        out=att[0][:, half * 256:(half + 1) * 256].rearrange("p (h d) -> p h d", d=64),
                        in0=po[:, :, 0:64], in1=den[0][:, hs].unsqueeze(2).to_broadcast([128, 4, 64]), op=ALU.mult),
                        r=[RB[7], den[1]], w=[att[1]])
                act.op(lambda e: e.activation(out=junk[0][:, 0:512], in_=att[0][:], func=AF.Square, accum_out=ssn[0][:, 0:1]),
                       r=[att[1]], w=[junk[1], R_ss[0]])
                self.rstd(ssn[0][:, 0:1], 512, rsn[0][:, 0:1], [R_ss[0]], [R_rs[0]])
                act.op(lambda e: e.activation(
                    out=mixed[0][:, 0:512].rearrange("p (hh i d) -> p hh i d", hh=2, i=4),
                    in_=att[0][:].rearrange("p (hh i d) -> p hh i d", hh=2, i=4), func=AF.Copy, scale=rsn[0][:, 0:1]),
                    r=[att[1], R_rs[0]], w=[R_mixa])
                xbt, R_xb = xb[s]
                tokt, R_tok = tok[s]
                for g in range(2):
                    pe.op(lambda e: e.matmul(bank[0][:, g * 128:(g + 1) * 128], lhsT=xbt[:, 4 + g, :], rhs=xbt[:, 6 + g, :],
                                             start=(g == 0), stop=(g == 1)), r=[R_xb], w=[RB[0]])
                cols = slice(c * 128, (c + 1) * 128)
                for d in range(2):
                    msk = mle if d == 0 else mge
                    for g in range(2):
                        bi = 1 + g
                        hd0 = d * 8 + g * 4
                        for hh in range(4):
                            pe.op(lambda e: e.matmul(bank[bi][:, hh * 128:(hh + 1) * 128], lhsT=ind[0:96, hd0 + hh, :],
                                                     rhs=self.PHI3[0:96, cols], start=(hh == 0), stop=False),
                                  r=[self.R_cbf, self.R_PHI3], w=[RB[bi]])
                        pe.op(lambda e: e.matmul(bank[bi][:, :], lhsT=self.PSI3[0:96, cols],
                                                 rhs=ind[0:96, hd0:hd0 + 4, :].rearrange("p h t -> p (h t)"),
                                                 start=False, stop=False), r=[self.R_cbf, self.R_PSI3], w=[RB[bi]])
                        pe.op(lambda e: e.matmul(bank[bi][:, :], lhsT=identb, rhs=msk, start=False, stop=True),
                              r=[self.R_cbf], w=[RB[bi]])
                        act.op(lambda e: e.activation(out=LT[d][0][:, g * 4:(g + 1) * 4, :].rearrange("p h t -> p (h t)"),
                                                      in_=bank[bi][:, :], func=AF.Exp), r=[RB[bi]], w=[LT[d][1]])
                    cbb = bank[0][:, 0:256].rearrange("p (g t) -> p g t", g=2).unsqueeze(2).to_broadcast([128, 2, 4, 128])
                    dve.op(lambda e: e.tensor_tensor(out=WT[d][0][:].rearrange("p (g h) t -> p g h t", g=2),
                                                     in0=LT[d][0][:].rearrange("p (g h) t -> p g h t", g=2), in1=cbb,
                                                     op=ALU.mult), r=[LT[d][1], RB[0]], w=[WT[d][1]])
                for h in range(8):
                    for d in range(2):
                        pe.op(lambda e: e.matmul(bank[4][:, h * 64:(h + 1) * 64], lhsT=WT[d][0][:, h, :],
                                                 rhs=tokt[:, h * 64:(h + 1) * 64], start=(h == 0 and d == 0), stop=(h == 7 and d == 1)),
                              r=[WT[d][1], R_tok], w=[RB[4]])
                act.op(lambda e: e.activation(out=pvb[0][:], in_=Hb[0][:], func=AF.Copy), r=[Hb[1]], w=[pvb[1]])
                for d in range(2):
                    pv_t, R_pvx = (pf[s] if d == 0 else pvb)
                    for g in range(2):
                        pe.op(lambda e: e.matmul(bank[5 + d][:, g * 256:(g + 1) * 256], lhsT=xbt[:, 6 + g, :],
                                                 rhs=pv_t[:, g * 256:(g + 1) * 256], start=(g == 0), stop=(g == 1)),
                              r=[R_xb, R_pvx], w=[RB[5 + d]])
                v3 = lambda ap: ap.rearrange("p (h q) -> p h q", q=64)
                eyf = self.TM[:, c, 0:8].unsqueeze(2).to_broadcast([128, 8, 64])
                eyb = self.TM[:, c, 8:16].unsqueeze(2).to_broadcast([128, 8, 64])
                dsk = self.c_("dskip").unsqueeze(2).to_broadcast([128, 8, 64])
                y0, y1, y2, y3 = ytmp
                dve.op(lambda e: e.tensor_tensor(out=v3(y0[0][:]), in0=v3(tokt[:, 0:512]), in1=dsk, op=ALU.mult),
                       r=[R_tok, self.R_cst], w=[y0[1]])
                dve.op(lambda e: e.tensor_tensor(out=v3(y1[0][:]), in0=v3(bank[5][:, :]), in1=eyf, op=ALU.mult),
                       r=[RB[5], self.R_TM], w=[y1[1]])
                dve.op(lambda e: e.tensor_tensor(out=v3(y2[0][:]), in0=v3(bank[6][:, :]), in1=eyb, op=ALU.mult),
                       r=[RB[6], self.R_TM], w=[y2[1]])
                dve.op(lambda e: e.tensor_tensor(out=y3[0][:], in0=bank[4][:, :], in1=y0[0][:], op=ALU.add),
                       r=[RB[4], y0[1]], w=[y3[1]])
                dve.op(lambda e: e.tensor_tensor(out=y1[0][:], in0=y1[0][:], in1=y2[0][:], op=ALU.add),
                       r=[y2[1]], w=[y1[1]])
                dve.op(lambda e: e.tensor_tensor(out=y3[0][:], in0=y3[0][:], in1=y1[0][:], op=ALU.add),
                       r=[y1[1]], w=[y3[1]])
                dve.op(lambda e: e.tensor_tensor(out=y3[0][:], in0=y3[0][:], in1=gate[s][0][:], op=ALU.mult),
                       r=[gate[s][1]], w=[y3[1]])
                for g in range(2):
                    act.op(lambda e: e.activation(out=junk[0][:, g * 256:(g + 1) * 256], in_=y3[0][:, g * 256:(g + 1) * 256],
                                                  func=AF.Square, accum_out=ssn[0][:, 1 + g:2 + g]),
                           r=[y3[1]], w=[junk[1], R_ss[1 + g]])
                self.rstd(ssn[0][:, 1:3], 256, rsn[0][:, 1:3], [R_ss[1], R_ss[2]], [R_rs[1], R_rs[2]])
                for g in range(2):
                    act.op(lambda e: e.activation(out=mixed[0][:, 512 + g * 256:512 + (g + 1) * 256],
                                                  in_=y3[0][:, g * 256:(g + 1) * 256], func=AF.Copy,
                                                  scale=rsn[0][:, 1 + g:2 + g]), r=[y3[1], R_rs[1 + g]], w=[R_mixs])
                self.state_update(tokt, R_tok, xdec[0], xdec[1], Hb[0], Hb[1], c, 1, bankid=7)
                for k in range(8):
                    pe.op(lambda e: e.transpose(out=tb3[:, k * 128:(k + 1) * 128], in_=mixed[0][:, k * 128:(k + 1) * 128],
                                                identity=identb), r=[R_mixa, R_mixs, self.R_cbf], w=[RB[3]])
                dve.op(lambda e: e.tensor_copy(out=mT[0][:].rearrange("p k t -> p (k t)"), in_=tb3[:, :]),
                       r=[RB[3]], w=[mT[1]])
                for cg in range(2):
                    for k in range(8):
                        pe.op(lambda e: e.matmul(bank[cg][:, :], lhsT=mT[0][:, k, :], rhs=wo[:, k, cg * 512:(cg + 1) * 512],
                                                 start=(k == 0), stop=(k == 7)), r=[mT[1], R_wo], w=[RB[cg]])
                x1t, R_x1 = x1[s]
                for cg in range(2):
                    dve.op(lambda e: e.tensor_tensor(out=x1t[:, cg * 512:(cg + 1) * 512], in0=bank[cg][:, :],
                                                     in1=xr[s][0][:, cg * 512:(cg + 1) * 512], op=ALU.add),
                           r=[RB[cg], xr[s][1]], w=[R_x1])
                sp.dma(self.x1s[c * 128:(c + 1) * 128, :], x1t[:], R_x1, r=[R_x1], w=[self.R_x1[c]])
                act.op(lambda e: e.activation(out=junk[0][:], in_=x1t[:], func=AF.Square, accum_out=ssn[0][:, 3:4]),
                       r=[R_x1], w=[junk[1], R_ss[3]])
                self.rstd(ssn[0][:, 3:4], D, rsn[0][:, 3:4], [R_ss[3]], [R_rs[3]])
                act.op(lambda e: e.activation(out=h2[0][:], in_=x1t[:], func=AF.Copy, scale=rsn[0][:, 3:4]),
                       r=[R_x1, R_rs[3]], w=[h2[1]])
                for k in range(8):
                    pe.op(lambda e: e.transpose(out=tb3[:, k * 128:(k + 1) * 128], in_=h2[0][:, k * 128:(k + 1) * 128],
                                                identity=identb), r=[h2[1], self.R_cbf], w=[RB[3]])
                dve.op(lambda e: e.tensor_copy(out=h2T[s][0][:].rearrange("p k t -> p (k t)"), in_=tb3[:, :]),
                       r=[RB[3]], w=[h2T[s][1]])
                sp.dma(self.h2T[:, :, 1 + c * 128:1 + (c + 1) * 128], h2T[s][0][:], h2T[s][1], r=[h2T[s][1]],
                       w=[self.R_h2T[c]])

            for b in range(nb - 1, -1, -1):
                stage1(b)
                if b + 1 < nb:
                    stage2(b + 1)
            stage2(0)

    def phase_f(self):
        nc, S, nb = self.nc, self.S, self.nb
        pe, act, dve, pool, sp = self.pe, self.act, self.dve, self.pool, self.sp
        RB = self.R_bank
        bank = self.bank
        NJ = DFF // 128
        with ExitStack() as es:
            wu, _ = self.tile(es, "wup", [128, 8, 2 * DFF], BF16)
            wd, _ = self.tile(es, "wdown", [128, NJ, D], BF16)
            gft, R_gf = self.tile(es, "gft", [128, D], F32)
            R_wu = [Res() for _ in range(NJ // 2)]
            R_wd = [Res() for _ in range(2)]
            for r in R_wu + R_wd:
                self.alltok_res.append(r)
            for p in range(NJ // 2):
                for base in (0, DFF):
                    c0 = base + p * 256
                    sp.dma(wu[:, :, c0:c0 + 256], self.wu_s[:, :, c0:c0 + 256], R_wu[p], w=[R_wu[p]])
            for h in range(2):
                sp.dma(wd[:, h * 11:(h + 1) * 11, :], self.wd_s[:, h * 11:(h + 1) * 11, :], R_wd[h], w=[R_wd[h]])
            sp.dma(gft[:], self.gf_d, R_gf, w=[R_gf])
            gf = gft[:]
            fcw = self.c_("fcw")
            fcb = self.c_("fcb")
            NW = FT + 2
            hw = [self.tile(es, "hwF%d" % i, [128, 8, NW], BF16) for i in range(2)]
            gT = self.tile(es, "gT", [128, NJ, FT], BF16)
            ag = [self.tile(es, "agF%d" % i, [128, FT], F32) for i in range(2)]
            av = [self.tile(es, "avF%d" % i, [128, FT], F32) for i in range(2)]
            tg = [self.tile(es, "tgF%d" % i, [128, FT], F32) for i in range(2)]
            x1 = [self.tile(es, "x1F%d" % i, [128, D], F32) for i in range(2)]
            x2 = [self.tile(es, "x2F%d" % i, [128, D], F32) for i in range(2)]
            junk = self.tile(es, "junkF", [128, D], BF16)
            ssf = self.tile(es, "ssf", [128, 2], F32)
            rsf = self.tile(es, "rsf", [128, 2], F32)
            R_ssf = [Res(), Res()]
            R_rsf = [Res(), Res()]
            tiles = []
            t0 = 0
            while t0 < S:
                tiles.append((t0, min(FT, S - t0)))
                t0 += FT
            blkctr = [0]

            def load(ti):
                t0, nt = tiles[ti]
                hwt, R_hw = hw[ti % 2]
                n = nt + 2
                b0, b1 = t0 // 128, min(nb - 1, (t0 + nt) // 128)
                rd = [self.R_h2pad] + [self.R_h2T[i] for i in range(max(0, b0 - 1), min(nb, b1 + 2))]
                sp.dma(hwt[:, :, 0:n], self.h2T[:, :, t0:t0 + n], R_hw, r=rd, w=[R_hw])

            def up(ti):
                t0, nt = tiles[ti]
                hwt, R_hw = hw[ti % 2]
                n = nt + 2
                for j in range(NJ):
                    p = j % 2
                    bg, bv = 2 * p, 2 * p + 1
                    for (bi, c0) in ((bg, j * 128), (bv, DFF + j * 128)):
                        pe.mm_group([(lambda e, k=k: e.matmul(bank[bi][:, 0:n], lhsT=wu[:, k, c0:c0 + 128],
                                                              rhs=hwt[:, k, 0:n], start=(k == 0), stop=(k == 7)))
                                     for k in range(8)], r=[R_wu[j // 2], R_hw], w=[RB[bi]])
                    agt, R_ag = ag[p]
                    avt, R_av = av[p]
                    tgt, R_tg = tg[p]
                    for (bi, acc, R_a, wc, bc_) in ((bg, agt, R_ag, j * 3, j), (bv, avt, R_av, 66 + j * 3, 22 + j)):
                        act.op(lambda e: e.activation(out=acc[:, 0:nt], in_=bank[bi][:, 0:nt], func=AF.Identity,
                                                      scale=fcw[:, wc:wc + 1], bias=fcb[:, bc_:bc_ + 1]),
                               r=[RB[bi], self.R_cst], w=[R_a])
                    for t in (1, 2):
                        for (bi, acc, R_a, wc) in ((bg, agt, R_ag, j * 3), (bv, avt, R_av, 66 + j * 3)):
                            dve.op(lambda e: e.scalar_tensor_tensor(out=acc[:, 0:nt], in0=bank[bi][:, t:t + nt],
                                                                    scalar=fcw[:, wc + t:wc + t + 1], in1=acc[:, 0:nt],
                                                                    op0=ALU.mult, op1=ALU.add),
                                   r=[RB[bi], self.R_cst], w=[R_a])
                    act.op(lambda e: e.activation(out=tgt[:, 0:nt], in_=agt[:, 0:nt], func=AF.Silu), r=[R_ag], w=[R_tg])
                    dve.op(lambda e: e.tensor_tensor(out=gT[0][:, j, 0:nt], in0=tgt[:, 0:nt], in1=avt[:, 0:nt], op=ALU.mult),
                           r=[R_tg, R_av], w=[gT[1]])

            def down(ti):
                t0, nt = tiles[ti]
                o = 0
                while o < nt:
                    m = min(128, nt - o)
                    q = blkctr[0] % 2
                    blkctr[0] += 1
                    r0 = t0 + o
                    x1t, R_x1 = x1[q]
                    x2t, R_x2 = x2[q]
                    bl0, bl1 = r0 // 128, (r0 + m - 1) // 128
                    sp.dma(x1t[0:m, :], self.x1s[r0:r0 + m, :], R_x1, r=[self.R_x1[i] for i in range(bl0, bl1 + 1)], w=[R_x1])
                    for cg in range(2):
                        bi = 4 + 2 * q + cg
                        pe.mm_group([(lambda e, j=j: e.matmul(bank[bi][0:m, :], lhsT=gT[0][:, j, o:o + m],
                                                              rhs=wd[:, j, cg * 512:(cg + 1) * 512], start=(j == 0),
                                                              stop=(j == NJ - 1))) for j in range(NJ)],
                                    r=[gT[1]] + R_wd, w=[RB[bi]])
                    for cg in range(2):
                        bi = 4 + 2 * q + cg
                        dve.op(lambda e: e.tensor_tensor(out=x2t[0:m, cg * 512:(cg + 1) * 512], in0=bank[bi][0:m, :],
                                                         in1=x1t[0:m, cg * 512:(cg + 1) * 512], op=ALU.add),
                               r=[RB[bi], R_x1], w=[R_x2])
                    act.op(lambda e: e.activation(out=junk[0][0:m, :], in_=x2t[0:m, :], func=AF.Square,
                                                  accum_out=ssf[0][0:m, q:q + 1]), r=[R_x2], w=[junk[1], R_ssf[q]])
                    self.rstd(ssf[0][0:m, q:q + 1], D, rsf[0][0:m, q:q + 1], [R_ssf[q]], [R_rsf[q]])
                    dve.op(lambda e: e.scalar_tensor_tensor(out=x2t[0:m, :], in0=x2t[0:m, :], scalar=rsf[0][0:m, q:q + 1],
                                                            in1=gf[0:m, :], op0=ALU.mult, op1=ALU.mult),
                           r=[R_rsf[q], R_gf], w=[R_x2])
                    sp.dma(self.out[r0:r0 + m, :], x2t[0:m, :], R_x2, r=[R_x2], w=[self.R_out])
                    o += m

            load(0)
            for ti in range(len(tiles)):
                if ti + 1 < len(tiles):
                    load(ti + 1)
                up(ti)
                down(ti)


def host_consts(S, p):
    nb = S // 128
    cst = np.zeros((128, CST_COLS), np.float32)

    def put(name, arr):
        a, b = _c[name]
        cst[:, a:b] = np.asarray(arr, np.float32).reshape(128, b - a)

    put("g1", p["norm1_g"][0].reshape(8, 128).T)
    put("g2", p["norm2_g"][0].reshape(8, 128).T)
    put("ga", p["attn_out_g"][0].reshape(4, 128).T)
    put("gs", p["ssd_norm_g"][0].reshape(4, 128).T)
    put("scw", p["ssd_conv_w"][0].reshape(5, 8, 128).transpose(2, 1, 0))
    put("scb", p["ssd_conv_b"][0].reshape(8, 128).T)
    put("fcw", p["ffn_conv_w"][0].reshape(3, 44, 128).transpose(2, 1, 0))
    put("fcb", p["ffn_conv_b"][0].reshape(44, 128).T)
    dtb = np.zeros((128, 1), np.float32)
    alog = np.zeros((128, 1), np.float32)
    sel = np.zeros((128, 3), np.float32)
    both_b = np.concatenate([p["ssd_dt_bias_fwd"][0], p["ssd_dt_bias_bwd"][0]])
    both_a = np.concatenate([p["ssd_a_log_fwd"][0], p["ssd_a_log_bwd"][0]])
    for grp in range(3):
        dtb[grp * 32:grp * 32 + 16, 0] = both_b
        alog[grp * 32:grp * 32 + 16, 0] = both_a
        sel[grp * 32:grp * 32 + 8, 0] = 1.0
        sel[grp * 32 + 8:grp * 32 + 16, 0] = -1.0
        sel[grp * 32 + 8:grp * 32 + 16, 1] = 1.0
        sel[grp * 32:grp * 32 + 8, 2] = 1.0
    put("dtb", dtb)
    put("alog", alog)
    put("sel", sel)
    put("dskip", np.tile(p["ssd_d"][0][None, :], (128, 1)))
    put("sink", np.tile(p["attn_sink"][0][None, :], (128, 1)))
    put("identf", np.eye(128, dtype=np.float32))
    cbf = np.zeros((128, CBF_COLS), np.float32)
    cbf[:, 0:128] = np.eye(128)
    pp = np.arange(128)[:, None]
    ff = np.arange(128)[None, :]
    mle = np.where(pp <= ff, 0.0, NEGM)
    mge = np.where(pp >= ff, 0.0, NEGM)
    cbf[:, 128:640] = np.tile(mle, (1, 4))
    cbf[:, 640:1152] = np.tile(mge, (1, 4))
    ind = np.zeros((128, 16, 128), np.float32)
    for grp in range(3):
        for hd in range(16):
            ind[grp * 32 + hd, hd, :] = 1.0
    cbf[:, 1152:] = ind.reshape(128, 2048)
    cbf = cbf.astype(ml_dtypes.bfloat16)
    pos = np.arange(S, dtype=np.float32)
    inv = (np.float32(500000.0) ** (-np.arange(0, 16, 2, dtype=np.float32) / np.float32(16))).astype(np.float32)
    ang = (pos[:, None] * inv[None, :]).astype(np.float32)
    tab = np.concatenate([np.cos(ang), np.sin(ang)], axis=1).astype(np.float32)
    rope = tab.reshape(nb, 128, 16).transpose(1, 0, 2).reshape(128, nb * 16).copy()
    gfc = np.ascontiguousarray(np.tile(np.asarray(p["final_norm_g"], np.float32)[None, :], (128, 1)))
    return cst, cbf, rope, gfc


_NC_CACHE = {}


def run(inputs, S, n_cores, dbg=False, phases="A1,D,AB,C,F"):
    p = {k: np.asarray(v) for k, v in inputs.items()}
    key = (S, dbg, phases)
    if key not in _NC_CACHE:
        _NC_CACHE[key] = KB(S, dbg=dbg, phases=phases).build()
    nc = _NC_CACHE[key]
    cst, cbf, rope, gfc = host_consts(S, p)
    shared = {
        "w_in": np.ascontiguousarray(p["w_in"][0], np.float32),
        "w_out": np.ascontiguousarray(p["w_out"][0], np.float32),
        "w_up": np.ascontiguousarray(p["w_up"][0], np.float32),
        "w_down": np.ascontiguousarray(p["w_down"][0], np.float32),
        "cst": cst, "cbf": cbf, "rope": rope, "gfc": gfc,
    }
    in_maps = []
    for c in range(n_cores):
        m = dict(shared)
        m["x"] = np.ascontiguousarray(p["x"][c], np.float32)
        in_maps.append(m)
    res = run_bass_kernel_spmd(nc, in_maps, core_ids=list(range(n_cores)))
    return res


def kernel(**inputs):
    res = run(inputs, SEQ, NCORES)
    return np.stack([np.asarray(r["out"], np.float32) for r in res.results], axis=0)
```

```python
import numpy as np
import ml_dtypes
from contextlib import ExitStack
import concourse.bass as bass
import concourse.mybir as mybir
from concourse.bass_utils import run_bass_kernel_spmd

F32 = mybir.dt.float32
BF16 = mybir.dt.bfloat16
AF = mybir.ActivationFunctionType
ALU = mybir.AluOpType

D = 1024
SEQ = 8192
NCORES = 8
EPS = 1e-5
NEGM = -30000.0
IN_COLS = 2320
DFF = 2816
FT = 382

_c = {}
_o = 0
for _n, _w in [("g1", 8), ("g2", 8), ("ga", 4), ("gs", 4), ("scw", 40), ("scb", 8), ("fcw", 132), ("fcb", 44),
               ("dtb", 1), ("alog", 1), ("sel", 3), ("dskip", 8), ("sink", 8), ("identf", 128)]:
    _c[_n] = (_o, _o + _w)
    _o += _w
CST_COLS = _o
_b = {"identb": (0, 128), "mle": (128, 640), "mge": (640, 1152), "ind": (1152, 1152 + 2048)}
CBF_COLS = 1152 + 2048


class Res:
    __slots__ = ("w", "r", "sem", "cnt")

    def __init__(self):
        self.w = None
        self.r = {}
        self.sem = None
        self.cnt = 0


class Eng:
    def __init__(self, kb, e, sem, is_pe=False):
        self.kb = kb
        self.e = e
        self.sem = sem
        self.cnt = 0
        self.seen = {}
        self.is_pe = is_pe
        self.pend_r = []
        self.pend_w = []

    def wait(self, tok, raw=True):
        if tok is None:
            return
        sem, val = tok
        if self.seen.get(sem, 0) >= val:
            return
        if sem is self.sem and (self.is_pe or not raw):
            return
        self.e.wait_ge(sem, val)
        self.seen[sem] = val

    def deps(self, reads, writes):
        for r in reads:
            self.wait(r.w, raw=True)
        for w in writes:
            self.wait(w.w, raw=True)
            for s, v in w.r.items():
                self.wait((s, v), raw=True)

    @staticmethod
    def mark(tok, reads, writes):
        for r in reads:
            if r.r.get(tok[0], 0) < tok[1]:
                r.r[tok[0]] = tok[1]
        for w in writes:
            w.w = tok
            w.r = {}

    def op(self, fn, r=(), w=(), inc=True):
        self.deps(r, w)
        ins = fn(self.e)
        if not inc:
            self.pend_r += list(r)
            self.pend_w += list(w)
            return None
        self.cnt += 1
        ins.then_inc(self.sem, 1)
        tok = (self.sem, self.cnt)
        self.mark(tok, list(r) + self.pend_r, list(w) + self.pend_w)
        self.pend_r = []
        self.pend_w = []
        return tok

    def mm_group(self, fns, r=(), w=()):
        tok = None
        for i, fn in enumerate(fns):
            tok = self.op(fn, r=r, w=w, inc=(i == len(fns) - 1))
        return tok

    def dma(self, out, in_, semres, r=(), w=(), slow=False):
        self.deps(r, w)
        if semres.sem is None:
            semres.sem = self.kb.new_sem()
        ins = self.e.dma_start(out=out, in_=in_, allow_slow_non_contiguous=True) if slow else self.e.dma_start(out=out, in_=in_)
        semres.cnt += 16
        ins.then_inc(semres.sem, 16)
        tok = (semres.sem, semres.cnt)
        self.mark(tok, r, w)
        return tok


class KB:
    def __init__(self, S, dbg=False, phases="A1,D,AB,C,F"):
        self.S = S
        self.nb = S // 128
        self.dbg = dbg
        self.phases = phases.split(",")
        self.nc = bass.Bass("TRN2", target_bir_lowering=False)
        self.es = ExitStack()
        self.nsem = 0
        self.alltok_res = []

    def new_sem(self):
        self.nsem += 1
        return self.es.enter_context(self.nc.semaphore("s%d" % self.nsem))

    def tile(self, es, name, shape, dt):
        t = es.enter_context(self.nc.sbuf_tensor("t_" + name, list(shape), dt))
        r = Res()
        self.alltok_res.append(r)
        return t, r

    def dram(self, name, shape, dt, kind="Internal"):
        return self.nc.dram_tensor(name, list(shape), dt, kind=kind).ap()

    def barrier(self):
        toks = [(e.sem, e.cnt) for e in self.engs if e.cnt > 0]
        for r in self.alltok_res + self.dram_res:
            if r.sem is not None and r.cnt > 0:
                toks.append((r.sem, r.cnt))
        for e in self.engs:
            for t in toks:
                e.wait(t)

    def build(self):
        nc, S, nb = self.nc, self.S, self.nb
        es = self.es
        okind = "ExternalOutput" if self.dbg else "Internal"
        self.x = self.dram("x", [S, D], F32, "ExternalInput")
        self.w_in = self.dram("w_in", [D, IN_COLS], F32, "ExternalInput")
        self.w_out = self.dram("w_out", [D, D], F32, "ExternalInput")
        self.w_up = self.dram("w_up", [D, 2 * DFF], F32, "ExternalInput")
        self.w_down = self.dram("w_down", [DFF, D], F32, "ExternalInput")
        self.cst_d = self.dram("cst", [128, CST_COLS], F32, "ExternalInput")
        self.cbf_d = self.dram("cbf", [128, CBF_COLS], BF16, "ExternalInput")
        self.rope_d = self.dram("rope", [128, nb * 16], F32, "ExternalInput")
        self.gf_d = self.dram("gfc", [128, D], F32, "ExternalInput")
        self.out = self.dram("out", [S, D], F32, "ExternalOutput")
        self.h1T = self.dram("h1T", [128, 8, S + 4], BF16, okind)
        self.xbcT = self.dram("xbcT", [nb, 128, 1024], BF16, okind)
        self.prevf = self.dram("prevf", [nb, 128, 512], BF16, okind)
        self.x1s = self.dram("x1s", [S, D], F32, okind)
        self.h2T = self.dram("h2T", [128, 8, S + 2], BF16, okind)
        self.wq_s = self.dram("wq_s", [128, 8, 1280], BF16)
        self.wo_s = self.dram("wo_s", [128, 8, 1024], BF16)
        self.wu_s = self.dram("wu_s", [128, 8, 2 * DFF], BF16)
        self.wd_s = self.dram("wd_s", [128, DFF // 128, D], BF16)
        self.R_wscr = Res()
        if self.dbg:
            self.dbgD = self.dram("dbgD", [128, nb * 32 + nb * 16], F32, "ExternalOutput")
        self.R_h1T = [Res() for _ in range(nb)]
        self.R_h1pad = Res()
        self.R_xbcT = [Res() for _ in range(nb)]
        self.R_prevf = [Res() for _ in range(nb)]
        self.R_x1 = [Res() for _ in range(nb)]
        self.R_h2T = [Res() for _ in range(nb)]
        self.R_h2pad = Res()
        self.R_out = Res()
        self.dram_res = self.R_h1T + self.R_xbcT + self.R_prevf + self.R_x1 + self.R_h2T + [self.R_h1pad, self.R_h2pad, self.R_out, self.R_wscr]

        self.pe = Eng(self, nc.tensor, self.new_sem(), is_pe=True)
        self.act = Eng(self, nc.scalar, self.new_sem())
        self.dve = Eng(self, nc.vector, self.new_sem())
        self.pool = Eng(self, nc.gpsimd, self.new_sem())
        self.sp = Eng(self, nc.sync, self.new_sem())
        self.engs = [self.pe, self.act, self.dve, self.pool, self.sp]

        self.bank = []
        self.R_bank = []
        for i in range(8):
            t = es.enter_context(nc.psum_tensor("bank%d" % i, [128, 512], F32))
            self.bank.append(t)
            self.R_bank.append(Res())

        self.cst, self.R_cst = self.tile(es, "cst", [128, CST_COLS], F32)
        self.cbf, self.R_cbf = self.tile(es, "cbf", [128, CBF_COLS], BF16)
        self.mh, self.R_mh = self.tile(es, "mh", [128, 8], F32)
        self.zero, self.R_zero = self.tile(es, "zero", [128, 16], BF16)
        self.sp.dma(self.cst[:], self.cst_d, self.R_cst, w=[self.R_cst])
        self.sp.dma(self.cbf[:], self.cbf_d, self.R_cbf, w=[self.R_cbf])
        self.epsc, self.R_epsc = self.tile(es, "epsc", [128, 2], F32)
        self.pool.op(lambda e: e.memset(self.epsc[:], EPS), w=[self.R_epsc])
        self.pool.op(lambda e: e.memset(self.mh[:], -0.5), w=[self.R_mh])
        self.pool.op(lambda e: e.memset(self.zero[:], 0.0), w=[self.R_zero])

        with ExitStack() as es_ssd:
            self.PHI3, self.R_PHI3 = self.tile(es_ssd, "PHI3", [128, S], BF16)
            self.PSI3, self.R_PSI3 = self.tile(es_ssd, "PSI3", [128, S], BF16)
            self.TM, self.R_TM = self.tile(es_ssd, "TM", [128, nb, 32], F32)
            self.DECB, self.R_DECB = self.tile(es_ssd, "DECB", [128, nb, 16], F32)
            with ExitStack() as es_x:
                self.X = []
                self.R_X = []
                for i in range(4):
                    t, r = self.tile(es_x, "X%d" % i, [128, S], F32)
                    self.X.append(t)
                    self.R_X.append(r)
                if "A1" in self.phases:
                    self.phase_a1()
                    self.barrier()
                if "D" in self.phases:
                    self.phase_d()
                    self.barrier()
            if "AB" in self.phases:
                self.phase_ab()
                self.barrier()
            if "C" in self.phases:
                self.phase_c()
                self.barrier()
        if "F" in self.phases:
            self.phase_f()
            self.barrier()
        self.barrier()
        self.es.close()
        return nc

    def c_(self, name, lo=None, hi=None):
        a, b = _c[name]
        if lo is not None:
            a, b = a + lo, a + hi
        return self.cst[:, a:b]

    def rstd(self, ss_ap, n, out_ap, rr, rw):
        k = out_ap.shape[1]
        self.pool.op(lambda e: e.tensor_scalar(out=out_ap, in0=ss_ap, scalar1=1.0 / n, scalar2=EPS,
                                               op0=ALU.mult, op1=ALU.add), r=rr, w=rw)
        self.pool.op(lambda e: e.tensor_tensor(out=out_ap, in0=out_ap, in1=self.mh[0:out_ap.shape[0], 0:k], op=ALU.pow),
                     r=[self.R_mh], w=rw)

    def prep_weight(self, es, src, ncols, emit, nk=8, tag="w"):
        stg = []
        for i in range(2):
            stg.append(self.tile(es, "stg_%s%d" % (tag, i), [128, ncols], F32))
        for k in range(nk):
            t, r = stg[k % 2]
            self.sp.dma(t[:], src[k * 128:(k + 1) * 128, :], r, w=[r])
            emit(k, t, r)

    def phase_a1(self):
        nc, S, nb = self.nc, self.S, self.nb
        pe, act, dve, pool, sp = self.pe, self.act, self.dve, self.pool, self.sp
        with ExitStack() as es:
            xa = [self.tile(es, "xa%d" % i, [128, D], F32) for i in range(3)]
            junk, R_junk = self.tile(es, "junkA", [128, D], BF16)
            hb = [self.tile(es, "hb%d" % i, [128, D], BF16) for i in range(2)]
            h1 = [self.tile(es, "h1_%d" % i, [128, 8, 128], BF16) for i in range(2)]
            ss, _ = self.tile(es, "ssA", [128, 4], F32)
            rs, _ = self.tile(es, "rsA", [128, 4], F32)
            R_ssl = [Res() for _ in range(4)]
            R_rsl = [Res() for _ in range(4)]
            wdt, R_wdt = self.tile(es, "wdt3", [128, 8, 96], BF16)
            wst, R_wst = self.tile(es, "wdtst", [128, 8, 16], F32)
            sp.dma(wst[:], self.w_in[:, 2304:2320].rearrange("(k p) c -> p k c", p=128), R_wst, w=[R_wst])
            dve.op(lambda e: e.memset(wdt[:], 0.0), w=[R_wdt])
            g1b = self.c_("g1").unsqueeze(2).to_broadcast([128, 8, 16])
            for grp in range(3):
                dve.op(lambda e: e.tensor_tensor(out=wdt[:, :, grp * 32:grp * 32 + 16], in0=wst[:], in1=g1b,
                                                 op=ALU.mult), r=[R_wst, self.R_cst], w=[R_wdt])
            sp.dma(self.h1T[:, :, 0:2], self.zero[:, 0:16].rearrange("p (k c) -> p k c", k=8), self.R_zero,
                   r=[self.R_zero], w=[self.R_h1pad])
            sp.dma(self.h1T[:, :, S + 2:S + 4], self.zero[:, 0:16].rearrange("p (k c) -> p k c", k=8), self.R_zero,
                   r=[self.R_zero], w=[self.R_h1pad])
            identb = self.cbf[:, 0:128]
            tbank = (3, 5)
            dbank = (4, 6)

            def s0(b):
                xt, R_x = xa[b % 3]
                sp.dma(xt[:], self.x[b * 128:(b + 1) * 128, :], R_x, w=[R_x])

            def s1(b):
                xt, R_x = xa[b % 3]
                q = b % 4
                act.op(lambda e: e.activation(out=junk[:], in_=xt[:], func=AF.Square, accum_out=ss[:, q:q + 1]),
                       r=[R_x], w=[R_junk, R_ssl[q]])
                act.op(lambda e: e.activation(out=rs[:, q:q + 1], in_=ss[:, q:q + 1], func=AF.Ln, scale=1.0 / D,
                                              bias=self.epsc[:, 0:1]), r=[R_ssl[q], self.R_epsc], w=[R_rsl[q]])
                act.op(lambda e: e.activation(out=rs[:, q:q + 1], in_=rs[:, q:q + 1], func=AF.Exp, scale=-0.5),
                       r=[], w=[R_rsl[q]])

            def s2(b):
                xt, R_x = xa[b % 3]
                q = b % 4
                hbt, R_hb = hb[b % 2]
                dve.op(lambda e: e.tensor_scalar(out=hbt[:], in0=xt[:], scalar1=rs[:, q:q + 1], scalar2=None,
                                                 op0=ALU.mult), r=[R_x, R_rsl[q]], w=[R_hb])

            def s3(b):
                hbt, R_hb = hb[b % 2]
                bi = tbank[b % 2]
                tb = self.bank[bi][:].bitcast(BF16)
                pe.mm_group([(lambda e, k=k: e.transpose(out=tb[:, k * 128:(k + 1) * 128],
                                                         in_=hbt[:, k * 128:(k + 1) * 128], identity=identb))
                             for k in range(8)], r=[R_hb, self.R_cbf], w=[self.R_bank[bi]])

            def s4(b):
                h1t, R_h1 = h1[b % 2]
                bi = tbank[b % 2]
                tb = self.bank[bi][:].bitcast(BF16)
                dve.op(lambda e: e.tensor_copy(out=h1t[:].rearrange("p k t -> p (k t)"), in_=tb[:, :]),
                       r=[self.R_bank[bi]], w=[R_h1])
                sp.dma(self.h1T[:, :, 2 + b * 128:2 + (b + 1) * 128], h1t[:], R_h1, r=[R_h1], w=[self.R_h1T[b]])

            def s5(b):
                h1t, R_h1 = h1[b % 2]
                bi = dbank[b % 2]
                pe.mm_group([(lambda e, k=k: e.matmul(self.bank[bi][0:96, 0:128], lhsT=wdt[:, k, :], rhs=h1t[:, k, :],
                                                      start=(k == 0), stop=(k == 7))) for k in range(8)],
                            r=[R_wdt, R_h1], w=[self.R_bank[bi]])

            def s6(b):
                bi = dbank[b % 2]
                act.op(lambda e: e.activation(out=self.X[0][0:96, b * 128:(b + 1) * 128], in_=self.bank[bi][0:96, 0:128],
                                              func=AF.Exp, bias=self.c_("dtb")[0:96, :]),
                       r=[self.R_bank[bi], self.R_cst], w=[self.R_X[0]])

            stages = [s0, s1, s2, s3, s4, s5, s6]
            for t in range(nb + len(stages) - 1):
                for si in range(len(stages) - 1, -1, -1):
                    b = t - si
                    if 0 <= b < nb:
                        stages[si](b)

    def phase_d(self):
        nc, S, nb = self.nc, self.S, self.nb
        pe, act, dve, pool, sp = self.pe, self.act, self.dve, self.pool, self.sp
        X, RX = self.X, self.R_X
        P = slice(0, 96)
        sgn = self.c_("sel", 0, 1)[P]
        bsel = self.c_("sel", 1, 2)[P]
        fsel = self.c_("sel", 2, 3)[P]
        with ExitStack() as es:
            negA, R_negA = self.tile(es, "negA", [128, 1], F32)
            CSE, R_CSE = self.tile(es, "CSE", [128, nb], F32)
            c1, R_c1 = self.tile(es, "c1", [128, nb], F32)
            c2, R_c2 = self.tile(es, "c2", [128, nb], F32)
            DEC, R_DEC = self.tile(es, "DEC", [128, nb], F32)
            Dg, R_Dg = self.tile(es, "Dg", [16, nb, 16], F32)
            ones, R_ones = self.tile(es, "onesD", [16, 128], F32)
            act.op(lambda e: e.activation(out=X[0][P], in_=X[0][P], func=AF.Ln, bias=1.0), r=[], w=[RX[0]])
            act.op(lambda e: e.activation(out=X[1][P], in_=X[0][P], func=AF.Ln), r=[RX[0]], w=[RX[1]])
            act.op(lambda e: e.activation(out=negA[P], in_=self.c_("alog")[P], func=AF.Exp), r=[self.R_cst], w=[R_negA])
            dve.op(lambda e: e.tensor_scalar(out=negA[P], in0=negA[P], scalar1=-1.0, scalar2=None, op0=ALU.mult),
                   r=[], w=[R_negA])
            dve.op(lambda e: e.tensor_scalar(out=X[2][P], in0=X[0][P], scalar1=negA[P], scalar2=None, op0=ALU.mult),
                   r=[RX[0], R_negA], w=[RX[2]])
            dve.op(lambda e: e.memset(self.PHI3[P], 1.0), w=[self.R_PHI3])
            dve.op(lambda e: e.memset(self.PHI3[P].rearrange("p (c j) -> p c j", j=128)[:, :, 0:1], 0.0),
                   w=[self.R_PHI3])
            dve.op(lambda e: e.tensor_tensor_scan(out=X[3][P], data0=self.PHI3[P], data1=X[2][P], initial=0.0,
                                                  op0=ALU.mult, op1=ALU.add), r=[self.R_PHI3, RX[2]], w=[RX[3]])
            dve.op(lambda e: e.tensor_copy(out=CSE[P], in_=X[3][P].rearrange("p (c j) -> p c j", j=128)[:, :, 127]),
                   r=[RX[3]], w=[R_CSE])
            dve.op(lambda e: e.tensor_scalar(out=X[2][P], in0=X[2][P], scalar1=bsel, scalar2=None, op0=ALU.mult),
                   r=[self.R_cst], w=[RX[2]])
            dve.op(lambda e: e.scalar_tensor_tensor(out=X[3][P], in0=X[3][P], scalar=sgn, in1=X[2][P],
                                                    op0=ALU.mult, op1=ALU.add), r=[RX[2], self.R_cst], w=[RX[3]])
            dve.op(lambda e: e.tensor_tensor(out=X[1][P], in0=X[1][P], in1=X[3][P], op=ALU.subtract),
                   r=[RX[3]], w=[RX[1]])
            dve.op(lambda e: e.tensor_scalar(out=c1[P], in0=CSE[P], scalar1=bsel, scalar2=None, op0=ALU.mult),
                   r=[R_CSE], w=[R_c1])
            dve.op(lambda e: e.tensor_scalar(out=c2[P], in0=CSE[P], scalar1=fsel, scalar2=None, op0=ALU.mult),
                   r=[R_CSE], w=[R_c2])
            v3 = lambda t: t[P].rearrange("p (c j) -> p c j", j=128)
            bc = lambda t: t[P].unsqueeze(2).to_broadcast([96, nb, 128])
            dve.op(lambda e: e.tensor_tensor(out=v3(X[2]), in0=v3(X[3]), in1=bc(c1), op=ALU.add),
                   r=[RX[3], R_c1], w=[RX[2]])
            act.op(lambda e: e.activation(out=X[2][P], in_=X[2][P], func=AF.Exp), r=[], w=[RX[2]])
            dve.op(lambda e: e.tensor_tensor(out=v3(X[0]), in0=v3(X[1]), in1=bc(c2), op=ALU.add),
                   r=[RX[1], R_c2], w=[RX[0]])
            act.op(lambda e: e.activation(out=X[0][P], in_=X[0][P], func=AF.Exp), r=[], w=[RX[0]])
            act.op(lambda e: e.activation(out=DEC[P], in_=CSE[P], func=AF.Exp), r=[R_CSE], w=[R_DEC])
            identf = self.c_("identf")
            for g0 in range(0, nb, 16):
                gn = min(16, nb - g0)
                bk = self.bank[0][:].rearrange("p (b c) -> p b c", c=32)
                for bi in range(gn):
                    b = g0 + bi
                    pe.op(lambda e: e.transpose(out=bk[:, bi, 0:16], in_=X[2][0:16, b * 128:(b + 1) * 128],
                                                identity=identf[0:16, 0:16]), r=[RX[2], self.R_cst], w=[self.R_bank[0]])
                    pe.op(lambda e: e.transpose(out=bk[:, bi, 16:32], in_=X[0][0:16, b * 128:(b + 1) * 128],
                                                identity=identf[0:16, 0:16]), r=[RX[0], self.R_cst], w=[self.R_bank[0]])
                dve.op(lambda e: e.tensor_copy(out=self.TM[:, g0:g0 + gn, :], in_=bk[:, 0:gn, :]),
                       r=[self.R_bank[0]], w=[self.R_TM])
            dve.op(lambda e: e.memset(ones[:], 1.0), w=[R_ones])
            dve.op(lambda e: e.tensor_tensor(out=Dg[:], in0=DEC[0:16].unsqueeze(2).to_broadcast([16, nb, 16]),
                                             in1=identf[0:16, 0:16].unsqueeze(1).to_broadcast([16, nb, 16]),
                                             op=ALU.mult), r=[R_DEC, self.R_cst], w=[R_Dg])
            for c0 in range(0, nb, 32):
                cn = min(32, nb - c0)
                pe.op(lambda e: e.matmul(self.bank[1][:, 0:cn * 16], lhsT=ones[:],
                                         rhs=Dg[:, c0:c0 + cn, :].rearrange("p b h -> p (b h)"), start=True, stop=True),
                      r=[R_ones, R_Dg], w=[self.R_bank[1]])
                dve.op(lambda e: e.tensor_copy(out=self.DECB[:, c0:c0 + cn, :].rearrange("p b h -> p (b h)"),
                                               in_=self.bank[1][:, 0:cn * 16]), r=[self.R_bank[1]], w=[self.R_DECB])
            if self.dbg:
                sp.dma(self.dbgD[:, 0:nb * 32], self.TM[:].rearrange("p b c -> p (b c)"), self.R_TM, r=[self.R_TM])
                sp.dma(self.dbgD[:, nb * 32:nb * 48], self.DECB[:].rearrange("p b c -> p (b c)"), self.R_DECB,
                       r=[self.R_DECB])
            for src, R_src, scr, R_scr, dst, R_dst in ((X[3], RX[3], X[2], RX[2], self.PHI3, self.R_PHI3),
                                                       (X[1], RX[1], X[0], RX[0], self.PSI3, self.R_PSI3)):
                dve.op(lambda e: e.tensor_copy(out=dst[0:96], in_=src[0:96]), r=[R_src], w=[R_dst])
                for (a, b) in ((32, 64), (64, 96)):
                    dve.op(lambda e: e.tensor_tensor(out=scr[a:b], in0=src[a:b], in1=dst[a:b], op=ALU.subtract),
                           r=[R_src, R_dst], w=[R_scr])
                for (a, b) in ((32, 64), (64, 96)):
                    dve.op(lambda e: e.tensor_copy(out=dst[a:b], in_=scr[a:b]), r=[R_scr], w=[R_dst])
                dve.op(lambda e: e.tensor_tensor(out=scr[64:96], in0=scr[64:96], in1=dst[64:96], op=ALU.subtract),
                       r=[R_dst], w=[R_scr])
                dve.op(lambda e: e.tensor_copy(out=dst[64:96], in_=scr[64:96]), r=[R_scr], w=[R_dst])

    def tok_transposes(self, xbt, R_xb, tokt, R_tok, bankid=3):
        tb = self.bank[bankid][:].bitcast(BF16)
        identb = self.cbf[:, 0:128]
        for j in range(6):
            self.pe.op(lambda e: e.transpose(out=tb[:, j * 128:(j + 1) * 128], in_=xbt[:, j, :], identity=identb),
                       r=[R_xb, self.R_cbf], w=[self.R_bank[bankid]])
        self.dve.op(lambda e: e.tensor_copy(out=tokt[:], in_=tb[:, 0:768]), r=[self.R_bank[bankid]], w=[R_tok])

    def state_update(self, tokt, R_tok, xdec, R_xdec, H, R_H, b, d, bankid):
        dve, pe, pool = self.dve, self.pe, self.pool
        sd = self.TM[:, b, 16 + 8 * d:24 + 8 * d].unsqueeze(2).to_broadcast([128, 8, 64])
        dve.op(lambda e: e.tensor_tensor(out=xdec[:].rearrange("p (h q) -> p h q", q=64),
                                         in0=tokt[:, 0:512].rearrange("p (h q) -> p h q", q=64), in1=sd, op=ALU.mult),
               r=[R_tok, self.R_TM], w=[R_xdec])
        for g in range(2):
            pe.op(lambda e: e.matmul(self.bank[bankid][:, g * 256:(g + 1) * 256],
                                     lhsT=tokt[:, 512 + g * 128:512 + (g + 1) * 128],
                                     rhs=xdec[:, g * 256:(g + 1) * 256], start=(g == 0), stop=(g == 1)),
                  r=[R_tok, R_xdec], w=[self.R_bank[bankid]])
        dec = self.DECB[:, b, 8 * d:8 * d + 8].unsqueeze(2).to_broadcast([128, 8, 64])
        dve.op(lambda e: e.tensor_tensor(out=H[:].rearrange("p (h q) -> p h q", q=64),
                                         in0=H[:].rearrange("p (h q) -> p h q", q=64), in1=dec, op=ALU.mult),
               r=[self.R_DECB], w=[R_H])
        dve.op(lambda e: e.tensor_tensor(out=H[:], in0=H[:], in1=self.bank[bankid][:, :], op=ALU.add),
               r=[self.R_bank[bankid]], w=[R_H])

    def conv_jobs(self):
        jobs = []
        for k in range(8):
            rows = slice(k * 128, (k + 1) * 128)
            jobs.append((self.w_in[rows, 0:512], self.wq_s[:, k, 0:512], "qperm", ("g1", k), 512))
            jobs.append((self.w_in[rows, 512:768], self.wq_s[:, k, 512:768], "scale", ("g1", k), 256))
            jobs.append((self.w_in[rows, 768:1280], self.wq_s[:, k, 768:1280], "scale", ("g1h", k), 512))
            jobs.append((self.w_out[rows, :], self.wo_s[:, k, :], "scale", ("ga", k) if k < 4 else ("gs", k - 4), 1024))
        for k in range(8):
            rows = slice(k * 128, (k + 1) * 128)
            for c0 in range(0, 2 * DFF, 1408):
                jobs.append((self.w_up[rows, c0:c0 + 1408], self.wu_s[:, k, c0:c0 + 1408], "scale", ("g2", k), 1408))
        for k in range(DFF // 128):
            jobs.append((self.w_down[k * 128:(k + 1) * 128, :], self.wd_s[:, k, :], "copy", None, 1024))
        return jobs

    def phase_ab(self):
        nc, S, nb = self.nc, self.S, self.nb
        pe, act, dve, pool, sp = self.pe, self.act, self.dve, self.pool, self.sp
        RB, bank = self.R_bank, self.bank
        tiles = [(b0, min(2, nb - b0)) for b0 in range(0, nb, 2)]
        nt = len(tiles)
        with ExitStack() as es:
            wx, R_wx = self.tile(es, "wxbc", [128, 8, 1024], BF16)
            g1h, R_g1h = self.tile(es, "g1h", [128, 8], F32)
            dve.op(lambda e: e.tensor_scalar(out=g1h[:], in0=self.c_("g1"), scalar1=0.5, scalar2=None, op0=ALU.mult),
                   r=[self.R_cst], w=[R_g1h])
            wst = [self.tile(es, "wxst%d" % i, [128, 1024], F32) for i in range(2)]
            for k in range(8):
                t, r = wst[k % 2]
                sp.dma(t[:], self.w_in[k * 128:(k + 1) * 128, 1280:2304], r, w=[r])
                if k % 2 == 0:
                    dve.op(lambda e: e.tensor_scalar(out=wx[:, k, :], in0=t[:], scalar1=self.c_("g1", k, k + 1),
                                                     scalar2=None, op0=ALU.mult), r=[r, self.R_cst], w=[R_wx])
                else:
                    act.op(lambda e: e.activation(out=wx[:, k, :], in_=t[:], func=AF.Copy,
                                                  scale=self.c_("g1", k, k + 1)), r=[r, self.R_cst], w=[R_wx])
            scw = self.c_("scw")
            scb = self.c_("scb")
            hw = [self.tile(es, "hw%d" % i, [128, 8, 260], BF16) for i in range(3)]
            acc = [self.tile(es, "accAB%d" % i, [128, 256], F32) for i in range(4)]
            xb2 = [self.tile(es, "xb2_%d" % i, [128, 8, 256], BF16) for i in range(2)]
            tok = [self.tile(es, "tokAB%d" % i, [128, 768], BF16) for i in range(2)]
            xdec = [self.tile(es, "xdecAB%d" % i, [128, 512], BF16) for i in range(2)]
            H, R_H = self.tile(es, "HfAB", [128, 512], F32)
            pv = [self.tile(es, "pvAB%d" % i, [128, 512], BF16) for i in range(2)]
            stg = [self.tile(es, "cvs%d" % i, [128, 1408], F32) for i in range(3)]
            cvt = [self.tile(es, "cvo%d" % i, [128, 1408], BF16) for i in range(2)]
            dve.op(lambda e: e.memset(H[:], 0.0), w=[R_H])
            identb = self.cbf[:, 0:128]
            xpb = [0, 1, 2, 3]
            trb = [4, 5]
            stb = [6, 7]
            jobs = self.conv_jobs()
            gtab = {"g1": self.c_("g1"), "g2": self.c_("g2"), "ga": self.c_("ga"), "gs": self.c_("gs"), "g1h": g1h[:]}
            gres = {"g1": self.R_cst, "g2": self.R_cst, "ga": self.R_cst, "gs": self.R_cst, "g1h": R_g1h}
            jstate = {"loaded": 0, "done": 0}

            def job_load(i):
                if i >= len(jobs):
                    return
                src, dst, kind, g, n = jobs[i]
                t, r = stg[i % 3]
                act.dma(t[:, 0:n], src, r, w=[r])

            def job_run(i):
                if i >= len(jobs):
                    return
                src, dst, kind, g, n = jobs[i]
                t, r = stg[i % 3]
                o, ro = cvt[i % 2]
                if kind == "copy":
                    act.op(lambda e: e.activation(out=o[:, 0:n], in_=t[:, 0:n], func=AF.Copy), r=[r], w=[ro])
                else:
                    gcol = gtab[g[0]][:, g[1]:g[1] + 1]
                    if kind == "qperm":
                        for hh in range(2):
                            act.op(lambda e: e.activation(
                                out=o[:, 0:512].rearrange("p (i hh d) -> p hh i d", i=4, hh=2)[:, hh],
                                in_=t[:, hh * 256:(hh + 1) * 256].rearrange("p (i d) -> p i d", i=4),
                                func=AF.Copy, scale=gcol), r=[r, gres[g[0]]], w=[ro])
                    else:
                        act.op(lambda e: e.activation(out=o[:, 0:n], in_=t[:, 0:n], func=AF.Copy, scale=gcol),
                               r=[r, gres[g[0]]], w=[ro])
                act.dma(dst, o[:, 0:n], ro, r=[ro])

            def do_jobs(cnt):
                for _ in range(cnt):
                    i = jstate["done"]
                    if i >= len(jobs):
                        return
                    while jstate["loaded"] < min(len(jobs), i + 3):
                        job_load(jstate["loaded"])
                        jstate["loaded"] += 1
                    job_run(i)
                    jstate["done"] += 1

            per_iter = -(-len(jobs) // nt)
            cc = {"n": 0}

            def load(T):
                b0, n = tiles[T]
                W = n * 128
                hwt, R_hw = hw[T % 3]
                rd = [self.R_h1pad] + [self.R_h1T[i] for i in range(b0 - 1, b0 + n + 1) if 0 <= i < nb]
                sp.dma(hwt[:, :, 0:W + 4], self.h1T[:, :, b0 * 128:b0 * 128 + W + 4], R_hw, r=rd, w=[R_hw])

            def chunk_pair(T, jp):
                b0, n = tiles[T]
                W = n * 128
                hwt, R_hw = hw[T % 3]
                xbt, R_xb = xb2[T % 2]
                info = []
                for j in (2 * jp, 2 * jp + 1):
                    c = cc["n"]
                    cc["n"] += 1
                    bi = xpb[c % 4]
                    at, R_a = acc[c % 4]
                    pe.mm_group([(lambda e, k=k: e.matmul(bank[bi][:, 0:W + 4], lhsT=wx[:, k, j * 128:(j + 1) * 128],
                                                          rhs=hwt[:, k, 0:W + 4], start=(k == 0), stop=(k == 7)))
                                 for k in range(8)], r=[R_wx, R_hw], w=[RB[bi]])
                    act.op(lambda e: e.activation(out=at[:, 0:W], in_=bank[bi][:, 0:W], func=AF.Identity,
                                                  scale=scw[:, j * 5:j * 5 + 1], bias=scb[:, j:j + 1]),
                           r=[RB[bi], self.R_cst], w=[R_a])
                    info.append((j, bi, at, R_a))
                for t in range(1, 5):
                    for (j, bi, at, R_a) in info:
                        dve.op(lambda e: e.scalar_tensor_tensor(out=at[:, 0:W], in0=bank[bi][:, t:t + W],
                                                                scalar=scw[:, j * 5 + t:j * 5 + t + 1], in1=at[:, 0:W],
                                                                op0=ALU.mult, op1=ALU.add),
                               r=[RB[bi], self.R_cst], w=[R_a])
                for (j, bi, at, R_a) in info:
                    act.op(lambda e: e.activation(out=xbt[:, j, 0:W], in_=at[:, 0:W], func=AF.Silu), r=[R_a], w=[R_xb])

            def store(T):
                b0, n = tiles[T]
                xbt, R_xb = xb2[T % 2]
                for bi_ in range(n):
                    b = b0 + bi_
                    sp.dma(self.xbcT[b].rearrange("p (k t) -> p k t", k=8), xbt[:, :, bi_ * 128:(bi_ + 1) * 128], R_xb,
                           r=[R_xb], w=[self.R_xbcT[b]])

            def tr(T):
                b0, n = tiles[T]
                xbt, R_xb = xb2[T % 2]
                for bi_ in range(n):
                    b = b0 + bi_
                    tbk = trb[bi_]
                    tb = bank[tbk][:].bitcast(BF16)
                    pe.mm_group([(lambda e, j=j: e.transpose(out=tb[:, j * 128:(j + 1) * 128],
                                                             in_=xbt[:, j, bi_ * 128:(bi_ + 1) * 128], identity=identb))
                                 for j in range(6)], r=[R_xb, self.R_cbf], w=[RB[tbk]])

            def evac(T):
                b0, n = tiles[T]
                for bi_ in range(n):
                    b = b0 + bi_
                    tbk = trb[bi_]
                    tb = bank[tbk][:].bitcast(BF16)
                    tokt, R_tok = tok[b % 2]
                    xd, R_xd = xdec[b % 2]
                    dve.op(lambda e: e.tensor_copy(out=tokt[:], in_=tb[:, 0:768]), r=[RB[tbk]], w=[R_tok])
                    sd = self.TM[:, b, 16:24].unsqueeze(2).to_broadcast([128, 8, 64])
                    dve.op(lambda e: e.tensor_tensor(out=xd[:].rearrange("p (h q) -> p h q", q=64),
                                                     in0=tokt[:, 0:512].rearrange("p (h q) -> p h q", q=64), in1=sd,
                                                     op=ALU.mult), r=[R_tok, self.R_TM], w=[R_xd])

            def states(T):
                b0, n = tiles[T]
                for bi_ in range(n):
                    b = b0 + bi_
                    tokt, R_tok = tok[b % 2]
                    xd, R_xd = xdec[b % 2]
                    sb = stb[bi_]
                    pe.mm_group([(lambda e, g=g: e.matmul(bank[sb][:, g * 256:(g + 1) * 256],
                                                          lhsT=tokt[:, 512 + g * 128:512 + (g + 1) * 128],
                                                          rhs=xd[:, g * 256:(g + 1) * 256], start=(g == 0), stop=(g == 1)))
                                 for g in range(2)], r=[R_tok, R_xd], w=[RB[sb]])

            def scan(T, bi_):
                b0, n = tiles[T]
                if bi_ >= n:
                    return
                b = b0 + bi_
                sb = stb[bi_]
                pvt, R_pv = pv[b % 2]
                act.op(lambda e: e.activation(out=pvt[:], in_=H[:], func=AF.Copy), r=[R_H], w=[R_pv])
                sp.dma(self.prevf[b], pvt[:], R_pv, r=[R_pv], w=[self.R_prevf[b]])
                dec = self.DECB[:, b, 0:8].unsqueeze(2).to_broadcast([128, 8, 64])
                dve.op(lambda e: e.tensor_tensor(out=H[:].rearrange("p (h q) -> p h q", q=64),
                                                 in0=H[:].rearrange("p (h q) -> p h q", q=64), in1=dec, op=ALU.mult),
                       r=[self.R_DECB], w=[R_H])
                dve.op(lambda e: e.tensor_tensor(out=H[:], in0=H[:], in1=bank[sb][:, :], op=ALU.add),
                       r=[RB[sb]], w=[R_H])

            load(0)
            if nt > 1:
                load(1)
            for T in range(nt + 1):
                P = T - 1
                if T + 2 < nt:
                    load(T + 2)
                if P >= 0:
                    tr(P)
                if T < nt:
                    chunk_pair(T, 0)
                if P >= 0:
                    evac(P)
                    states(P)
                if T < nt:
                    chunk_pair(T, 1)
                if P >= 0:
                    scan(P, 0)
                if T < nt:
                    chunk_pair(T, 2)
                do_jobs(per_iter)
                if T < nt:
                    chunk_pair(T, 3)
                if P >= 0:
                    scan(P, 1)
                if T < nt:
                    store(T)
            do_jobs(len(jobs))

    def phase_c(self):
        nc, S, nb = self.nc, self.S, self.nb
        pe, act, dve, pool, sp = self.pe, self.act, self.dve, self.pool, self.sp
        RB = self.R_bank
        bank = self.bank
        identb = self.cbf[:, 0:128]
        mle = self.cbf[:, _b["mle"][0]:_b["mle"][1]]
        mge = self.cbf[:, _b["mge"][0]:_b["mge"][1]]
        ind = self.cbf[:, _b["ind"][0]:_b["ind"][1]].rearrange("p (h t) -> p h t", t=128)
        with ExitStack() as es:
            wq, R_wq = self.tile(es, "wqkvz", [128, 8, 1280], BF16)
            wo, R_wo = self.tile(es, "wout", [128, 8, 1024], BF16)
            rope, R_rope = self.tile(es, "rope", [128, nb, 16], F32)
            esink, R_esink = self.tile(es, "esink", [128, 8], F32)
            sp.dma(rope[:].rearrange("p b c -> p (b c)"), self.rope_d, R_rope, w=[R_rope])
            act.op(lambda e: e.activation(out=esink[:], in_=self.c_("sink"), func=AF.Exp), r=[self.R_cst], w=[R_esink])
            sp.dma(wq[:], self.wq_s, R_wq, w=[R_wq])
            sp.dma(wo[:], self.wo_s, R_wo, w=[R_wo])
            Dm, R_Dm = self.tile(es, "Dm", [128, 8, 128], BF16)
            for h in range(8):
                dve.op(lambda e: e.tensor_scalar(out=Dm[:, h, :], in0=identb, scalar1=self.c_("dskip", h, h + 1),
                                                 scalar2=None, op0=ALU.mult), r=[self.R_cbf, self.R_cst], w=[R_Dm])
            NR = 3
            hc = [self.tile(es, "hc%d" % i, [128, 8, 128], BF16) for i in range(2)]
            qk = self.tile(es, "qktm", [128, 640], BF16)
            rtmp = [self.tile(es, "rtmp%d" % i, [128, 10, 8], F32) for i in range(4)]
            qT = [self.tile(es, "qT%d" % i, [128, 4, 128], BF16) for i in range(2)]
            KP = [[self.tile(es, "KP%d_%d" % (i, h), [128, 128], BF16) for h in range(2)] for i in range(NR)]
            V1 = [self.tile(es, "V1_%d" % i, [128, 2, 65], BF16) for i in range(NR)]
            gate = [self.tile(es, "gate%d" % i, [128, 512], F32) for i in range(2)]
            thz = self.tile(es, "thz", [128, 512], F32)
            xb = [self.tile(es, "xbC%d" % i, [128, 8, 128], BF16) for i in range(2)]
            pf = [self.tile(es, "pfC%d" % i, [128, 512], BF16) for i in range(2)]
            tok = [self.tile(es, "tokC%d" % i, [128, 768], BF16) for i in range(2)]
            PT = [self.tile(es, "PT%d" % i, [128, 512], BF16) for i in range(3)]
            den = self.tile(es, "den", [128, 8], F32)
            att = self.tile(es, "att", [128, 512], F32)
            junk = self.tile(es, "junkC", [128, 1024], F32)
            ssn = self.tile(es, "ssn", [128, 4], F32)
            rsn = self.tile(es, "rsn", [128, 4], F32)
            R_ss = [Res() for _ in range(4)]
            R_rs = [Res() for _ in range(4)]
            mixed = self.tile(es, "mixed", [128, 1024], BF16)
            R_mixa, R_mixs = Res(), Res()
            LT = [self.tile(es, "LT%d" % i, [128, 8, 128], F32) for i in range(2)]
            WT = [self.tile(es, "WT%d" % i, [128, 8, 128], BF16) for i in range(2)]
            ytmp = [self.tile(es, "ytmp%d" % i, [128, 512], F32) for i in range(4)]
            xdec = self.tile(es, "xdecC", [128, 512], BF16)
            Hb = self.tile(es, "HbC", [128, 512], F32)
            pvb = self.tile(es, "pvbC", [128, 512], BF16)
            mT = self.tile(es, "mT", [128, 8, 128], BF16)
            xr = [self.tile(es, "xr%d" % i, [128, D], F32) for i in range(2)]
            x1 = [self.tile(es, "x1_%d" % i, [128, D], F32) for i in range(2)]
            h2 = self.tile(es, "h2", [128, D], BF16)
            h2T = [self.tile(es, "h2T%d" % i, [128, 8, 128], BF16) for i in range(2)]
            dve.op(lambda e: e.memset(Hb[0][:], 0.0), w=[Hb[1]])
            for i in range(NR):
                for h in range(2):
                    dve.op(lambda e: e.memset(KP[i][h][0][:], 0.0), w=[KP[i][h][1]])
                dve.op(lambda e: e.memset(V1[i][0][:], 1.0), w=[V1[i][1]])
            sp.dma(self.h2T[:, :, 0:1], self.zero[:, 0:8].rearrange("p (k c) -> p k c", k=8), self.R_zero,
                   r=[self.R_zero], w=[self.R_h2pad], slow=True)
            sp.dma(self.h2T[:, :, S + 1:S + 2], self.zero[:, 0:8].rearrange("p (k c) -> p k c", k=8), self.R_zero,
                   r=[self.R_zero], w=[self.R_h2pad], slow=True)
            tb3 = bank[3][:].bitcast(BF16)

            def stage1(b):
                s = b % 2
                hct, R_hc = hc[s]
                sp.dma(hct[:], self.h1T[:, :, 2 + b * 128:2 + (b + 1) * 128], R_hc, r=[self.R_h1T[b]], w=[R_hc])
                sp.dma(xb[s][0][:].rearrange("p k t -> p (k t)"), self.xbcT[b], xb[s][1], r=[self.R_xbcT[b]], w=[xb[s][1]])
                sp.dma(pf[s][0][:], self.prevf[b], pf[s][1], r=[self.R_prevf[b]], w=[pf[s][1]])
                sp.dma(xr[s][0][:], self.x[b * 128:(b + 1) * 128, :], xr[s][1], w=[xr[s][1]])
                for (c0, cn, bi) in ((0, 512, 0), (512, 512, 1), (1024, 256, 2)):
                    pe.mm_group([(lambda e, k=k: e.matmul(bank[bi][:, 0:cn], lhsT=hct[:, k, :], rhs=wq[:, k, c0:c0 + cn],
                                                          start=(k == 0), stop=(k == 7))) for k in range(8)],
                                r=[R_hc, R_wq], w=[RB[bi]])
                cosb = rope[:, b, 0:8]
                sinb = rope[:, b, 8:16]
                for (bi, c0, nh, o0) in ((0, 0, 8, 0), (1, 0, 2, 512)):
                    src = bank[bi][:, c0:c0 + nh * 64].rearrange("p (h d) -> p h d", d=64)
                    dst = qk[0][:, o0:o0 + nh * 64].rearrange("p (h d) -> p h d", d=64)
                    cb_ = cosb.unsqueeze(1).to_broadcast([128, nh, 8])
                    sb_ = sinb.unsqueeze(1).to_broadcast([128, nh, 8])
                    t0, t1, t2, t3 = [rtmp[i][0][:, 0:nh, :] for i in range(4)]
                    rr = [RB[bi], R_rope]
                    dve.op(lambda e: e.tensor_tensor(out=t0, in0=src[:, :, 0:8], in1=cb_, op=ALU.mult), r=rr, w=[rtmp[0][1]])
                    dve.op(lambda e: e.tensor_tensor(out=t1, in0=src[:, :, 8:16], in1=sb_, op=ALU.mult), r=rr, w=[rtmp[1][1]])
                    dve.op(lambda e: e.tensor_tensor(out=t2, in0=src[:, :, 8:16], in1=cb_, op=ALU.mult), r=rr, w=[rtmp[2][1]])
                    dve.op(lambda e: e.tensor_tensor(out=t3, in0=src[:, :, 0:8], in1=sb_, op=ALU.mult), r=rr, w=[rtmp[3][1]])
                    dve.op(lambda e: e.tensor_tensor(out=dst[:, :, 0:8], in0=t0, in1=t1, op=ALU.subtract),
                           r=[rtmp[0][1], rtmp[1][1]], w=[qk[1]])
                    dve.op(lambda e: e.tensor_tensor(out=dst[:, :, 8:16], in0=t2, in1=t3, op=ALU.add),
                           r=[rtmp[2][1], rtmp[3][1]], w=[qk[1]])
                    act.op(lambda e: e.activation(out=dst[:, :, 16:64], in_=src[:, :, 16:64], func=AF.Copy),
                           r=[RB[bi]], w=[qk[1]])
                v1t, R_v1 = V1[b % NR]
                act.op(lambda e: e.activation(out=v1t[:, :, 0:64],
                                              in_=bank[1][:, 128:256].rearrange("p (h d) -> p h d", d=64), func=AF.Copy),
                       r=[RB[1]], w=[R_v1])
                zsrc = lambda: [bank[1][:, 256:512], bank[2][:, 0:256]]
                for zi in range(2):
                    zs = zsrc()[zi]
                    act.op(lambda e: e.activation(out=thz[0][:, zi * 256:(zi + 1) * 256], in_=zs, func=AF.Tanh),
                           r=[RB[1 + zi]], w=[thz[1]])
                    dve.op(lambda e: e.scalar_tensor_tensor(out=gate[s][0][:, zi * 256:(zi + 1) * 256],
                                                            in0=thz[0][:, zi * 256:(zi + 1) * 256], scalar=1.0, in1=zs,
                                                            op0=ALU.add, op1=ALU.mult),
                           r=[thz[1], RB[1 + zi]], w=[gate[s][1]])
                pe.mm_group([(lambda e, j=j: e.transpose(out=tb3[:, j * 128:(j + 1) * 128],
                                                         in_=qk[0][:, j * 128:(j + 1) * 128], identity=identb))
                             for j in range(5)], r=[qk[1], self.R_cbf], w=[RB[3]])
                dve.op(lambda e: e.tensor_copy(out=qT[s][0][:].rearrange("p c t -> p (c t)"), in_=tb3[:, 0:512]),
                       r=[RB[3]], w=[qT[s][1]])
                for h in range(2):
                    kpt, R_kp = KP[b % NR][h]
                    dve.op(lambda e: e.tensor_copy(out=kpt[h * 64:(h + 1) * 64, :], in_=tb3[h * 64:(h + 1) * 64, 512:640]),
                           r=[RB[3]], w=[R_kp])
                self.tok_transposes(xb[s][0], xb[s][1], tok[s][0], tok[s][1], bankid=3)

            def stage2(c):
                s = c % 2
                kbs = [kb for kb in (c - 1, c, c + 1) if 0 <= kb < nb]
                for half in range(2):
                    for ki, kb in enumerate(kbs):
                        bi = 4 + ki
                        last = (kb == c)
                        pe.op(lambda e: e.matmul(bank[bi][:, :], lhsT=KP[kb % NR][half][0][:, :],
                                                 rhs=qT[s][0][:].rearrange("p c t -> p (c t)"), start=True, stop=last),
                              r=[KP[kb % NR][half][1], qT[s][1]], w=[RB[bi]])
                        if not last:
                            m = mge if kb < c else mle
                            pe.op(lambda e: e.matmul(bank[bi][:, :], lhsT=identb, rhs=m, start=False, stop=True),
                                  r=[self.R_cbf], w=[RB[bi]])
                        act.op(lambda e: e.activation(out=PT[ki][0][:], in_=bank[bi][:, :], func=AF.Exp, scale=0.125),
                               r=[RB[bi]], w=[PT[ki][1]])
                    po = bank[7][:, 0:260].rearrange("p (h d) -> p h d", d=65)
                    for i in range(4):
                        for ki, kb in enumerate(kbs):
                            pe.op(lambda e: e.matmul(po[:, i, :], lhsT=PT[ki][0][:, i * 128:(i + 1) * 128],
                                                     rhs=V1[kb % NR][0][:, half, :], start=(ki == 0), stop=(ki == len(kbs) - 1)),
                                  r=[PT[ki][1], V1[kb % NR][1]], w=[RB[7]])
                    hs = slice(half * 4, half * 4 + 4)
                    dve.op(lambda e: e.tensor_tensor(out=den[0][:, hs], in0=po[:, :, 64], in1=esink[:, hs], op=ALU.add),
                           r=[RB[7], R_esink], w=[den[1]])
                    dve.op(lambda e: e.reciprocal(out=den[0][:, hs], in_=den[0][:, hs]), r=[], w=[den[1]])
                    dve.op(lambda e: e.tensor_tensor(
                        out=att[0][:, half * 256:(half + 1) * 256].rearrange("p (h d) -> p h d", d=64),
                        in0=po[:, :, 0:64], in1=den[0][:, hs].unsqueeze(2).to_broadcast([128, 4, 64]), op=ALU.mult),
                        r=[RB[7], den[1]], w=[att[1]])
                act.op(lambda e: e.activation(out=junk[0][:, 0:512], in_=att[0][:], func=AF.Square, accum_out=ssn[0][:, 0:1]),
                       r=[att[1]], w=[junk[1], R_ss[0]])
                self.rstd(ssn[0][:, 0:1], 512, rsn[0][:, 0:1], [R_ss[0]], [R_rs[0]])
                act.op(lambda e: e.activation(
                    out=mixed[0][:, 0:512].rearrange("p (hh i d) -> p hh i d", hh=2, i=4),
                    in_=att[0][:].rearrange("p (hh i d) -> p hh i d", hh=2, i=4), func=AF.Copy, scale=rsn[0][:, 0:1]),
                    r=[att[1], R_rs[0]], w=[R_mixa])
                xbt, R_xb = xb[s]
                tokt, R_tok = tok[s]
                for g in range(2):
                    pe.op(lambda e: e.matmul(bank[0][:, g * 128:(g + 1) * 128], lhsT=xbt[:, 4 + g, :], rhs=xbt[:, 6 + g, :],
                                             start=(g == 0), stop=(g == 1)), r=[R_xb], w=[RB[0]])
                cols = slice(c * 128, (c + 1) * 128)
                for d in range(2):
                    msk = mle if d == 0 else mge
                    for g in range(2):
                        bi = 1 + g
                        hd0 = d * 8 + g * 4
                        for hh in range(4):
                            pe.op(lambda e: e.matmul(bank[bi][:, hh * 128:(hh + 1) * 128], lhsT=ind[0:96, hd0 + hh, :],
                                                     rhs=self.PHI3[0:96, cols], start=(hh == 0), stop=False),
                                  r=[self.R_cbf, self.R_PHI3], w=[RB[bi]])
                        pe.op(lambda e: e.matmul(bank[bi][:, :], lhsT=self.PSI3[0:96, cols],
                                                 rhs=ind[0:96, hd0:hd0 + 4, :].rearrange("p h t -> p (h t)"),
                                                 start=False, stop=False), r=[self.R_cbf, self.R_PSI3], w=[RB[bi]])
                        pe.op(lambda e: e.matmul(bank[bi][:, :], lhsT=identb, rhs=msk, start=False, stop=True),
                              r=[self.R_cbf], w=[RB[bi]])
                        act.op(lambda e: e.activation(out=LT[d][0][:, g * 4:(g + 1) * 4, :].rearrange("p h t -> p (h t)"),
                                                      in_=bank[bi][:, :], func=AF.Exp), r=[RB[bi]], w=[LT[d][1]])
                    cbb = bank[0][:, 0:256].rearrange("p (g t) -> p g t", g=2).unsqueeze(2).to_broadcast([128, 2, 4, 128])
                    dve.op(lambda e: e.tensor_tensor(out=WT[d][0][:].rearrange("p (g h) t -> p g h t", g=2),
                                                     in0=LT[d][0][:].rearrange("p (g h) t -> p g h t", g=2), in1=cbb,
                                                     op=ALU.mult), r=[LT[d][1], RB[0]], w=[WT[d][1]])
                fns = []
                for h in range(8):
                    for (i, lt) in enumerate((WT[0][0][:, h, :], WT[1][0][:, h, :], Dm[:, h, :])):
                        fns.append(lambda e, h=h, i=i, lt=lt: e.matmul(bank[4][:, h * 64:(h + 1) * 64], lhsT=lt,
                                                                      rhs=tokt[:, h * 64:(h + 1) * 64],
                                                                      start=(h == 0 and i == 0), stop=(h == 7 and i == 2)))
                pe.mm_group(fns, r=[WT[0][1], WT[1][1], R_Dm, R_tok], w=[RB[4]])
                act.op(lambda e: e.activation(out=pvb[0][:], in_=Hb[0][:], func=AF.Copy), r=[Hb[1]], w=[pvb[1]])
                for d in range(2):
                    pv_t, R_pvx = (pf[s] if d == 0 else pvb)
                    for g in range(2):
                        pe.op(lambda e: e.matmul(bank[5 + d][:, g * 256:(g + 1) * 256], lhsT=xbt[:, 6 + g, :],
                                                 rhs=pv_t[:, g * 256:(g + 1) * 256], start=(g == 0), stop=(g == 1)),
                              r=[R_xb, R_pvx], w=[RB[5 + d]])
                v3 = lambda ap: ap.rearrange("p (h q) -> p h q", q=64)
                eyf = self.TM[:, c, 0:8].unsqueeze(2).to_broadcast([128, 8, 64])
                eyb = self.TM[:, c, 8:16].unsqueeze(2).to_broadcast([128, 8, 64])
                dsk = self.c_("dskip").unsqueeze(2).to_broadcast([128, 8, 64])
                y0, y1, y2, y3 = ytmp
                dve.op(lambda e: e.tensor_tensor(out=v3(y1[0][:]), in0=v3(bank[5][:, :]), in1=eyf, op=ALU.mult),
                       r=[RB[5], self.R_TM], w=[y1[1]])
                dve.op(lambda e: e.tensor_tensor(out=v3(y2[0][:]), in0=v3(bank[6][:, :]), in1=eyb, op=ALU.mult),
                       r=[RB[6], self.R_TM], w=[y2[1]])
                dve.op(lambda e: e.tensor_tensor(out=y3[0][:], in0=bank[4][:, :], in1=y1[0][:], op=ALU.add),
                       r=[RB[4], y1[1]], w=[y3[1]])
                dve.op(lambda e: e.tensor_tensor(out=y3[0][:], in0=y3[0][:], in1=y2[0][:], op=ALU.add),
                       r=[y2[1]], w=[y3[1]])
                dve.op(lambda e: e.tensor_tensor(out=y3[0][:], in0=y3[0][:], in1=gate[s][0][:], op=ALU.mult),
                       r=[gate[s][1]], w=[y3[1]])
                for g in range(2):
                    act.op(lambda e: e.activation(out=junk[0][:, g * 256:(g + 1) * 256], in_=y3[0][:, g * 256:(g + 1) * 256],
                                                  func=AF.Square, accum_out=ssn[0][:, 1 + g:2 + g]),
                           r=[y3[1]], w=[junk[1], R_ss[1 + g]])
                self.rstd(ssn[0][:, 1:3], 256, rsn[0][:, 1:3], [R_ss[1], R_ss[2]], [R_rs[1], R_rs[2]])
                for g in range(2):
                    act.op(lambda e: e.activation(out=mixed[0][:, 512 + g * 256:512 + (g + 1) * 256],
                                                  in_=y3[0][:, g * 256:(g + 1) * 256], func=AF.Copy,
                                                  scale=rsn[0][:, 1 + g:2 + g]), r=[y3[1], R_rs[1 + g]], w=[R_mixs])
                self.state_update(tokt, R_tok, xdec[0], xdec[1], Hb[0], Hb[1], c, 1, bankid=7)
                pe.mm_group([(lambda e, k=k: e.transpose(out=tb3[:, k * 128:(k + 1) * 128],
                                                         in_=mixed[0][:, k * 128:(k + 1) * 128], identity=identb))
                             for k in range(8)], r=[R_mixa, R_mixs, self.R_cbf], w=[RB[3]])
                dve.op(lambda e: e.tensor_copy(out=mT[0][:].rearrange("p k t -> p (k t)"), in_=tb3[:, :]),
                       r=[RB[3]], w=[mT[1]])
                for cg in range(2):
                    pe.mm_group([(lambda e, k=k: e.matmul(bank[cg][:, :], lhsT=mT[0][:, k, :],
                                                          rhs=wo[:, k, cg * 512:(cg + 1) * 512], start=(k == 0), stop=(k == 7)))
                                 for k in range(8)], r=[mT[1], R_wo], w=[RB[cg]])
                x1t, R_x1 = x1[s]
                for cg in range(2):
                    dve.op(lambda e: e.tensor_tensor(out=x1t[:, cg * 512:(cg + 1) * 512], in0=bank[cg][:, :],
                                                     in1=xr[s][0][:, cg * 512:(cg + 1) * 512], op=ALU.add),
                           r=[RB[cg], xr[s][1]], w=[R_x1])
                sp.dma(self.x1s[c * 128:(c + 1) * 128, :], x1t[:], R_x1, r=[R_x1], w=[self.R_x1[c]])
                act.op(lambda e: e.activation(out=junk[0][:], in_=x1t[:], func=AF.Square, accum_out=ssn[0][:, 3:4]),
                       r=[R_x1], w=[junk[1], R_ss[3]])
                self.rstd(ssn[0][:, 3:4], D, rsn[0][:, 3:4], [R_ss[3]], [R_rs[3]])
                act.op(lambda e: e.activation(out=h2[0][:], in_=x1t[:], func=AF.Copy, scale=rsn[0][:, 3:4]),
                       r=[R_x1, R_rs[3]], w=[h2[1]])
                pe.mm_group([(lambda e, k=k: e.transpose(out=tb3[:, k * 128:(k + 1) * 128],
                                                         in_=h2[0][:, k * 128:(k + 1) * 128], identity=identb))
                             for k in range(8)], r=[h2[1], self.R_cbf], w=[RB[3]])
                dve.op(lambda e: e.tensor_copy(out=h2T[s][0][:].rearrange("p k t -> p (k t)"), in_=tb3[:, :]),
                       r=[RB[3]], w=[h2T[s][1]])
                sp.dma(self.h2T[:, :, 1 + c * 128:1 + (c + 1) * 128], h2T[s][0][:], h2T[s][1], r=[h2T[s][1]],
                       w=[self.R_h2T[c]])

            for b in range(nb - 1, -1, -1):
                stage1(b)
                if b + 1 < nb:
                    stage2(b + 1)
            stage2(0)

    def phase_f(self):
        nc, S, nb = self.nc, self.S, self.nb
        pe, act, dve, pool, sp = self.pe, self.act, self.dve, self.pool, self.sp
        RB = self.R_bank
        bank = self.bank
        NJ = DFF // 128
        with ExitStack() as es:
            wu, _ = self.tile(es, "wup", [128, 8, 2 * DFF], BF16)
            wd, _ = self.tile(es, "wdown", [128, NJ, D], BF16)
            gft, R_gf = self.tile(es, "gft", [128, D], F32)
            R_wu = [Res() for _ in range(NJ // 2)]
            R_wd = [Res() for _ in range(2)]
            for r in R_wu + R_wd:
                self.alltok_res.append(r)
            for p in range(NJ // 2):
                for base in (0, DFF):
                    c0 = base + p * 256
                    sp.dma(wu[:, :, c0:c0 + 256], self.wu_s[:, :, c0:c0 + 256], R_wu[p], w=[R_wu[p]])
            for h in range(2):
                sp.dma(wd[:, h * 11:(h + 1) * 11, :], self.wd_s[:, h * 11:(h + 1) * 11, :], R_wd[h], w=[R_wd[h]])
            sp.dma(gft[:], self.gf_d, R_gf, w=[R_gf])
            gf = gft[:]
            fcw = self.c_("fcw")
            fcb = self.c_("fcb")
            NW = FT + 2
            hw = [self.tile(es, "hwF%d" % i, [128, 8, NW], BF16) for i in range(2)]
            gT = self.tile(es, "gT", [128, NJ, FT], BF16)
            ag = [self.tile(es, "agF%d" % i, [128, FT], F32) for i in range(2)]
            av = [self.tile(es, "avF%d" % i, [128, FT], F32) for i in range(2)]
            tg = [self.tile(es, "tgF%d" % i, [128, FT], F32) for i in range(2)]
            x1 = [self.tile(es, "x1F%d" % i, [128, D], F32) for i in range(2)]
            x2 = [self.tile(es, "x2F%d" % i, [128, D], F32) for i in range(2)]
            junk = self.tile(es, "junkF", [128, D], BF16)
            ssf = self.tile(es, "ssf", [128, 2], F32)
            rsf = self.tile(es, "rsf", [128, 2], F32)
            R_ssf = [Res(), Res()]
            R_rsf = [Res(), Res()]
            tiles = []
            t0 = 0
            while t0 < S:
                tiles.append((t0, min(FT, S - t0)))
                t0 += FT
            blkctr = [0]

            def load(ti):
                t0, nt = tiles[ti]
                hwt, R_hw = hw[ti % 2]
                n = nt + 2
                b0, b1 = t0 // 128, min(nb - 1, (t0 + nt) // 128)
                rd = [self.R_h2pad] + [self.R_h2T[i] for i in range(max(0, b0 - 1), min(nb, b1 + 2))]
                sp.dma(hwt[:, :, 0:n], self.h2T[:, :, t0:t0 + n], R_hw, r=rd, w=[R_hw])

            def up(ti):
                t0, nt = tiles[ti]
                hwt, R_hw = hw[ti % 2]
                n = nt + 2
                for j in range(NJ):
                    p = j % 2
                    bg, bv = 2 * p, 2 * p + 1
                    for (bi, c0) in ((bg, j * 128), (bv, DFF + j * 128)):
                        pe.mm_group([(lambda e, k=k: e.matmul(bank[bi][:, 0:n], lhsT=wu[:, k, c0:c0 + 128],
                                                              rhs=hwt[:, k, 0:n], start=(k == 0), stop=(k == 7)))
                                     for k in range(8)], r=[R_wu[j // 2], R_hw], w=[RB[bi]])
                    agt, R_ag = ag[p]
                    avt, R_av = av[p]
                    tgt, R_tg = tg[p]
                    for (bi, acc, R_a, wc, bc_) in ((bg, agt, R_ag, j * 3, j), (bv, avt, R_av, 66 + j * 3, 22 + j)):
                        act.op(lambda e: e.activation(out=acc[:, 0:nt], in_=bank[bi][:, 0:nt], func=AF.Identity,
                                                      scale=fcw[:, wc:wc + 1], bias=fcb[:, bc_:bc_ + 1]),
                               r=[RB[bi], self.R_cst], w=[R_a])
                    for t in (1, 2):
                        for (bi, acc, R_a, wc) in ((bg, agt, R_ag, j * 3), (bv, avt, R_av, 66 + j * 3)):
                            dve.op(lambda e: e.scalar_tensor_tensor(out=acc[:, 0:nt], in0=bank[bi][:, t:t + nt],
                                                                    scalar=fcw[:, wc + t:wc + t + 1], in1=acc[:, 0:nt],
                                                                    op0=ALU.mult, op1=ALU.add),
                                   r=[RB[bi], self.R_cst], w=[R_a])
                    act.op(lambda e: e.activation(out=tgt[:, 0:nt], in_=agt[:, 0:nt], func=AF.Silu), r=[R_ag], w=[R_tg])
                    dve.op(lambda e: e.tensor_tensor(out=gT[0][:, j, 0:nt], in0=tgt[:, 0:nt], in1=avt[:, 0:nt], op=ALU.mult),
                           r=[R_tg, R_av], w=[gT[1]])

            def down(ti):
                t0, nt = tiles[ti]
                o = 0
                while o < nt:
                    m = min(128, nt - o)
                    q = blkctr[0] % 2
                    blkctr[0] += 1
                    r0 = t0 + o
                    x1t, R_x1 = x1[q]
                    x2t, R_x2 = x2[q]
                    bl0, bl1 = r0 // 128, (r0 + m - 1) // 128
                    sp.dma(x1t[0:m, :], self.x1s[r0:r0 + m, :], R_x1, r=[self.R_x1[i] for i in range(bl0, bl1 + 1)], w=[R_x1])
                    for cg in range(2):
                        bi = 4 + 2 * q + cg
                        pe.mm_group([(lambda e, j=j: e.matmul(bank[bi][0:m, :], lhsT=gT[0][:, j, o:o + m],
                                                              rhs=wd[:, j, cg * 512:(cg + 1) * 512], start=(j == 0),
                                                              stop=(j == NJ - 1))) for j in range(NJ)],
                                    r=[gT[1]] + R_wd, w=[RB[bi]])
                    for cg in range(2):
                        bi = 4 + 2 * q + cg
                        dve.op(lambda e: e.tensor_tensor(out=x2t[0:m, cg * 512:(cg + 1) * 512], in0=bank[bi][0:m, :],
                                                         in1=x1t[0:m, cg * 512:(cg + 1) * 512], op=ALU.add),
                               r=[RB[bi], R_x1], w=[R_x2])
                    act.op(lambda e: e.activation(out=junk[0][0:m, :], in_=x2t[0:m, :], func=AF.Square,
                                                  accum_out=ssf[0][0:m, q:q + 1]), r=[R_x2], w=[junk[1], R_ssf[q]])
                    self.rstd(ssf[0][0:m, q:q + 1], D, rsf[0][0:m, q:q + 1], [R_ssf[q]], [R_rsf[q]])
                    dve.op(lambda e: e.scalar_tensor_tensor(out=x2t[0:m, :], in0=x2t[0:m, :], scalar=rsf[0][0:m, q:q + 1],
                                                            in1=gf[0:m, :], op0=ALU.mult, op1=ALU.mult),
                           r=[R_rsf[q], R_gf], w=[R_x2])
                    sp.dma(self.out[r0:r0 + m, :], x2t[0:m, :], R_x2, r=[R_x2], w=[self.R_out])
                    o += m

            load(0)
            for ti in range(len(tiles)):
                if ti + 1 < len(tiles):
                    load(ti + 1)
                up(ti)
                down(ti)


def host_consts(S, p):
    nb = S // 128
    cst = np.zeros((128, CST_COLS), np.float32)

    def put(name, arr):
        a, b = _c[name]
        cst[:, a:b] = np.asarray(arr, np.float32).reshape(128, b - a)

    put("g1", p["norm1_g"][0].reshape(8, 128).T)
    put("g2", p["norm2_g"][0].reshape(8, 128).T)
    put("ga", p["attn_out_g"][0].reshape(4, 128).T)
    put("gs", p["ssd_norm_g"][0].reshape(4, 128).T)
    put("scw", p["ssd_conv_w"][0].reshape(5, 8, 128).transpose(2, 1, 0))
    put("scb", p["ssd_conv_b"][0].reshape(8, 128).T)
    put("fcw", p["ffn_conv_w"][0].reshape(3, 44, 128).transpose(2, 1, 0))
    put("fcb", p["ffn_conv_b"][0].reshape(44, 128).T)
    dtb = np.zeros((128, 1), np.float32)
    alog = np.zeros((128, 1), np.float32)
    sel = np.zeros((128, 3), np.float32)
    both_b = np.concatenate([p["ssd_dt_bias_fwd"][0], p["ssd_dt_bias_bwd"][0]])
    both_a = np.concatenate([p["ssd_a_log_fwd"][0], p["ssd_a_log_bwd"][0]])
    for grp in range(3):
        dtb[grp * 32:grp * 32 + 16, 0] = both_b
        alog[grp * 32:grp * 32 + 16, 0] = both_a
        sel[grp * 32:grp * 32 + 8, 0] = 1.0
        sel[grp * 32 + 8:grp * 32 + 16, 0] = -1.0
        sel[grp * 32 + 8:grp * 32 + 16, 1] = 1.0
        sel[grp * 32:grp * 32 + 8, 2] = 1.0
    put("dtb", dtb)
    put("alog", alog)
    put("sel", sel)
    put("dskip", np.tile(p["ssd_d"][0][None, :], (128, 1)))
    put("sink", np.tile(p["attn_sink"][0][None, :], (128, 1)))
    put("identf", np.eye(128, dtype=np.float32))
    cbf = np.zeros((128, CBF_COLS), np.float32)
    cbf[:, 0:128] = np.eye(128)
    pp = np.arange(128)[:, None]
    ff = np.arange(128)[None, :]
    mle = np.where(pp <= ff, 0.0, NEGM)
    mge = np.where(pp >= ff, 0.0, NEGM)
    cbf[:, 128:640] = np.tile(mle, (1, 4))
    cbf[:, 640:1152] = np.tile(mge, (1, 4))
    ind = np.zeros((128, 16, 128), np.float32)
    for grp in range(3):
        for hd in range(16):
            ind[grp * 32 + hd, hd, :] = 1.0
    cbf[:, 1152:] = ind.reshape(128, 2048)
    cbf = cbf.astype(ml_dtypes.bfloat16)
    pos = np.arange(S, dtype=np.float32)
    inv = (np.float32(500000.0) ** (-np.arange(0, 16, 2, dtype=np.float32) / np.float32(16))).astype(np.float32)
    ang = (pos[:, None] * inv[None, :]).astype(np.float32)
    tab = np.concatenate([np.cos(ang), np.sin(ang)], axis=1).astype(np.float32)
    rope = tab.reshape(nb, 128, 16).transpose(1, 0, 2).reshape(128, nb * 16).copy()
    gfc = np.ascontiguousarray(np.tile(np.asarray(p["final_norm_g"], np.float32)[None, :], (128, 1)))
    return cst, cbf, rope, gfc


_NC_CACHE = {}


def run(inputs, S, n_cores, dbg=False, phases="A1,D,AB,C,F"):
    p = {k: np.asarray(v) for k, v in inputs.items()}
    key = (S, dbg, phases)
    if key not in _NC_CACHE:
        _NC_CACHE[key] = KB(S, dbg=dbg, phases=phases).build()
    nc = _NC_CACHE[key]
    cst, cbf, rope, gfc = host_consts(S, p)
    shared = {
        "w_in": np.ascontiguousarray(p["w_in"][0], np.float32),
        "w_out": np.ascontiguousarray(p["w_out"][0], np.float32),
        "w_up": np.ascontiguousarray(p["w_up"][0], np.float32),
        "w_down": np.ascontiguousarray(p["w_down"][0], np.float32),
        "cst": cst, "cbf": cbf, "rope": rope, "gfc": gfc,
    }
    in_maps = []
    for c in range(n_cores):
        m = dict(shared)
        m["x"] = np.ascontiguousarray(p["x"][c], np.float32)
        in_maps.append(m)
    res = run_bass_kernel_spmd(nc, in_maps, core_ids=list(range(n_cores)))
    return res


def kernel(**inputs):
    res = run(inputs, SEQ, NCORES)
    return np.stack([np.asarray(r["out"], np.float32) for r in res.results], axis=0)
```

```python
import numpy as np
import ml_dtypes
from contextlib import ExitStack
import concourse.bass as bass
import concourse.mybir as mybir
from concourse.bass_utils import run_bass_kernel_spmd

F32 = mybir.dt.float32
BF16 = mybir.dt.bfloat16
AF = mybir.ActivationFunctionType
ALU = mybir.AluOpType

D = 1024
SEQ = 8192
NCORES = 8
EPS = 1e-5
NEGM = -30000.0
IN_COLS = 2320
DFF = 2816
FT = 382

_c = {}
_o = 0
for _n, _w in [("g1", 8), ("g2", 8), ("ga", 4), ("gs", 4), ("scw", 40), ("scb", 8), ("fcw", 132), ("fcb", 44),
               ("dtb", 1), ("alog", 1), ("sel", 3), ("dskip", 8), ("sink", 8), ("identf", 128)]:
    _c[_n] = (_o, _o + _w)
    _o += _w
CST_COLS = _o
_b = {"identb": (0, 128), "mle": (128, 640), "mge": (640, 1152), "ind": (1152, 1152 + 2048)}
CBF_COLS = 1152 + 2048


class Res:
    __slots__ = ("w", "r", "sem", "cnt", "psum")

    def __init__(self, psum=False):
        self.w = None
        self.r = {}
        self.sem = None
        self.cnt = 0
        self.psum = psum


class Eng:
    def __init__(self, kb, e, sem, is_pe=False):
        self.kb = kb
        self.e = e
        self.sem = sem
        self.cnt = 0
        self.seen = {}
        self.is_pe = is_pe
        self.pend_r = []
        self.pend_w = []

    def wait(self, tok, raw=True):
        if tok is None:
            return
        sem, val = tok
        if self.seen.get(sem, 0) >= val:
            return
        if sem is self.sem and (self.is_pe or not raw):
            return
        self.e.wait_ge(sem, val)
        self.seen[sem] = val

    def deps(self, reads, writes):
        for r in reads:
            self.wait(r.w, raw=True)
            if r.psum:
                for s, v in r.r.items():
                    if s is not self.sem:
                        self.wait((s, v), raw=True)
        for w in writes:
            self.wait(w.w, raw=True)
            for s, v in w.r.items():
                self.wait((s, v), raw=True)

    @staticmethod
    def mark(tok, reads, writes):
        for r in reads:
            if r.r.get(tok[0], 0) < tok[1]:
                r.r[tok[0]] = tok[1]
        for w in writes:
            w.w = tok
            w.r = {}

    def op(self, fn, r=(), w=(), inc=True):
        self.deps(r, w)
        ins = fn(self.e)
        if not inc:
            self.pend_r += list(r)
            self.pend_w += list(w)
            return None
        self.cnt += 1
        ins.then_inc(self.sem, 1)
        tok = (self.sem, self.cnt)
        self.mark(tok, list(r) + self.pend_r, list(w) + self.pend_w)
        self.pend_r = []
        self.pend_w = []
        return tok

    def mm_group(self, fns, r=(), w=()):
        tok = None
        for i, fn in enumerate(fns):
            tok = self.op(fn, r=r, w=w, inc=(i == len(fns) - 1))
        return tok

    def dma(self, out, in_, semres, r=(), w=(), slow=False):
        self.deps(r, w)
        if semres.sem is None:
            semres.sem = self.kb.new_sem()
        ins = self.e.dma_start(out=out, in_=in_, allow_slow_non_contiguous=True) if slow else self.e.dma_start(out=out, in_=in_)
        semres.cnt += 16
        ins.then_inc(semres.sem, 16)
        tok = (semres.sem, semres.cnt)
        self.mark(tok, r, w)
        return tok


class KB:
    def __init__(self, S, dbg=False, phases="A1,D,AB,C,F"):
        self.S = S
        self.nb = S // 128
        self.dbg = dbg
        self.phases = phases.split(",")
        self.nc = bass.Bass("TRN2", target_bir_lowering=False)
        self.es = ExitStack()
        self.nsem = 0
        self.alltok_res = []

    def new_sem(self):
        self.nsem += 1
        return self.es.enter_context(self.nc.semaphore("s%d" % self.nsem))

    def tile(self, es, name, shape, dt):
        t = es.enter_context(self.nc.sbuf_tensor("t_" + name, list(shape), dt))
        r = Res()
        self.alltok_res.append(r)
        return t, r

    def dram(self, name, shape, dt, kind="Internal"):
        return self.nc.dram_tensor(name, list(shape), dt, kind=kind).ap()

    def barrier(self):
        toks = [(e.sem, e.cnt) for e in self.engs if e.cnt > 0]
        for r in self.alltok_res + self.dram_res:
            if r.sem is not None and r.cnt > 0:
                toks.append((r.sem, r.cnt))
        for e in self.engs:
            for t in toks:
                e.wait(t)

    def build(self):
        nc, S, nb = self.nc, self.S, self.nb
        es = self.es
        okind = "ExternalOutput" if self.dbg else "Internal"
        self.x = self.dram("x", [S, D], F32, "ExternalInput")
        self.w_in = self.dram("w_in", [D, IN_COLS], F32, "ExternalInput")
        self.w_out = self.dram("w_out", [D, D], F32, "ExternalInput")
        self.w_up = self.dram("w_up", [D, 2 * DFF], F32, "ExternalInput")
        self.w_down = self.dram("w_down", [DFF, D], F32, "ExternalInput")
        self.cst_d = self.dram("cst", [128, CST_COLS], F32, "ExternalInput")
        self.cbf_d = self.dram("cbf", [128, CBF_COLS], BF16, "ExternalInput")
        self.rope_d = self.dram("rope", [128, nb * 16], F32, "ExternalInput")
        self.gf_d = self.dram("gfc", [128, D], F32, "ExternalInput")
        self.out = self.dram("out", [S, D], F32, "ExternalOutput")
        self.h1T = self.dram("h1T", [128, 8, S + 4], BF16, okind)
        self.xbcT = self.dram("xbcT", [nb, 128, 1024], BF16, okind)
        self.prevf = self.dram("prevf", [nb, 128, 512], BF16, okind)
        self.x1s = self.dram("x1s", [S, D], F32, okind)
        self.h2T = self.dram("h2T", [128, 8, S + 2], BF16, okind)
        self.wq_s = self.dram("wq_s", [128, 8, 1280], BF16)
        self.wo_s = self.dram("wo_s", [128, 8, 1024], BF16)
        self.wu_s = self.dram("wu_s", [128, 8, 2 * DFF], BF16)
        self.wd_s = self.dram("wd_s", [128, DFF // 128, D], BF16)
        self.R_wscr = Res()
        if self.dbg:
            self.dbgD = self.dram("dbgD", [128, nb * 32 + nb * 16], F32, "ExternalOutput")
        self.R_h1T = [Res() for _ in range(nb)]
        self.R_h1pad = Res()
        self.R_xbcT = [Res() for _ in range(nb)]
        self.R_prevf = [Res() for _ in range(nb)]
        self.R_x1 = [Res() for _ in range(nb)]
        self.R_h2T = [Res() for _ in range(nb)]
        self.R_h2pad = Res()
        self.R_out = Res()
        self.dram_res = self.R_h1T + self.R_xbcT + self.R_prevf + self.R_x1 + self.R_h2T + [self.R_h1pad, self.R_h2pad, self.R_out, self.R_wscr]

        self.pe = Eng(self, nc.tensor, self.new_sem(), is_pe=True)
        self.act = Eng(self, nc.scalar, self.new_sem())
        self.dve = Eng(self, nc.vector, self.new_sem())
        self.pool = Eng(self, nc.gpsimd, self.new_sem())
        self.sp = Eng(self, nc.sync, self.new_sem())
        self.engs = [self.pe, self.act, self.dve, self.pool, self.sp]

        self.bank = []
        self.R_bank = []
        for i in range(8):
            t = es.enter_context(nc.psum_tensor("bank%d" % i, [128, 512], F32))
            self.bank.append(t)
            self.R_bank.append(Res(psum=True))

        self.cst, self.R_cst = self.tile(es, "cst", [128, CST_COLS], F32)
        self.cbf, self.R_cbf = self.tile(es, "cbf", [128, CBF_COLS], BF16)
        self.mh, self.R_mh = self.tile(es, "mh", [128, 8], F32)
        self.zero, self.R_zero = self.tile(es, "zero", [128, 16], BF16)
        self.sp.dma(self.cst[:], self.cst_d, self.R_cst, w=[self.R_cst])
        self.sp.dma(self.cbf[:], self.cbf_d, self.R_cbf, w=[self.R_cbf])
        self.epsc, self.R_epsc = self.tile(es, "epsc", [128, 2], F32)
        self.pool.op(lambda e: e.memset(self.epsc[:], EPS), w=[self.R_epsc])
        self.pool.op(lambda e: e.memset(self.mh[:], -0.5), w=[self.R_mh])
        self.pool.op(lambda e: e.memset(self.zero[:], 0.0), w=[self.R_zero])

        with ExitStack() as es_ssd:
            self.PHI3, self.R_PHI3 = self.tile(es_ssd, "PHI3", [128, S], BF16)
            self.PSI3, self.R_PSI3 = self.tile(es_ssd, "PSI3", [128, S], BF16)
            self.TM, self.R_TM = self.tile(es_ssd, "TM", [128, nb, 32], F32)
            self.DECB, self.R_DECB = self.tile(es_ssd, "DECB", [128, nb, 16], F32)
            with ExitStack() as es_x:
                self.X = []
                self.R_X = []
                for i in range(4):
                    t, r = self.tile(es_x, "X%d" % i, [128, S], F32)
                    self.X.append(t)
                    self.R_X.append(r)
                if "A1" in self.phases:
                    self.phase_a1()
                    self.barrier()
                if "D" in self.phases:
                    self.phase_d()
                    self.barrier()
            if "AB" in self.phases:
                self.phase_ab()
                self.barrier()
            if "C" in self.phases:
                self.phase_c()
                self.barrier()
        if "F" in self.phases:
            self.phase_f()
            self.barrier()
        self.barrier()
        self.es.close()
        return nc

    def c_(self, name, lo=None, hi=None):
        a, b = _c[name]
        if lo is not None:
            a, b = a + lo, a + hi
        return self.cst[:, a:b]

    def rstd(self, ss_ap, n, out_ap, rr, rw):
        k = out_ap.shape[1]
        self.pool.op(lambda e: e.tensor_scalar(out=out_ap, in0=ss_ap, scalar1=1.0 / n, scalar2=EPS,
                                               op0=ALU.mult, op1=ALU.add), r=rr, w=rw)
        self.pool.op(lambda e: e.tensor_tensor(out=out_ap, in0=out_ap, in1=self.mh[0:out_ap.shape[0], 0:k], op=ALU.pow),
                     r=[self.R_mh], w=rw)

    def prep_weight(self, es, src, ncols, emit, nk=8, tag="w"):
        stg = []
        for i in range(2):
            stg.append(self.tile(es, "stg_%s%d" % (tag, i), [128, ncols], F32))
        for k in range(nk):
            t, r = stg[k % 2]
            self.sp.dma(t[:], src[k * 128:(k + 1) * 128, :], r, w=[r])
            emit(k, t, r)

    def phase_a1(self):
        nc, S, nb = self.nc, self.S, self.nb
        pe, act, dve, pool, sp = self.pe, self.act, self.dve, self.pool, self.sp
        with ExitStack() as es:
            xa = [self.tile(es, "xa%d" % i, [128, D], F32) for i in range(3)]
            junk, R_junk = self.tile(es, "junkA", [128, D], BF16)
            hb = [self.tile(es, "hb%d" % i, [128, D], BF16) for i in range(2)]
            h1 = [self.tile(es, "h1_%d" % i, [128, 8, 128], BF16) for i in range(2)]
            ss, _ = self.tile(es, "ssA", [128, 4], F32)
            rs, _ = self.tile(es, "rsA", [128, 4], F32)
            R_ssl = [Res() for _ in range(4)]
            R_rsl = [Res() for _ in range(4)]
            wdt, R_wdt = self.tile(es, "wdt3", [128, 8, 96], BF16)
            wst, R_wst = self.tile(es, "wdtst", [128, 8, 16], F32)
            sp.dma(wst[:], self.w_in[:, 2304:2320].rearrange("(k p) c -> p k c", p=128), R_wst, w=[R_wst])
            dve.op(lambda e: e.memset(wdt[:], 0.0), w=[R_wdt])
            g1b = self.c_("g1").unsqueeze(2).to_broadcast([128, 8, 16])
            for grp in range(3):
                dve.op(lambda e: e.tensor_tensor(out=wdt[:, :, grp * 32:grp * 32 + 16], in0=wst[:], in1=g1b,
                                                 op=ALU.mult), r=[R_wst, self.R_cst], w=[R_wdt])
            sp.dma(self.h1T[:, :, 0:2], self.zero[:, 0:16].rearrange("p (k c) -> p k c", k=8), self.R_zero,
                   r=[self.R_zero], w=[self.R_h1pad])
            sp.dma(self.h1T[:, :, S + 2:S + 4], self.zero[:, 0:16].rearrange("p (k c) -> p k c", k=8), self.R_zero,
                   r=[self.R_zero], w=[self.R_h1pad])
            identb = self.cbf[:, 0:128]
            tbank = (3, 5)
            dbank = (4, 6)

            def s0(b):
                xt, R_x = xa[b % 3]
                sp.dma(xt[:], self.x[b * 128:(b + 1) * 128, :], R_x, w=[R_x])

            def s1(b):
                xt, R_x = xa[b % 3]
                q = b % 4
                act.op(lambda e: e.activation(out=junk[:], in_=xt[:], func=AF.Square, accum_out=ss[:, q:q + 1]),
                       r=[R_x], w=[R_junk, R_ssl[q]])
                act.op(lambda e: e.activation(out=rs[:, q:q + 1], in_=ss[:, q:q + 1], func=AF.Ln, scale=1.0 / D,
                                              bias=self.epsc[:, 0:1]), r=[R_ssl[q], self.R_epsc], w=[R_rsl[q]])
                act.op(lambda e: e.activation(out=rs[:, q:q + 1], in_=rs[:, q:q + 1], func=AF.Exp, scale=-0.5),
                       r=[], w=[R_rsl[q]])

            def s2(b):
                xt, R_x = xa[b % 3]
                q = b % 4
                hbt, R_hb = hb[b % 2]
                dve.op(lambda e: e.tensor_scalar(out=hbt[:], in0=xt[:], scalar1=rs[:, q:q + 1], scalar2=None,
                                                 op0=ALU.mult), r=[R_x, R_rsl[q]], w=[R_hb])

            def s3(b):
                hbt, R_hb = hb[b % 2]
                bi = tbank[b % 2]
                tb = self.bank[bi][:].bitcast(BF16)
                pe.mm_group([(lambda e, k=k: e.transpose(out=tb[:, k * 128:(k + 1) * 128],
                                                         in_=hbt[:, k * 128:(k + 1) * 128], identity=identb))
                             for k in range(8)], r=[R_hb, self.R_cbf], w=[self.R_bank[bi]])

            def s4(b):
                h1t, R_h1 = h1[b % 2]
                bi = tbank[b % 2]
                tb = self.bank[bi][:].bitcast(BF16)
                dve.op(lambda e: e.tensor_copy(out=h1t[:].rearrange("p k t -> p (k t)"), in_=tb[:, :]),
                       r=[self.R_bank[bi]], w=[R_h1])
                sp.dma(self.h1T[:, :, 2 + b * 128:2 + (b + 1) * 128], h1t[:], R_h1, r=[R_h1], w=[self.R_h1T[b]])

            def s5(b):
                h1t, R_h1 = h1[b % 2]
                bi = dbank[b % 2]
                pe.mm_group([(lambda e, k=k: e.matmul(self.bank[bi][0:96, 0:128], lhsT=wdt[:, k, :], rhs=h1t[:, k, :],
                                                      start=(k == 0), stop=(k == 7))) for k in range(8)],
                            r=[R_wdt, R_h1], w=[self.R_bank[bi]])

            def s6(b):
                bi = dbank[b % 2]
                act.op(lambda e: e.activation(out=self.X[0][0:96, b * 128:(b + 1) * 128], in_=self.bank[bi][0:96, 0:128],
                                              func=AF.Exp, bias=self.c_("dtb")[0:96, :]),
                       r=[self.R_bank[bi], self.R_cst], w=[self.R_X[0]])

            stages = [s0, s1, s2, s3, s4, s5, s6]
            for t in range(nb + len(stages) - 1):
                for si in range(len(stages) - 1, -1, -1):
                    b = t - si
                    if 0 <= b < nb:
                        stages[si](b)

    def phase_d(self):
        nc, S, nb = self.nc, self.S, self.nb
        pe, act, dve, pool, sp = self.pe, self.act, self.dve, self.pool, self.sp
        X, RX = self.X, self.R_X
        P = slice(0, 96)
        sgn = self.c_("sel", 0, 1)[P]
        bsel = self.c_("sel", 1, 2)[P]
        fsel = self.c_("sel", 2, 3)[P]
        with ExitStack() as es:
            negA, R_negA = self.tile(es, "negA", [128, 1], F32)
            CSE, R_CSE = self.tile(es, "CSE", [128, nb], F32)
            c1, R_c1 = self.tile(es, "c1", [128, nb], F32)
            c2, R_c2 = self.tile(es, "c2", [128, nb], F32)
            DEC, R_DEC = self.tile(es, "DEC", [128, nb], F32)
            Dg, R_Dg = self.tile(es, "Dg", [16, nb, 16], F32)
            ones, R_ones = self.tile(es, "onesD", [16, 128], F32)
            act.op(lambda e: e.activation(out=X[0][P], in_=X[0][P], func=AF.Ln, bias=1.0), r=[], w=[RX[0]])
            act.op(lambda e: e.activation(out=X[1][P], in_=X[0][P], func=AF.Ln), r=[RX[0]], w=[RX[1]])
            act.op(lambda e: e.activation(out=negA[P], in_=self.c_("alog")[P], func=AF.Exp), r=[self.R_cst], w=[R_negA])
            dve.op(lambda e: e.tensor_scalar(out=negA[P], in0=negA[P], scalar1=-1.0, scalar2=None, op0=ALU.mult),
                   r=[], w=[R_negA])
            dve.op(lambda e: e.tensor_scalar(out=X[2][P], in0=X[0][P], scalar1=negA[P], scalar2=None, op0=ALU.mult),
                   r=[RX[0], R_negA], w=[RX[2]])
            dve.op(lambda e: e.memset(self.PHI3[P], 1.0), w=[self.R_PHI3])
            dve.op(lambda e: e.memset(self.PHI3[P].rearrange("p (c j) -> p c j", j=128)[:, :, 0:1], 0.0),
                   w=[self.R_PHI3])
            dve.op(lambda e: e.tensor_tensor_scan(out=X[3][P], data0=self.PHI3[P], data1=X[2][P], initial=0.0,
                                                  op0=ALU.mult, op1=ALU.add), r=[self.R_PHI3, RX[2]], w=[RX[3]])
            dve.op(lambda e: e.tensor_copy(out=CSE[P], in_=X[3][P].rearrange("p (c j) -> p c j", j=128)[:, :, 127]),
                   r=[RX[3]], w=[R_CSE])
            dve.op(lambda e: e.tensor_scalar(out=X[2][P], in0=X[2][P], scalar1=bsel, scalar2=None, op0=ALU.mult),
                   r=[self.R_cst], w=[RX[2]])
            dve.op(lambda e: e.scalar_tensor_tensor(out=X[3][P], in0=X[3][P], scalar=sgn, in1=X[2][P],
                                                    op0=ALU.mult, op1=ALU.add), r=[RX[2], self.R_cst], w=[RX[3]])
            dve.op(lambda e: e.tensor_tensor(out=X[1][P], in0=X[1][P], in1=X[3][P], op=ALU.subtract),
                   r=[RX[3]], w=[RX[1]])
            dve.op(lambda e: e.tensor_scalar(out=c1[P], in0=CSE[P], scalar1=bsel, scalar2=None, op0=ALU.mult),
                   r=[R_CSE], w=[R_c1])
            dve.op(lambda e: e.tensor_scalar(out=c2[P], in0=CSE[P], scalar1=fsel, scalar2=None, op0=ALU.mult),
                   r=[R_CSE], w=[R_c2])
            v3 = lambda t: t[P].rearrange("p (c j) -> p c j", j=128)
            bc = lambda t: t[P].unsqueeze(2).to_broadcast([96, nb, 128])
            dve.op(lambda e: e.tensor_tensor(out=v3(X[2]), in0=v3(X[3]), in1=bc(c1), op=ALU.add),
                   r=[RX[3], R_c1], w=[RX[2]])
            act.op(lambda e: e.activation(out=X[2][P], in_=X[2][P], func=AF.Exp), r=[], w=[RX[2]])
            dve.op(lambda e: e.tensor_tensor(out=v3(X[0]), in0=v3(X[1]), in1=bc(c2), op=ALU.add),
                   r=[RX[1], R_c2], w=[RX[0]])
            act.op(lambda e: e.activation(out=X[0][P], in_=X[0][P], func=AF.Exp), r=[], w=[RX[0]])
            act.op(lambda e: e.activation(out=DEC[P], in_=CSE[P], func=AF.Exp), r=[R_CSE], w=[R_DEC])
            identf = self.c_("identf")
            for g0 in range(0, nb, 16):
                gn = min(16, nb - g0)
                bk = self.bank[0][:].rearrange("p (b c) -> p b c", c=32)
                for bi in range(gn):
                    b = g0 + bi
                    pe.op(lambda e: e.transpose(out=bk[:, bi, 0:16], in_=X[2][0:16, b * 128:(b + 1) * 128],
                                                identity=identf[0:16, 0:16]), r=[RX[2], self.R_cst], w=[self.R_bank[0]])
                    pe.op(lambda e: e.transpose(out=bk[:, bi, 16:32], in_=X[0][0:16, b * 128:(b + 1) * 128],
                                                identity=identf[0:16, 0:16]), r=[RX[0], self.R_cst], w=[self.R_bank[0]])
                dve.op(lambda e: e.tensor_copy(out=self.TM[:, g0:g0 + gn, :], in_=bk[:, 0:gn, :]),
                       r=[self.R_bank[0]], w=[self.R_TM])
            dve.op(lambda e: e.memset(ones[:], 1.0), w=[R_ones])
            dve.op(lambda e: e.tensor_tensor(out=Dg[:], in0=DEC[0:16].unsqueeze(2).to_broadcast([16, nb, 16]),
                                             in1=identf[0:16, 0:16].unsqueeze(1).to_broadcast([16, nb, 16]),
                                             op=ALU.mult), r=[R_DEC, self.R_cst], w=[R_Dg])
            for c0 in range(0, nb, 32):
                cn = min(32, nb - c0)
                pe.op(lambda e: e.matmul(self.bank[1][:, 0:cn * 16], lhsT=ones[:],
                                         rhs=Dg[:, c0:c0 + cn, :].rearrange("p b h -> p (b h)"), start=True, stop=True),
                      r=[R_ones, R_Dg], w=[self.R_bank[1]])
                dve.op(lambda e: e.tensor_copy(out=self.DECB[:, c0:c0 + cn, :].rearrange("p b h -> p (b h)"),
                                               in_=self.bank[1][:, 0:cn * 16]), r=[self.R_bank[1]], w=[self.R_DECB])
            if self.dbg:
                sp.dma(self.dbgD[:, 0:nb * 32], self.TM[:].rearrange("p b c -> p (b c)"), self.R_TM, r=[self.R_TM])
                sp.dma(self.dbgD[:, nb * 32:nb * 48], self.DECB[:].rearrange("p b c -> p (b c)"), self.R_DECB,
                       r=[self.R_DECB])
            for src, R_src, scr, R_scr, dst, R_dst in ((X[3], RX[3], X[2], RX[2], self.PHI3, self.R_PHI3),
                                                       (X[1], RX[1], X[0], RX[0], self.PSI3, self.R_PSI3)):
                dve.op(lambda e: e.tensor_copy(out=dst[0:96], in_=src[0:96]), r=[R_src], w=[R_dst])
                for (a, b) in ((32, 64), (64, 96)):
                    dve.op(lambda e: e.tensor_tensor(out=scr[a:b], in0=src[a:b], in1=dst[a:b], op=ALU.subtract),
                           r=[R_src, R_dst], w=[R_scr])
                for (a, b) in ((32, 64), (64, 96)):
                    dve.op(lambda e: e.tensor_copy(out=dst[a:b], in_=scr[a:b]), r=[R_scr], w=[R_dst])
                dve.op(lambda e: e.tensor_tensor(out=scr[64:96], in0=scr[64:96], in1=dst[64:96], op=ALU.subtract),
                       r=[R_dst], w=[R_scr])
                dve.op(lambda e: e.tensor_copy(out=dst[64:96], in_=scr[64:96]), r=[R_scr], w=[R_dst])

    def tok_transposes(self, xbt, R_xb, tokt, R_tok, bankid=3):
        tb = self.bank[bankid][:].bitcast(BF16)
        identb = self.cbf[:, 0:128]
        for j in range(6):
            self.pe.op(lambda e: e.transpose(out=tb[:, j * 128:(j + 1) * 128], in_=xbt[:, j, :], identity=identb),
                       r=[R_xb, self.R_cbf], w=[self.R_bank[bankid]])
        self.dve.op(lambda e: e.tensor_copy(out=tokt[:], in_=tb[:, 0:768]), r=[self.R_bank[bankid]], w=[R_tok])

    def state_update(self, tokt, R_tok, xdec, R_xdec, H, R_H, b, d, bankid):
        dve, pe, pool = self.dve, self.pe, self.pool
        sd = self.TM[:, b, 16 + 8 * d:24 + 8 * d].unsqueeze(2).to_broadcast([128, 8, 64])
        dve.op(lambda e: e.tensor_tensor(out=xdec[:].rearrange("p (h q) -> p h q", q=64),
                                         in0=tokt[:, 0:512].rearrange("p (h q) -> p h q", q=64), in1=sd, op=ALU.mult),
               r=[R_tok, self.R_TM], w=[R_xdec])
        for g in range(2):
            pe.op(lambda e: e.matmul(self.bank[bankid][:, g * 256:(g + 1) * 256],
                                     lhsT=tokt[:, 512 + g * 128:512 + (g + 1) * 128],
                                     rhs=xdec[:, g * 256:(g + 1) * 256], start=(g == 0), stop=(g == 1)),
                  r=[R_tok, R_xdec], w=[self.R_bank[bankid]])
        dec = self.DECB[:, b, 8 * d:8 * d + 8].unsqueeze(2).to_broadcast([128, 8, 64])
        dve.op(lambda e: e.tensor_tensor(out=H[:].rearrange("p (h q) -> p h q", q=64),
                                         in0=H[:].rearrange("p (h q) -> p h q", q=64), in1=dec, op=ALU.mult),
               r=[self.R_DECB], w=[R_H])
        dve.op(lambda e: e.tensor_tensor(out=H[:], in0=H[:], in1=self.bank[bankid][:, :], op=ALU.add),
               r=[self.R_bank[bankid]], w=[R_H])

    def conv_jobs(self):
        jobs = []
        for k in range(8):
            rows = slice(k * 128, (k + 1) * 128)
            jobs.append((self.w_in[rows, 0:512], self.wq_s[:, k, 0:512], "qperm", ("g1", k), 512))
            jobs.append((self.w_in[rows, 512:768], self.wq_s[:, k, 512:768], "scale", ("g1", k), 256))
            jobs.append((self.w_in[rows, 768:1280], self.wq_s[:, k, 768:1280], "scale", ("g1h", k), 512))
            jobs.append((self.w_out[rows, :], self.wo_s[:, k, :], "scale", ("ga", k) if k < 4 else ("gs", k - 4), 1024))
        for k in range(8):
            rows = slice(k * 128, (k + 1) * 128)
            for c0 in range(0, 2 * DFF, 1408):
                jobs.append((self.w_up[rows, c0:c0 + 1408], self.wu_s[:, k, c0:c0 + 1408], "scale", ("g2", k), 1408))
        for k in range(DFF // 128):
            jobs.append((self.w_down[k * 128:(k + 1) * 128, :], self.wd_s[:, k, :], "copy", None, 1024))
        return jobs

    def phase_ab(self):
        nc, S, nb = self.nc, self.S, self.nb
        pe, act, dve, pool, sp = self.pe, self.act, self.dve, self.pool, self.sp
        RB, bank = self.R_bank, self.bank
        tiles = [(b0, min(2, nb - b0)) for b0 in range(0, nb, 2)]
        nt = len(tiles)
        with ExitStack() as es:
            wx, R_wx = self.tile(es, "wxbc", [128, 8, 1024], BF16)
            g1h, R_g1h = self.tile(es, "g1h", [128, 8], F32)
            dve.op(lambda e: e.tensor_scalar(out=g1h[:], in0=self.c_("g1"), scalar1=0.5, scalar2=None, op0=ALU.mult),
                   r=[self.R_cst], w=[R_g1h])
            wst = [self.tile(es, "wxst%d" % i, [128, 1024], F32) for i in range(2)]
            for k in range(8):
                t, r = wst[k % 2]
                sp.dma(t[:], self.w_in[k * 128:(k + 1) * 128, 1280:2304], r, w=[r])
                if k % 2 == 0:
                    dve.op(lambda e: e.tensor_scalar(out=wx[:, k, :], in0=t[:], scalar1=self.c_("g1", k, k + 1),
                                                     scalar2=None, op0=ALU.mult), r=[r, self.R_cst], w=[R_wx])
                else:
                    act.op(lambda e: e.activation(out=wx[:, k, :], in_=t[:], func=AF.Copy,
                                                  scale=self.c_("g1", k, k + 1)), r=[r, self.R_cst], w=[R_wx])
            scw = self.c_("scw")
            scb = self.c_("scb")
            hw = [self.tile(es, "hw%d" % i, [128, 8, 260], BF16) for i in range(3)]
            acc = [self.tile(es, "accAB%d" % i, [128, 256], F32) for i in range(4)]
            xb2 = [self.tile(es, "xb2_%d" % i, [128, 8, 256], BF16) for i in range(2)]
            tok = [self.tile(es, "tokAB%d" % i, [128, 768], BF16) for i in range(2)]
            xdec = [self.tile(es, "xdecAB%d" % i, [128, 512], BF16) for i in range(2)]
            H, R_H = self.tile(es, "HfAB", [128, 512], F32)
            pv = [self.tile(es, "pvAB%d" % i, [128, 512], BF16) for i in range(2)]
            stg = [self.tile(es, "cvs%d" % i, [128, 1408], F32) for i in range(3)]
            cvt = [self.tile(es, "cvo%d" % i, [128, 1408], BF16) for i in range(2)]
            dve.op(lambda e: e.memset(H[:], 0.0), w=[R_H])
            identb = self.cbf[:, 0:128]
            xpb = [0, 1, 2, 3]
            trb = [4, 5]
            stb = [6, 7]
            jobs = self.conv_jobs()
            gtab = {"g1": self.c_("g1"), "g2": self.c_("g2"), "ga": self.c_("ga"), "gs": self.c_("gs"), "g1h": g1h[:]}
            gres = {"g1": self.R_cst, "g2": self.R_cst, "ga": self.R_cst, "gs": self.R_cst, "g1h": R_g1h}
            jstate = {"loaded": 0, "done": 0}

            def job_load(i):
                if i >= len(jobs):
                    return
                src, dst, kind, g, n = jobs[i]
                t, r = stg[i % 3]
                act.dma(t[:, 0:n], src, r, w=[r])

            def job_run(i):
                if i >= len(jobs):
                    return
                src, dst, kind, g, n = jobs[i]
                t, r = stg[i % 3]
                o, ro = cvt[i % 2]
                if kind == "copy":
                    act.op(lambda e: e.activation(out=o[:, 0:n], in_=t[:, 0:n], func=AF.Copy), r=[r], w=[ro])
                else:
                    gcol = gtab[g[0]][:, g[1]:g[1] + 1]
                    if kind == "qperm":
                        for hh in range(2):
                            act.op(lambda e: e.activation(
                                out=o[:, 0:512].rearrange("p (i hh d) -> p hh i d", i=4, hh=2)[:, hh],
                                in_=t[:, hh * 256:(hh + 1) * 256].rearrange("p (i d) -> p i d", i=4),
                                func=AF.Copy, scale=gcol), r=[r, gres[g[0]]], w=[ro])
                    else:
                        act.op(lambda e: e.activation(out=o[:, 0:n], in_=t[:, 0:n], func=AF.Copy, scale=gcol),
                               r=[r, gres[g[0]]], w=[ro])
                act.dma(dst, o[:, 0:n], ro, r=[ro])

            def do_jobs(cnt):
                for _ in range(cnt):
                    i = jstate["done"]
                    if i >= len(jobs):
                        return
                    while jstate["loaded"] < min(len(jobs), i + 3):
                        job_load(jstate["loaded"])
                        jstate["loaded"] += 1
                    job_run(i)
                    jstate["done"] += 1

            per_iter = -(-len(jobs) // nt)
            cc = {"n": 0}

            def load(T):
                b0, n = tiles[T]
                W = n * 128
                hwt, R_hw = hw[T % 3]
                rd = [self.R_h1pad] + [self.R_h1T[i] for i in range(b0 - 1, b0 + n + 1) if 0 <= i < nb]
                sp.dma(hwt[:, :, 0:W + 4], self.h1T[:, :, b0 * 128:b0 * 128 + W + 4], R_hw, r=rd, w=[R_hw])

            def chunk_pair(T, jp):
                b0, n = tiles[T]
                W = n * 128
                hwt, R_hw = hw[T % 3]
                xbt, R_xb = xb2[T % 2]
                info = []
                for j in (2 * jp, 2 * jp + 1):
                    c = cc["n"]
                    cc["n"] += 1
                    bi = xpb[c % 4]
                    at, R_a = acc[c % 4]
                    pe.mm_group([(lambda e, k=k: e.matmul(bank[bi][:, 0:W + 4], lhsT=wx[:, k, j * 128:(j + 1) * 128],
                                                          rhs=hwt[:, k, 0:W + 4], start=(k == 0), stop=(k == 7)))
                                 for k in range(8)], r=[R_wx, R_hw], w=[RB[bi]])
                    act.op(lambda e: e.activation(out=at[:, 0:W], in_=bank[bi][:, 0:W], func=AF.Identity,
                                                  scale=scw[:, j * 5:j * 5 + 1], bias=scb[:, j:j + 1]),
                           r=[RB[bi], self.R_cst], w=[R_a])
                    info.append((j, bi, at, R_a))
                for t in range(1, 5):
                    for (j, bi, at, R_a) in info:
                        dve.op(lambda e: e.scalar_tensor_tensor(out=at[:, 0:W], in0=bank[bi][:, t:t + W],
                                                                scalar=scw[:, j * 5 + t:j * 5 + t + 1], in1=at[:, 0:W],
                                                                op0=ALU.mult, op1=ALU.add),
                               r=[RB[bi], self.R_cst], w=[R_a])
                for (j, bi, at, R_a) in info:
                    act.op(lambda e: e.activation(out=xbt[:, j, 0:W], in_=at[:, 0:W], func=AF.Silu), r=[R_a], w=[R_xb])

            def store(T):
                b0, n = tiles[T]
                xbt, R_xb = xb2[T % 2]
                for bi_ in range(n):
                    b = b0 + bi_
                    sp.dma(self.xbcT[b].rearrange("p (k t) -> p k t", k=8), xbt[:, :, bi_ * 128:(bi_ + 1) * 128], R_xb,
                           r=[R_xb], w=[self.R_xbcT[b]])

            def tr(T):
                b0, n = tiles[T]
                xbt, R_xb = xb2[T % 2]
                for bi_ in range(n):
                    b = b0 + bi_
                    tbk = trb[bi_]
                    tb = bank[tbk][:].bitcast(BF16)
                    pe.mm_group([(lambda e, j=j: e.transpose(out=tb[:, j * 128:(j + 1) * 128],
                                                             in_=xbt[:, j, bi_ * 128:(bi_ + 1) * 128], identity=identb))
                                 for j in range(6)], r=[R_xb, self.R_cbf], w=[RB[tbk]])

            def evac(T):
                b0, n = tiles[T]
                for bi_ in range(n):
                    b = b0 + bi_
                    tbk = trb[bi_]
                    tb = bank[tbk][:].bitcast(BF16)
                    tokt, R_tok = tok[b % 2]
                    xd, R_xd = xdec[b % 2]
                    dve.op(lambda e: e.tensor_copy(out=tokt[:], in_=tb[:, 0:768]), r=[RB[tbk]], w=[R_tok])
                    sd = self.TM[:, b, 16:24].unsqueeze(2).to_broadcast([128, 8, 64])
                    dve.op(lambda e: e.tensor_tensor(out=xd[:].rearrange("p (h q) -> p h q", q=64),
                                                     in0=tokt[:, 0:512].rearrange("p (h q) -> p h q", q=64), in1=sd,
                                                     op=ALU.mult), r=[R_tok, self.R_TM], w=[R_xd])

            def states(T):
                b0, n = tiles[T]
                for bi_ in range(n):
                    b = b0 + bi_
                    tokt, R_tok = tok[b % 2]
                    xd, R_xd = xdec[b % 2]
                    sb = stb[bi_]
                    pe.mm_group([(lambda e, g=g: e.matmul(bank[sb][:, g * 256:(g + 1) * 256],
                                                          lhsT=tokt[:, 512 + g * 128:512 + (g + 1) * 128],
                                                          rhs=xd[:, g * 256:(g + 1) * 256], start=(g == 0), stop=(g == 1)))
                                 for g in range(2)], r=[R_tok, R_xd], w=[RB[sb]])

            def scan(T, bi_):
                b0, n = tiles[T]
                if bi_ >= n:
                    return
                b = b0 + bi_
                sb = stb[bi_]
                pvt, R_pv = pv[b % 2]
                act.op(lambda e: e.activation(out=pvt[:], in_=H[:], func=AF.Copy), r=[R_H], w=[R_pv])
                sp.dma(self.prevf[b], pvt[:], R_pv, r=[R_pv], w=[self.R_prevf[b]])
                dec = self.DECB[:, b, 0:8].unsqueeze(2).to_broadcast([128, 8, 64])
                dve.op(lambda e: e.tensor_tensor(out=H[:].rearrange("p (h q) -> p h q", q=64),
                                                 in0=H[:].rearrange("p (h q) -> p h q", q=64), in1=dec, op=ALU.mult),
                       r=[self.R_DECB], w=[R_H])
                dve.op(lambda e: e.tensor_tensor(out=H[:], in0=H[:], in1=bank[sb][:, :], op=ALU.add),
                       r=[RB[sb]], w=[R_H])

            load(0)
            if nt > 1:
                load(1)
            for T in range(nt + 1):
                P = T - 1
                if T + 2 < nt:
                    load(T + 2)
                if P >= 0:
                    tr(P)
                if T < nt:
                    chunk_pair(T, 0)
                if P >= 0:
                    evac(P)
                    states(P)
                if T < nt:
                    chunk_pair(T, 1)
                if P >= 0:
                    scan(P, 0)
                if T < nt:
                    chunk_pair(T, 2)
                do_jobs(per_iter)
                if T < nt:
                    chunk_pair(T, 3)
                if P >= 0:
                    scan(P, 1)
                if T < nt:
                    store(T)
            do_jobs(len(jobs))

    def phase_c(self):
        nc, S, nb = self.nc, self.S, self.nb
        pe, act, dve, pool, sp = self.pe, self.act, self.dve, self.pool, self.sp
        RB = self.R_bank
        bank = self.bank
        identb = self.cbf[:, 0:128]
        mle = self.cbf[:, _b["mle"][0]:_b["mle"][1]]
        mge = self.cbf[:, _b["mge"][0]:_b["mge"][1]]
        ind = self.cbf[:, _b["ind"][0]:_b["ind"][1]].rearrange("p (h t) -> p h t", t=128)
        bkc = [0]
        held = set()

        def ab(hold=False):
            for _ in range(8):
                i = bkc[0]
                bkc[0] = (i + 1) % 8
                if i not in held:
                    if hold:
                        held.add(i)
                    return i
            raise RuntimeError("all PSUM banks held")

        def rel(*bs):
            for i in bs:
                held.discard(i)

        with ExitStack() as es:
            wq, R_wq = self.tile(es, "wqkvz", [128, 8, 1280], BF16)
            wo, R_wo = self.tile(es, "wout", [128, 8, 1024], BF16)
            rope, R_rope = self.tile(es, "rope", [128, nb, 16], F32)
            esink, R_esink = self.tile(es, "esink", [128, 8], F32)
            Dm, R_Dm = self.tile(es, "Dm", [128, 8, 128], BF16)
            sp.dma(rope[:].rearrange("p b c -> p (b c)"), self.rope_d, R_rope, w=[R_rope])
            act.op(lambda e: e.activation(out=esink[:], in_=self.c_("sink"), func=AF.Exp), r=[self.R_cst], w=[R_esink])
            sp.dma(wq[:], self.wq_s, R_wq, w=[R_wq])
            sp.dma(wo[:], self.wo_s, R_wo, w=[R_wo])
            for h in range(8):
                dve.op(lambda e: e.tensor_scalar(out=Dm[:, h, :], in0=identb, scalar1=self.c_("dskip", h, h + 1),
                                                 scalar2=None, op0=ALU.mult), r=[self.R_cbf, self.R_cst], w=[R_Dm])
            T = lambda name, shape, dt, n: [self.tile(es, "%s%d" % (name, i), shape, dt) for i in range(n)]
            hc = T("hc", [128, 8, 128], BF16, 2)
            xb = T("xbC", [128, 8, 128], BF16, 3)
            pf = T("pfC", [128, 512], BF16, 3)
            xr = T("xr", [128, D], F32, 2)
            qk = T("qktm", [128, 640], BF16, 2)
            rtq = T("rtq", [128, 8, 8], F32, 4)
            rtk = T("rtk", [128, 2, 8], F32, 4)
            qT = T("qT", [128, 4, 128], BF16, 3)
            KP = [T("KP%d_" % i, [128, 128], BF16, 2) for i in range(4)]
            V1 = T("V1_", [128, 2, 65], BF16, 4)
            gate = T("gate", [128, 512], F32, 2)
            thz = T("thz", [128, 512], F32, 2)
            tok = T("tokC", [128, 768], BF16, 2)
            PT = T("PT", [128, 512], BF16, 6)
            den = T("den", [128, 8], F32, 2)
            att = T("att", [128, 512], F32, 2)
            junk = self.tile(es, "junkC", [128, 1024], BF16)
            ssn = self.tile(es, "ssn", [128, 16], F32)
            rsn = self.tile(es, "rsn", [128, 16], F32)
            R_ss = [Res() for _ in range(16)]
            R_rs = [Res() for _ in range(16)]
            mixed = T("mixed", [128, 1024], BF16, 3)
            R_mixa = [Res() for _ in range(3)]
            R_mixs = [Res() for _ in range(3)]
            LT = T("LT", [128, 8, 128], F32, 2)
            WT = T("WT", [128, 8, 128], BF16, 2)
            ytmp = T("ytmp", [128, 512], F32, 3)
            xdec = self.tile(es, "xdecC", [128, 512], BF16)
            Hb = self.tile(es, "HbC", [128, 512], F32)
            pvb = T("pvbC", [128, 512], BF16, 2)
            mT = T("mT", [128, 8, 128], BF16, 2)
            x1 = T("x1_", [128, D], F32, 2)
            h2 = T("h2_", [128, D], BF16, 2)
            h2T = T("h2T", [128, 8, 128], BF16, 2)
            dve.op(lambda e: e.memset(Hb[0][:], 0.0), w=[Hb[1]])
            for i in range(4):
                for h in range(2):
                    dve.op(lambda e: e.memset(KP[i][h][0][:], 0.0), w=[KP[i][h][1]])
                dve.op(lambda e: e.memset(V1[i][0][:], 1.0), w=[V1[i][1]])
            sp.dma(self.h2T[:, :, 0:1], self.zero[:, 0:8].rearrange("p (k c) -> p k c", k=8), self.R_zero,
                   r=[self.R_zero], w=[self.R_h2pad], slow=True)
            sp.dma(self.h2T[:, :, S + 1:S + 2], self.zero[:, 0:8].rearrange("p (k c) -> p k c", k=8), self.R_zero,
                   r=[self.R_zero], w=[self.R_h2pad], slow=True)
            ctx = {}
            v3 = lambda ap: ap.rearrange("p (h q) -> p h q", q=64)
            bfv = lambda bi: bank[bi][:].bitcast(BF16)

            def F_load(b):
                hct, R_hc = hc[b % 2]
                sp.dma(hct[:], self.h1T[:, :, 2 + b * 128:2 + (b + 1) * 128], R_hc, r=[self.R_h1T[b]], w=[R_hc])
                sp.dma(xb[b % 3][0][:].rearrange("p k t -> p (k t)"), self.xbcT[b], xb[b % 3][1], r=[self.R_xbcT[b]],
                       w=[xb[b % 3][1]])
                sp.dma(pf[b % 3][0][:], self.prevf[b], pf[b % 3][1], r=[self.R_prevf[b]], w=[pf[b % 3][1]])

            def F1(b):
                hct, R_hc = hc[b % 2]
                qb = [ab(True), ab(True), ab(True)]
                ctx[b] = {"qb": qb}
                for (c0, cn, bi) in ((0, 512, qb[0]), (512, 512, qb[1]), (1024, 256, qb[2])):
                    pe.mm_group([(lambda e, k=k: e.matmul(bank[bi][:, 0:cn], lhsT=hct[:, k, :], rhs=wq[:, k, c0:c0 + cn],
                                                          start=(k == 0), stop=(k == 7))) for k in range(8)],
                                r=[R_hc, R_wq], w=[RB[bi]])

            def rope_part(b, bi, nh, o0, rt):
                src = bank[bi][:, 0:nh * 64].rearrange("p (h d) -> p h d", d=64)
                dst = qk[b % 2][0][:, o0:o0 + nh * 64].rearrange("p (h d) -> p h d", d=64)
                R_qk = qk[b % 2][1]
                cb_ = rope[:, b, 0:8].unsqueeze(1).to_broadcast([128, nh, 8])
                sb_ = rope[:, b, 8:16].unsqueeze(1).to_broadcast([128, nh, 8])
                rr = [RB[bi], R_rope]
                dve.op(lambda e: e.tensor_tensor(out=rt[0][0][:], in0=src[:, :, 0:8], in1=cb_, op=ALU.mult), r=rr, w=[rt[0][1]])
                dve.op(lambda e: e.tensor_tensor(out=rt[1][0][:], in0=src[:, :, 8:16], in1=sb_, op=ALU.mult), r=rr, w=[rt[1][1]])
                dve.op(lambda e: e.tensor_tensor(out=rt[2][0][:], in0=src[:, :, 8:16], in1=cb_, op=ALU.mult), r=rr, w=[rt[2][1]])
                dve.op(lambda e: e.tensor_tensor(out=rt[3][0][:], in0=src[:, :, 0:8], in1=sb_, op=ALU.mult), r=rr, w=[rt[3][1]])
                act.op(lambda e: e.activation(out=dst[:, :, 16:64], in_=src[:, :, 16:64], func=AF.Copy), r=[RB[bi]], w=[R_qk])
                dve.op(lambda e: e.tensor_tensor(out=dst[:, :, 0:8], in0=rt[0][0][:], in1=rt[1][0][:], op=ALU.subtract),
                       r=[rt[0][1], rt[1][1]], w=[R_qk])
                dve.op(lambda e: e.tensor_tensor(out=dst[:, :, 8:16], in0=rt[2][0][:], in1=rt[3][0][:], op=ALU.add),
                       r=[rt[2][1], rt[3][1]], w=[R_qk])

            def F2(b):
                qb = ctx[b]["qb"]
                rope_part(b, qb[0], 8, 0, rtq)
                v1t, R_v1 = V1[b % 4]
                act.op(lambda e: e.activation(out=v1t[:, :, 0:64],
                                              in_=bank[qb[1]][:, 128:256].rearrange("p (h d) -> p h d", d=64), func=AF.Copy),
                       r=[RB[qb[1]]], w=[R_v1])

            def F3(b):
                qb = ctx[b]["qb"]
                rope_part(b, qb[1], 2, 512, rtk)
                zs = [bank[qb[1]][:, 256:512], bank[qb[2]][:, 0:256]]
                for zi in range(2):
                    act.op(lambda e: e.activation(out=thz[b % 2][0][:, zi * 256:(zi + 1) * 256], in_=zs[zi], func=AF.Tanh),
                           r=[RB[qb[1 + zi]]], w=[thz[b % 2][1]])
                for zi in range(2):
                    dve.op(lambda e: e.scalar_tensor_tensor(out=gate[b % 2][0][:, zi * 256:(zi + 1) * 256],
                                                            in0=thz[b % 2][0][:, zi * 256:(zi + 1) * 256], scalar=1.0,
                                                            in1=zs[zi], op0=ALU.add, op1=ALU.mult),
                           r=[thz[b % 2][1], RB[qb[1 + zi]]], w=[gate[b % 2][1]])
                rel(*qb)

            def F4(b):
                bi = ab()
                tb = bfv(bi)
                pe.mm_group([(lambda e, j=j: e.transpose(out=tb[:, j * 128:(j + 1) * 128],
                                                         in_=qk[b % 2][0][:, j * 128:(j + 1) * 128], identity=identb))
                             for j in range(5)], r=[qk[b % 2][1], self.R_cbf], w=[RB[bi]])
                dve.op(lambda e: e.tensor_copy(out=qT[b % 3][0][:].rearrange("p c t -> p (c t)"), in_=tb[:, 0:512]),
                       r=[RB[bi]], w=[qT[b % 3][1]])
                for h in range(2):
                    kpt, R_kp = KP[b % 4][h]
                    dve.op(lambda e: e.tensor_copy(out=kpt[h * 64:(h + 1) * 64, :], in_=tb[h * 64:(h + 1) * 64, 512:640]),
                           r=[RB[bi]], w=[R_kp])

            def F5(b):
                bi = ab()
                tb = bfv(bi)
                xbt, R_xb = xb[b % 3]
                pe.mm_group([(lambda e, j=j: e.transpose(out=tb[:, j * 128:(j + 1) * 128], in_=xbt[:, j, :], identity=identb))
                             for j in range(6)], r=[R_xb, self.R_cbf], w=[RB[bi]])
                dve.op(lambda e: e.tensor_copy(out=tok[b % 2][0][:], in_=tb[:, 0:768]), r=[RB[bi]], w=[tok[b % 2][1]])

            def S_exp(c, d):
                xbt, R_xb = xb[c % 3]
                cols = slice(c * 128, (c + 1) * 128)
                msk = mle if d == 0 else mge
                for g in range(2):
                    bi = ab()
                    hd0 = d * 8 + g * 4
                    fns = [(lambda e, hh=hh: e.matmul(bank[bi][:, hh * 128:(hh + 1) * 128], lhsT=ind[0:96, hd0 + hh, :],
                                                      rhs=self.PHI3[0:96, cols], start=(hh == 0), stop=False))
                           for hh in range(4)]
                    fns.append(lambda e: e.matmul(bank[bi][:, :], lhsT=self.PSI3[0:96, cols],
                                                  rhs=ind[0:96, hd0:hd0 + 4, :].rearrange("p h t -> p (h t)"),
                                                  start=False, stop=False))
                    fns.append(lambda e: e.matmul(bank[bi][:, :], lhsT=identb, rhs=msk, start=False, stop=True))
                    pe.mm_group(fns, r=[self.R_cbf, self.R_PHI3, self.R_PSI3], w=[RB[bi]])
                    act.op(lambda e: e.activation(out=LT[d][0][:, g * 4:(g + 1) * 4, :].rearrange("p h t -> p (h t)"),
                                                  in_=bank[bi][:, :], func=AF.Exp), r=[RB[bi]], w=[LT[d][1]])

            def S_wt(c, d):
                cbk = ctx[c]["cbk"]
                cbb = bank[cbk][:, 0:256].rearrange("p (g t) -> p g t", g=2).unsqueeze(2).to_broadcast([128, 2, 4, 128])
                dve.op(lambda e: e.tensor_tensor(out=WT[d][0][:].rearrange("p (g h) t -> p g h t", g=2),
                                                 in0=LT[d][0][:].rearrange("p (g h) t -> p g h t", g=2), in1=cbb,
                                                 op=ALU.mult), r=[LT[d][1], RB[cbk]], w=[WT[d][1]])

            def S1(c):
                xbt, R_xb = xb[c % 3]
                cbk = ab(True)
                ctx[c]["cbk"] = cbk
                pe.mm_group([(lambda e, g=g: e.matmul(bank[cbk][:, g * 128:(g + 1) * 128], lhsT=xbt[:, 4 + g, :],
                                                      rhs=xbt[:, 6 + g, :], start=(g == 0), stop=(g == 1)))
                             for g in range(2)], r=[R_xb], w=[RB[cbk]])
                S_exp(c, 0)

            def S2(c):
                S_wt(c, 0)
                S_exp(c, 1)

            def S3(c):
                S_wt(c, 1)
                rel(ctx[c]["cbk"])
                xbt, R_xb = xb[c % 3]
                act.op(lambda e: e.activation(out=pvb[c % 2][0][:], in_=Hb[0][:], func=AF.Copy), r=[Hb[1]], w=[pvb[c % 2][1]])
                yo = [ab(True), ab(True)]
                ctx[c]["yo"] = yo
                for d in range(2):
                    pv_t, R_pvx = (pf[c % 3] if d == 0 else pvb[c % 2])
                    pe.mm_group([(lambda e, g=g: e.matmul(bank[yo[d]][:, g * 256:(g + 1) * 256], lhsT=xbt[:, 6 + g, :],
                                                          rhs=pv_t[:, g * 256:(g + 1) * 256], start=(g == 0), stop=(g == 1)))
                                 for g in range(2)], r=[R_xb, R_pvx], w=[RB[yo[d]]])

            def S4(c):
                tokt, R_tok = tok[c % 2]
                yd = ab(True)
                ctx[c]["yd"] = yd
                fns = []
                for h in range(8):
                    for (i, lt) in enumerate((WT[0][0][:, h, :], WT[1][0][:, h, :], Dm[:, h, :])):
                        fns.append(lambda e, h=h, i=i, lt=lt: e.matmul(bank[yd][:, h * 64:(h + 1) * 64], lhsT=lt,
                                                                      rhs=tokt[:, h * 64:(h + 1) * 64],
                                                                      start=(h == 0 and i == 0), stop=(h == 7 and i == 2)))
                pe.mm_group(fns, r=[WT[0][1], WT[1][1], R_Dm, R_tok], w=[RB[yd]])
                yo = ctx[c]["yo"]
                eyf = self.TM[:, c, 0:8].unsqueeze(2).to_broadcast([128, 8, 64])
                eyb = self.TM[:, c, 8:16].unsqueeze(2).to_broadcast([128, 8, 64])
                y1, y2, y3 = ytmp
                dve.op(lambda e: e.tensor_tensor(out=v3(y1[0][:]), in0=v3(bank[yo[0]][:, :]), in1=eyf, op=ALU.mult),
                       r=[RB[yo[0]], self.R_TM], w=[y1[1]])
                dve.op(lambda e: e.tensor_tensor(out=v3(y2[0][:]), in0=v3(bank[yo[1]][:, :]), in1=eyb, op=ALU.mult),
                       r=[RB[yo[1]], self.R_TM], w=[y2[1]])
                rel(*yo)

            def S5(c):
                yd = ctx[c]["yd"]
                y1, y2, y3 = ytmp
                dve.op(lambda e: e.tensor_tensor(out=y3[0][:], in0=bank[yd][:, :], in1=y1[0][:], op=ALU.add),
                       r=[RB[yd], y1[1]], w=[y3[1]])
                rel(yd)
                dve.op(lambda e: e.tensor_tensor(out=y3[0][:], in0=y3[0][:], in1=y2[0][:], op=ALU.add),
                       r=[y2[1]], w=[y3[1]])
                dve.op(lambda e: e.tensor_tensor(out=y3[0][:], in0=y3[0][:], in1=gate[c % 2][0][:], op=ALU.mult),
                       r=[gate[c % 2][1]], w=[y3[1]])

            def S6(c):
                y3 = ytmp[2]
                q = (c % 4) * 4
                for g in range(2):
                    act.op(lambda e: e.activation(out=junk[0][:, g * 256:(g + 1) * 256], in_=y3[0][:, g * 256:(g + 1) * 256],
                                                  func=AF.Square, accum_out=ssn[0][:, q + 1 + g:q + 2 + g]),
                           r=[y3[1]], w=[junk[1], R_ss[q + 1 + g]])
                self.rstd(ssn[0][:, q + 1:q + 3], 256, rsn[0][:, q + 1:q + 3], [R_ss[q + 1], R_ss[q + 2]],
                          [R_rs[q + 1], R_rs[q + 2]])
                self.state_update(tok[c % 2][0], tok[c % 2][1], xdec[0], xdec[1], Hb[0], Hb[1], c, 1, bankid=ab())

            def S7(c):
                y3 = ytmp[2]
                q = (c % 4) * 4
                for g in range(2):
                    act.op(lambda e: e.activation(out=mixed[c % 3][0][:, 512 + g * 256:512 + (g + 1) * 256],
                                                  in_=y3[0][:, g * 256:(g + 1) * 256], func=AF.Copy,
                                                  scale=rsn[0][:, q + 1 + g:q + 2 + g]), r=[y3[1], R_rs[q + 1 + g]],
                           w=[R_mixs[c % 3]])

            def A_scores(c, half):
                kbs = [kb for kb in (c - 1, c, c + 1) if 0 <= kb < nb]
                for ki, kb in enumerate(kbs):
                    bi = ab()
                    last = (kb == c)
                    fns = [lambda e: e.matmul(bank[bi][:, :], lhsT=KP[kb % 4][half][0][:, :],
                                              rhs=qT[c % 3][0][:].rearrange("p c t -> p (c t)"), start=True, stop=last)]
                    if not last:
                        m = mge if kb < c else mle
                        fns.append(lambda e: e.matmul(bank[bi][:, :], lhsT=identb, rhs=m, start=False, stop=True))
                    pe.mm_group(fns, r=[KP[kb % 4][half][1], qT[c % 3][1], self.R_cbf], w=[RB[bi]])
                    pt = PT[half * 3 + ki]
                    act.op(lambda e: e.activation(out=pt[0][:], in_=bank[bi][:, :], func=AF.Exp, scale=0.125),
                           r=[RB[bi]], w=[pt[1]])

            def A_pv(c, half):
                kbs = [kb for kb in (c - 1, c, c + 1) if 0 <= kb < nb]
                pb = ab()
                po = bank[pb][:, 0:260].rearrange("p (h d) -> p h d", d=65)
                fns = []
                rd = []
                for i in range(4):
                    for ki, kb in enumerate(kbs):
                        pt = PT[half * 3 + ki]
                        fns.append(lambda e, i=i, ki=ki, kb=kb, pt=pt: e.matmul(
                            po[:, i, :], lhsT=pt[0][:, i * 128:(i + 1) * 128], rhs=V1[kb % 4][0][:, half, :],
                            start=(ki == 0), stop=(ki == len(kbs) - 1)))
                        rd += [pt[1], V1[kb % 4][1]]
                pe.mm_group(fns, r=rd, w=[RB[pb]])
                hs = slice(half * 4, half * 4 + 4)
                dn = den[c % 2]
                at_ = att[c % 2]
                dve.op(lambda e: e.tensor_tensor(out=dn[0][:, hs], in0=po[:, :, 64], in1=esink[:, hs], op=ALU.add),
                       r=[RB[pb], R_esink], w=[dn[1]])
                dve.op(lambda e: e.reciprocal(out=dn[0][:, hs], in_=dn[0][:, hs]), r=[], w=[dn[1]])
                dve.op(lambda e: e.tensor_tensor(
                    out=at_[0][:, half * 256:(half + 1) * 256].rearrange("p (h d) -> p h d", d=64),
                    in0=po[:, :, 0:64], in1=dn[0][:, hs].unsqueeze(2).to_broadcast([128, 4, 64]), op=ALU.mult),
                    r=[RB[pb], dn[1]], w=[at_[1]])

            def A5(c):
                q = (c % 4) * 4
                at_ = att[c % 2]
                act.op(lambda e: e.activation(out=junk[0][:, 0:512], in_=at_[0][:], func=AF.Square, accum_out=ssn[0][:, q:q + 1]),
                       r=[at_[1]], w=[junk[1], R_ss[q]])
                self.rstd(ssn[0][:, q:q + 1], 512, rsn[0][:, q:q + 1], [R_ss[q]], [R_rs[q]])

            def A6(c):
                q = (c % 4) * 4
                at_ = att[c % 2]
                act.op(lambda e: e.activation(out=mixed[c % 3][0][:, 0:512], in_=at_[0][:], func=AF.Copy,
                                              scale=rsn[0][:, q:q + 1]), r=[at_[1], R_rs[q]], w=[R_mixa[c % 3]])

            def O_load(c):
                sp.dma(xr[c % 2][0][:], self.x[c * 128:(c + 1) * 128, :], xr[c % 2][1], w=[xr[c % 2][1]])

            def O1(c):
                bi = ab()
                tb = bfv(bi)
                mx = mixed[c % 3][0]
                pe.mm_group([(lambda e, k=k: e.transpose(out=tb[:, k * 128:(k + 1) * 128], in_=mx[:, k * 128:(k + 1) * 128],
                                                         identity=identb)) for k in range(8)],
                            r=[R_mixa[c % 3], R_mixs[c % 3], self.R_cbf], w=[RB[bi]])
                dve.op(lambda e: e.tensor_copy(out=mT[c % 2][0][:].rearrange("p k t -> p (k t)"), in_=tb[:, :]),
                       r=[RB[bi]], w=[mT[c % 2][1]])

            def O2(c):
                ob = [ab(), ab()]
                for cg in range(2):
                    pe.mm_group([(lambda e, k=k: e.matmul(bank[ob[cg]][:, :], lhsT=mT[c % 2][0][:, k, :],
                                                          rhs=wo[:, k, cg * 512:(cg + 1) * 512], start=(k == 0), stop=(k == 7)))
                                 for k in range(8)], r=[mT[c % 2][1], R_wo], w=[RB[ob[cg]]])
                x1t, R_x1 = x1[c % 2]
                for cg in range(2):
                    dve.op(lambda e: e.tensor_tensor(out=x1t[:, cg * 512:(cg + 1) * 512], in0=bank[ob[cg]][:, :],
                                                     in1=xr[c % 2][0][:, cg * 512:(cg + 1) * 512], op=ALU.add),
                           r=[RB[ob[cg]], xr[c % 2][1]], w=[R_x1])
                sp.dma(self.x1s[c * 128:(c + 1) * 128, :], x1t[:], R_x1, r=[R_x1], w=[self.R_x1[c]])
                q = (c % 4) * 4 + 3
                act.op(lambda e: e.activation(out=junk[0][:], in_=x1t[:], func=AF.Square, accum_out=ssn[0][:, q:q + 1]),
                       r=[R_x1], w=[junk[1], R_ss[q]])
                self.rstd(ssn[0][:, q:q + 1], D, rsn[0][:, q:q + 1], [R_ss[q]], [R_rs[q]])

            def O3(c):
                x1t, R_x1 = x1[c % 2]
                q = (c % 4) * 4 + 3
                act.op(lambda e: e.activation(out=h2[c % 2][0][:], in_=x1t[:], func=AF.Copy, scale=rsn[0][:, q:q + 1]),
                       r=[R_x1, R_rs[q]], w=[h2[c % 2][1]])
                bi = ab()
                tb = bfv(bi)
                pe.mm_group([(lambda e, k=k: e.transpose(out=tb[:, k * 128:(k + 1) * 128],
                                                         in_=h2[c % 2][0][:, k * 128:(k + 1) * 128], identity=identb))
                             for k in range(8)], r=[h2[c % 2][1], self.R_cbf], w=[RB[bi]])
                dve.op(lambda e: e.tensor_copy(out=h2T[c % 2][0][:].rearrange("p k t -> p (k t)"), in_=tb[:, :]),
                       r=[RB[bi]], w=[h2T[c % 2][1]])
                sp.dma(self.h2T[:, :, 1 + c * 128:1 + (c + 1) * 128], h2T[c % 2][0][:], h2T[c % 2][1], r=[h2T[c % 2][1]],
                       w=[self.R_h2T[c]])

            order = list(range(nb - 1, -1, -1))
            chains = {
                "F": [F1, F2, F3, F4, F5],
                "S": [S1, S2, S3, S4, S5, S6, S7],
                "A": [lambda c: A_scores(c, 0), lambda c: A_pv(c, 0), lambda c: A_scores(c, 1), lambda c: A_pv(c, 1), A5, A6],
                "O": [O1, O2, O3],
            }
            lag = {"F": 0, "S": 1, "A": 2, "O": 3}
            F_load(order[0])
            for k in range(nb + 3):
                if k + 1 < nb:
                    F_load(order[k + 1])
                if 0 <= k - 3 < nb:
                    O_load(order[k - 3])
                nsteps = max(len(v) for v in chains.values())
                for i in range(nsteps):
                    for name in ("O", "A", "S", "F"):
                        p = k - lag[name]
                        if 0 <= p < nb and i < len(chains[name]):
                            chains[name][i](order[p])

    def phase_f(self):
        nc, S, nb = self.nc, self.S, self.nb
        pe, act, dve, pool, sp = self.pe, self.act, self.dve, self.pool, self.sp
        RB = self.R_bank
        bank = self.bank
        NJ = DFF // 128
        with ExitStack() as es:
            wu, _ = self.tile(es, "wup", [128, 8, 2 * DFF], BF16)
            wd, _ = self.tile(es, "wdown", [128, NJ, D], BF16)
            gft, R_gf = self.tile(es, "gft", [128, D], F32)
            R_wu = [Res() for _ in range(NJ // 2)]
            R_wd = [Res() for _ in range(2)]
            for r in R_wu + R_wd:
                self.alltok_res.append(r)
            for p in range(NJ // 2):
                for base in (0, DFF):
                    c0 = base + p * 256
                    sp.dma(wu[:, :, c0:c0 + 256], self.wu_s[:, :, c0:c0 + 256], R_wu[p], w=[R_wu[p]])
            for h in range(2):
                sp.dma(wd[:, h * 11:(h + 1) * 11, :], self.wd_s[:, h * 11:(h + 1) * 11, :], R_wd[h], w=[R_wd[h]])
            sp.dma(gft[:], self.gf_d, R_gf, w=[R_gf])
            gf = gft[:]
            fcw = self.c_("fcw")
            fcb = self.c_("fcb")
            NW = FT + 2
            hw = [self.tile(es, "hwF%d" % i, [128, 8, NW], BF16) for i in range(2)]
            gT = self.tile(es, "gT", [128, NJ, FT], BF16)
            ag = [self.tile(es, "agF%d" % i, [128, FT], F32) for i in range(2)]
            av = [self.tile(es, "avF%d" % i, [128, FT], F32) for i in range(2)]
            tg = [self.tile(es, "tgF%d" % i, [128, FT], F32) for i in range(2)]
            x1 = [self.tile(es, "x1F%d" % i, [128, D], F32) for i in range(2)]
            x2 = [self.tile(es, "x2F%d" % i, [128, D], F32) for i in range(2)]
            junk = self.tile(es, "junkF", [128, D], BF16)
            ssf = self.tile(es, "ssf", [128, 2], F32)
            rsf = self.tile(es, "rsf", [128, 2], F32)
            R_ssf = [Res(), Res()]
            R_rsf = [Res(), Res()]
            tiles = []
            t0 = 0
            while t0 < S:
                tiles.append((t0, min(FT, S - t0)))
                t0 += FT
            blkctr = [0]

            def load(ti):
                t0, nt = tiles[ti]
                hwt, R_hw = hw[ti % 2]
                n = nt + 2
                b0, b1 = t0 // 128, min(nb - 1, (t0 + nt) // 128)
                rd = [self.R_h2pad] + [self.R_h2T[i] for i in range(max(0, b0 - 1), min(nb, b1 + 2))]
                sp.dma(hwt[:, :, 0:n], self.h2T[:, :, t0:t0 + n], R_hw, r=rd, w=[R_hw])

            def up(ti):
                t0, nt = tiles[ti]
                hwt, R_hw = hw[ti % 2]
                n = nt + 2
                for j in range(NJ):
                    p = j % 2
                    bg, bv = 2 * p, 2 * p + 1
                    for (bi, c0) in ((bg, j * 128), (bv, DFF + j * 128)):
                        pe.mm_group([(lambda e, k=k: e.matmul(bank[bi][:, 0:n], lhsT=wu[:, k, c0:c0 + 128],
                                                              rhs=hwt[:, k, 0:n], start=(k == 0), stop=(k == 7)))
                                     for k in range(8)], r=[R_wu[j // 2], R_hw], w=[RB[bi]])
                    agt, R_ag = ag[p]
                    avt, R_av = av[p]
                    tgt, R_tg = tg[p]
                    for (bi, acc, R_a, wc, bc_) in ((bg, agt, R_ag, j * 3, j), (bv, avt, R_av, 66 + j * 3, 22 + j)):
                        act.op(lambda e: e.activation(out=acc[:, 0:nt], in_=bank[bi][:, 0:nt], func=AF.Identity,
                                                      scale=fcw[:, wc:wc + 1], bias=fcb[:, bc_:bc_ + 1]),
                               r=[RB[bi], self.R_cst], w=[R_a])
                    for t in (1, 2):
                        for (bi, acc, R_a, wc) in ((bg, agt, R_ag, j * 3), (bv, avt, R_av, 66 + j * 3)):
                            dve.op(lambda e: e.scalar_tensor_tensor(out=acc[:, 0:nt], in0=bank[bi][:, t:t + nt],
                                                                    scalar=fcw[:, wc + t:wc + t + 1], in1=acc[:, 0:nt],
                                                                    op0=ALU.mult, op1=ALU.add),
                                   r=[RB[bi], self.R_cst], w=[R_a])
                    act.op(lambda e: e.activation(out=tgt[:, 0:nt], in_=agt[:, 0:nt], func=AF.Silu), r=[R_ag], w=[R_tg])
                    dve.op(lambda e: e.tensor_tensor(out=gT[0][:, j, 0:nt], in0=tgt[:, 0:nt], in1=avt[:, 0:nt], op=ALU.mult),
                           r=[R_tg, R_av], w=[gT[1]])

            def down(ti):
                t0, nt = tiles[ti]
                o = 0
                while o < nt:
                    m = min(128, nt - o)
                    q = blkctr[0] % 2
                    blkctr[0] += 1
                    r0 = t0 + o
                    x1t, R_x1 = x1[q]
                    x2t, R_x2 = x2[q]
                    bl0, bl1 = r0 // 128, (r0 + m - 1) // 128
                    sp.dma(x1t[0:m, :], self.x1s[r0:r0 + m, :], R_x1, r=[self.R_x1[i] for i in range(bl0, bl1 + 1)], w=[R_x1])
                    for cg in range(2):
                        bi = 4 + 2 * q + cg
                        pe.mm_group([(lambda e, j=j: e.matmul(bank[bi][0:m, :], lhsT=gT[0][:, j, o:o + m],
                                                              rhs=wd[:, j, cg * 512:(cg + 1) * 512], start=(j == 0),
                                                              stop=(j == NJ - 1))) for j in range(NJ)],
                                    r=[gT[1]] + R_wd, w=[RB[bi]])
                    for cg in range(2):
                        bi = 4 + 2 * q + cg
                        dve.op(lambda e: e.tensor_tensor(out=x2t[0:m, cg * 512:(cg + 1) * 512], in0=bank[bi][0:m, :],
                                                         in1=x1t[0:m, cg * 512:(cg + 1) * 512], op=ALU.add),
                               r=[RB[bi], R_x1], w=[R_x2])
                    act.op(lambda e: e.activation(out=junk[0][0:m, :], in_=x2t[0:m, :], func=AF.Square,
                                                  accum_out=ssf[0][0:m, q:q + 1]), r=[R_x2], w=[junk[1], R_ssf[q]])
                    self.rstd(ssf[0][0:m, q:q + 1], D, rsf[0][0:m, q:q + 1], [R_ssf[q]], [R_rsf[q]])
                    dve.op(lambda e: e.scalar_tensor_tensor(out=x2t[0:m, :], in0=x2t[0:m, :], scalar=rsf[0][0:m, q:q + 1],
                                                            in1=gf[0:m, :], op0=ALU.mult, op1=ALU.mult),
                           r=[R_rsf[q], R_gf], w=[R_x2])
                    sp.dma(self.out[r0:r0 + m, :], x2t[0:m, :], R_x2, r=[R_x2], w=[self.R_out])
                    o += m

            load(0)
            for ti in range(len(tiles)):
                if ti + 1 < len(tiles):
                    load(ti + 1)
                up(ti)
                down(ti)


def host_consts(S, p):
    nb = S // 128
    cst = np.zeros((128, CST_COLS), np.float32)

    def put(name, arr):
        a, b = _c[name]
        cst[:, a:b] = np.asarray(arr, np.float32).reshape(128, b - a)

    put("g1", p["norm1_g"][0].reshape(8, 128).T)
    put("g2", p["norm2_g"][0].reshape(8, 128).T)
    put("ga", p["attn_out_g"][0].reshape(4, 128).T)
    put("gs", p["ssd_norm_g"][0].reshape(4, 128).T)
    put("scw", p["ssd_conv_w"][0].reshape(5, 8, 128).transpose(2, 1, 0))
    put("scb", p["ssd_conv_b"][0].reshape(8, 128).T)
    put("fcw", p["ffn_conv_w"][0].reshape(3, 44, 128).transpose(2, 1, 0))
    put("fcb", p["ffn_conv_b"][0].reshape(44, 128).T)
    dtb = np.zeros((128, 1), np.float32)
    alog = np.zeros((128, 1), np.float32)
    sel = np.zeros((128, 3), np.float32)
    both_b = np.concatenate([p["ssd_dt_bias_fwd"][0], p["ssd_dt_bias_bwd"][0]])
    both_a = np.concatenate([p["ssd_a_log_fwd"][0], p["ssd_a_log_bwd"][0]])
    for grp in range(3):
        dtb[grp * 32:grp * 32 + 16, 0] = both_b
        alog[grp * 32:grp * 32 + 16, 0] = both_a
        sel[grp * 32:grp * 32 + 8, 0] = 1.0
        sel[grp * 32 + 8:grp * 32 + 16, 0] = -1.0
        sel[grp * 32 + 8:grp * 32 + 16, 1] = 1.0
        sel[grp * 32:grp * 32 + 8, 2] = 1.0
    put("dtb", dtb)
    put("alog", alog)
    put("sel", sel)
    put("dskip", np.tile(p["ssd_d"][0][None, :], (128, 1)))
    put("sink", np.tile(p["attn_sink"][0][None, :], (128, 1)))
    put("identf", np.eye(128, dtype=np.float32))
    cbf = np.zeros((128, CBF_COLS), np.float32)
    cbf[:, 0:128] = np.eye(128)
    pp = np.arange(128)[:, None]
    ff = np.arange(128)[None, :]
    mle = np.where(pp <= ff, 0.0, NEGM)
    mge = np.where(pp >= ff, 0.0, NEGM)
    cbf[:, 128:640] = np.tile(mle, (1, 4))
    cbf[:, 640:1152] = np.tile(mge, (1, 4))
    ind = np.zeros((128, 16, 128), np.float32)
    for grp in range(3):
        for hd in range(16):
            ind[grp * 32 + hd, hd, :] = 1.0
    cbf[:, 1152:] = ind.reshape(128, 2048)
    cbf = cbf.astype(ml_dtypes.bfloat16)
    pos = np.arange(S, dtype=np.float32)
    inv = (np.float32(500000.0) ** (-np.arange(0, 16, 2, dtype=np.float32) / np.float32(16))).astype(np.float32)
    ang = (pos[:, None] * inv[None, :]).astype(np.float32)
    tab = np.concatenate([np.cos(ang), np.sin(ang)], axis=1).astype(np.float32)
    rope = tab.reshape(nb, 128, 16).transpose(1, 0, 2).reshape(128, nb * 16).copy()
    gfc = np.ascontiguousarray(np.tile(np.asarray(p["final_norm_g"], np.float32)[None, :], (128, 1)))
    return cst, cbf, rope, gfc


_NC_CACHE = {}


def run(inputs, S, n_cores, dbg=False, phases="A1,D,AB,C,F"):
    p = {k: np.asarray(v) for k, v in inputs.items()}
    key = (S, dbg, phases)
    if key not in _NC_CACHE:
        _NC_CACHE[key] = KB(S, dbg=dbg, phases=phases).build()
    nc = _NC_CACHE[key]
    cst, cbf, rope, gfc = host_consts(S, p)
    shared = {
        "w_in": np.ascontiguousarray(p["w_in"][0], np.float32),
        "w_out": np.ascontiguousarray(p["w_out"][0], np.float32),
        "w_up": np.ascontiguousarray(p["w_up"][0], np.float32),
        "w_down": np.ascontiguousarray(p["w_down"][0], np.float32),
        "cst": cst, "cbf": cbf, "rope": rope, "gfc": gfc,
    }
    in_maps = []
    for c in range(n_cores):
        m = dict(shared)
        m["x"] = np.ascontiguousarray(p["x"][c], np.float32)
        in_maps.append(m)
    res = run_bass_kernel_spmd(nc, in_maps, core_ids=list(range(n_cores)))
    return res


def kernel(**inputs):
    res = run(inputs, SEQ, NCORES)
    return np.stack([np.asarray(r["out"], np.float32) for r in res.results], axis=0)
```

```python
import numpy as np
import ml_dtypes
from contextlib import ExitStack
import concourse.bass as bass
import concourse.mybir as mybir
from concourse.bass_utils import run_bass_kernel_spmd

F32 = mybir.dt.float32
BF16 = mybir.dt.bfloat16
AF = mybir.ActivationFunctionType
ALU = mybir.AluOpType

D = 1024
SEQ = 8192
NCORES = 8
EPS = 1e-5
NEGM = -30000.0
IN_COLS = 2320
DFF = 2816
FT = 382

_c = {}
_o = 0
for _n, _w in [("g1", 8), ("g2", 8), ("ga", 4), ("gs", 4), ("scw", 40), ("scb", 8), ("fcw", 132), ("fcb", 44),
               ("dtb", 1), ("alog", 1), ("sel", 3), ("dskip", 8), ("sink", 8), ("identf", 128)]:
    _c[_n] = (_o, _o + _w)
    _o += _w
CST_COLS = _o
_b = {"identb": (0, 128), "mle": (128, 640), "mge": (640, 1152), "ind": (1152, 1152 + 2048)}
CBF_COLS = 1152 + 2048


class Res:
    __slots__ = ("w", "r", "sem", "cnt", "psum")

    def __init__(self, psum=False):
        self.w = None
        self.r = {}
        self.sem = None
        self.cnt = 0
        self.psum = psum


class Eng:
    def __init__(self, kb, e, sem, is_pe=False):
        self.kb = kb
        self.e = e
        self.sem = sem
        self.cnt = 0
        self.seen = {}
        self.is_pe = is_pe
        self.pend_r = []
        self.pend_w = []

    def wait(self, tok, raw=True):
        if tok is None:
            return
        sem, val = tok
        if self.seen.get(sem, 0) >= val:
            return
        if sem is self.sem and (self.is_pe or not raw):
            return
        self.e.wait_ge(sem, val)
        self.seen[sem] = val

    def deps(self, reads, writes):
        for r in reads:
            self.wait(r.w, raw=True)
            if r.psum:
                for s, v in r.r.items():
                    if s is not self.sem:
                        self.wait((s, v), raw=True)
        for w in writes:
            self.wait(w.w, raw=True)
            for s, v in w.r.items():
                self.wait((s, v), raw=True)

    @staticmethod
    def mark(tok, reads, writes):
        for r in reads:
            if r.r.get(tok[0], 0) < tok[1]:
                r.r[tok[0]] = tok[1]
        for w in writes:
            w.w = tok
            w.r = {}

    def op(self, fn, r=(), w=(), inc=True):
        self.deps(r, w)
        ins = fn(self.e)
        if not inc:
            self.pend_r += list(r)
            self.pend_w += list(w)
            return None
        self.cnt += 1
        ins.then_inc(self.sem, 1)
        tok = (self.sem, self.cnt)
        self.mark(tok, list(r) + self.pend_r, list(w) + self.pend_w)
        self.pend_r = []
        self.pend_w = []
        return tok

    def mm_group(self, fns, r=(), w=()):
        tok = None
        for i, fn in enumerate(fns):
            tok = self.op(fn, r=r, w=w, inc=(i == len(fns) - 1))
        return tok

    def dma(self, out, in_, semres, r=(), w=(), slow=False):
        self.deps(r, w)
        if semres.sem is None:
            semres.sem = self.kb.new_sem()
        ins = self.e.dma_start(out=out, in_=in_, allow_slow_non_contiguous=True) if slow else self.e.dma_start(out=out, in_=in_)
        semres.cnt += 16
        ins.then_inc(semres.sem, 16)
        tok = (semres.sem, semres.cnt)
        self.mark(tok, r, w)
        return tok


class KB:
    def __init__(self, S, dbg=False, phases="A1,D,AB,C,F"):
        self.S = S
        self.nb = S // 128
        self.dbg = dbg
        self.phases = phases.split(",")
        self.nc = bass.Bass("TRN2", target_bir_lowering=False)
        self.es = ExitStack()
        self.nsem = 0
        self.alltok_res = []

    def new_sem(self):
        self.nsem += 1
        return self.es.enter_context(self.nc.semaphore("s%d" % self.nsem))

    def tile(self, es, name, shape, dt):
        t = es.enter_context(self.nc.sbuf_tensor("t_" + name, list(shape), dt))
        r = Res()
        self.alltok_res.append(r)
        return t, r

    def dram(self, name, shape, dt, kind="Internal"):
        return self.nc.dram_tensor(name, list(shape), dt, kind=kind).ap()

    def barrier(self):
        toks = [(e.sem, e.cnt) for e in self.engs if e.cnt > 0]
        for r in self.alltok_res + self.dram_res:
            if r.sem is not None and r.cnt > 0:
                toks.append((r.sem, r.cnt))
        for e in self.engs:
            for t in toks:
                e.wait(t)

    def build(self):
        nc, S, nb = self.nc, self.S, self.nb
        es = self.es
        okind = "ExternalOutput" if self.dbg else "Internal"
        self.x = self.dram("x", [S, D], F32, "ExternalInput")
        self.w_in = self.dram("w_in", [D, IN_COLS], F32, "ExternalInput")
        self.w_out = self.dram("w_out", [D, D], F32, "ExternalInput")
        self.w_up = self.dram("w_up", [D, 2 * DFF], F32, "ExternalInput")
        self.w_down = self.dram("w_down", [DFF, D], F32, "ExternalInput")
        self.cst_d = self.dram("cst", [128, CST_COLS], F32, "ExternalInput")
        self.cbf_d = self.dram("cbf", [128, CBF_COLS], BF16, "ExternalInput")
        self.rope_d = self.dram("rope", [128, nb * 16], F32, "ExternalInput")
        self.gf_d = self.dram("gfc", [128, D], F32, "ExternalInput")
        self.out = self.dram("out", [S, D], F32, "ExternalOutput")
        self.h1T = self.dram("h1T", [128, 8, S + 4], BF16, okind)
        self.xbcT = self.dram("xbcT", [nb, 128, 1024], BF16, okind)
        self.prevf = self.dram("prevf", [nb, 128, 512], BF16, okind)
        self.x1s = self.dram("x1s", [S, D], F32, okind)
        self.h2T = self.dram("h2T", [128, 8, S + 2], BF16, okind)
        self.wq_s = self.dram("wq_s", [128, 8, 1280], BF16)
        self.wo_s = self.dram("wo_s", [128, 8, 1024], BF16)
        self.wu_s = self.dram("wu_s", [128, 8, 2 * DFF], BF16)
        self.wd_s = self.dram("wd_s", [128, DFF // 128, D], BF16)
        self.R_wscr = Res()
        if self.dbg:
            self.dbgD = self.dram("dbgD", [128, nb * 32 + nb * 16], F32, "ExternalOutput")
        self.R_h1T = [Res() for _ in range(nb)]
        self.R_h1pad = Res()
        self.R_xbcT = [Res() for _ in range(nb)]
        self.R_prevf = [Res() for _ in range(nb)]
        self.R_x1 = [Res() for _ in range(nb)]
        self.R_h2T = [Res() for _ in range(nb)]
        self.R_h2pad = Res()
        self.R_out = Res()
        self.dram_res = self.R_h1T + self.R_xbcT + self.R_prevf + self.R_x1 + self.R_h2T + [self.R_h1pad, self.R_h2pad, self.R_out, self.R_wscr]

        self.pe = Eng(self, nc.tensor, self.new_sem(), is_pe=True)
        self.act = Eng(self, nc.scalar, self.new_sem())
        self.dve = Eng(self, nc.vector, self.new_sem())
        self.pool = Eng(self, nc.gpsimd, self.new_sem())
        self.sp = Eng(self, nc.sync, self.new_sem())
        self.engs = [self.pe, self.act, self.dve, self.pool, self.sp]

        self.bank = []
        self.R_bank = []
        for i in range(8):
            t = es.enter_context(nc.psum_tensor("bank%d" % i, [128, 512], F32))
            self.bank.append(t)
            self.R_bank.append(Res(psum=True))

        self.cst, self.R_cst = self.tile(es, "cst", [128, CST_COLS], F32)
        self.cbf, self.R_cbf = self.tile(es, "cbf", [128, CBF_COLS], BF16)
        self.mh, self.R_mh = self.tile(es, "mh", [128, 8], F32)
        self.zero, self.R_zero = self.tile(es, "zero", [128, 16], BF16)
        self.sp.dma(self.cst[:], self.cst_d, self.R_cst, w=[self.R_cst])
        self.sp.dma(self.cbf[:], self.cbf_d, self.R_cbf, w=[self.R_cbf])
        self.epsc, self.R_epsc = self.tile(es, "epsc", [128, 2], F32)
        self.pool.op(lambda e: e.memset(self.epsc[:], EPS), w=[self.R_epsc])
        self.pool.op(lambda e: e.memset(self.mh[:], -0.5), w=[self.R_mh])
        self.pool.op(lambda e: e.memset(self.zero[:], 0.0), w=[self.R_zero])

        with ExitStack() as es_ssd:
            self.PHI3, self.R_PHI3 = self.tile(es_ssd, "PHI3", [128, S], BF16)
            self.PSI3, self.R_PSI3 = self.tile(es_ssd, "PSI3", [128, S], BF16)
            self.TM, self.R_TM = self.tile(es_ssd, "TM", [128, nb, 32], F32)
            self.DECB, self.R_DECB = self.tile(es_ssd, "DECB", [128, nb, 16], F32)
            with ExitStack() as es_x:
                self.X = []
                self.R_X = []
                for i in range(4):
                    t, r = self.tile(es_x, "X%d" % i, [128, S], F32)
                    self.X.append(t)
                    self.R_X.append(r)
                if "A1" in self.phases:
                    self.phase_a1()
                    self.barrier()
                if "D" in self.phases:
                    self.phase_d()
                    self.barrier()
            if "AB" in self.phases:
                self.phase_ab()
                self.barrier()
            if "C" in self.phases:
                self.phase_c()
                self.barrier()
        if "F" in self.phases:
            self.phase_f()
            self.barrier()
        self.barrier()
        self.es.close()
        return nc

    def c_(self, name, lo=None, hi=None):
        a, b = _c[name]
        if lo is not None:
            a, b = a + lo, a + hi
        return self.cst[:, a:b]

    def rstd(self, ss_ap, n, out_ap, rr, rw):
        k = out_ap.shape[1]
        self.pool.op(lambda e: e.tensor_scalar(out=out_ap, in0=ss_ap, scalar1=1.0 / n, scalar2=EPS,
                                               op0=ALU.mult, op1=ALU.add), r=rr, w=rw)
        self.pool.op(lambda e: e.tensor_tensor(out=out_ap, in0=out_ap, in1=self.mh[0:out_ap.shape[0], 0:k], op=ALU.pow),
                     r=[self.R_mh], w=rw)

    def prep_weight(self, es, src, ncols, emit, nk=8, tag="w"):
        stg = []
        for i in range(2):
            stg.append(self.tile(es, "stg_%s%d" % (tag, i), [128, ncols], F32))
        for k in range(nk):
            t, r = stg[k % 2]
            self.sp.dma(t[:], src[k * 128:(k + 1) * 128, :], r, w=[r])
            emit(k, t, r)

    def phase_a1(self):
        nc, S, nb = self.nc, self.S, self.nb
        pe, act, dve, pool, sp = self.pe, self.act, self.dve, self.pool, self.sp
        with ExitStack() as es:
            xa = [self.tile(es, "xa%d" % i, [128, D], F32) for i in range(3)]
            junk, R_junk = self.tile(es, "junkA", [128, D], BF16)
            hb = [self.tile(es, "hb%d" % i, [128, D], BF16) for i in range(2)]
            h1 = [self.tile(es, "h1_%d" % i, [128, 8, 128], BF16) for i in range(2)]
            ss, _ = self.tile(es, "ssA", [128, 4], F32)
            rs, _ = self.tile(es, "rsA", [128, 4], F32)
            R_ssl = [Res() for _ in range(4)]
            R_rsl = [Res() for _ in range(4)]
            wdt, R_wdt = self.tile(es, "wdt3", [128, 8, 96], BF16)
            wst, R_wst = self.tile(es, "wdtst", [128, 8, 16], F32)
            sp.dma(wst[:], self.w_in[:, 2304:2320].rearrange("(k p) c -> p k c", p=128), R_wst, w=[R_wst])
            dve.op(lambda e: e.memset(wdt[:], 0.0), w=[R_wdt])
            g1b = self.c_("g1").unsqueeze(2).to_broadcast([128, 8, 16])
            for grp in range(3):
                dve.op(lambda e: e.tensor_tensor(out=wdt[:, :, grp * 32:grp * 32 + 16], in0=wst[:], in1=g1b,
                                                 op=ALU.mult), r=[R_wst, self.R_cst], w=[R_wdt])
            sp.dma(self.h1T[:, :, 0:2], self.zero[:, 0:16].rearrange("p (k c) -> p k c", k=8), self.R_zero,
                   r=[self.R_zero], w=[self.R_h1pad])
            sp.dma(self.h1T[:, :, S + 2:S + 4], self.zero[:, 0:16].rearrange("p (k c) -> p k c", k=8), self.R_zero,
                   r=[self.R_zero], w=[self.R_h1pad])
            identb = self.cbf[:, 0:128]
            tbank = (3, 5)
            dbank = (4, 6)

            def s0(b):
                xt, R_x = xa[b % 3]
                sp.dma(xt[:], self.x[b * 128:(b + 1) * 128, :], R_x, w=[R_x])

            def s1(b):
                xt, R_x = xa[b % 3]
                q = b % 4
                act.op(lambda e: e.activation(out=junk[:], in_=xt[:], func=AF.Square, accum_out=ss[:, q:q + 1]),
                       r=[R_x], w=[R_junk, R_ssl[q]])
                act.op(lambda e: e.activation(out=rs[:, q:q + 1], in_=ss[:, q:q + 1], func=AF.Ln, scale=1.0 / D,
                                              bias=self.epsc[:, 0:1]), r=[R_ssl[q], self.R_epsc], w=[R_rsl[q]])
                act.op(lambda e: e.activation(out=rs[:, q:q + 1], in_=rs[:, q:q + 1], func=AF.Exp, scale=-0.5),
                       r=[], w=[R_rsl[q]])

            def s2(b):
                xt, R_x = xa[b % 3]
                q = b % 4
                hbt, R_hb = hb[b % 2]
                dve.op(lambda e: e.tensor_scalar(out=hbt[:], in0=xt[:], scalar1=rs[:, q:q + 1], scalar2=None,
                                                 op0=ALU.mult), r=[R_x, R_rsl[q]], w=[R_hb])

            def s3(b):
                hbt, R_hb = hb[b % 2]
                bi = tbank[b % 2]
                tb = self.bank[bi][:].bitcast(BF16)
                pe.mm_group([(lambda e, k=k: e.transpose(out=tb[:, k * 128:(k + 1) * 128],
                                                         in_=hbt[:, k * 128:(k + 1) * 128], identity=identb))
                             for k in range(8)], r=[R_hb, self.R_cbf], w=[self.R_bank[bi]])

            def s4(b):
                h1t, R_h1 = h1[b % 2]
                bi = tbank[b % 2]
                tb = self.bank[bi][:].bitcast(BF16)
                dve.op(lambda e: e.tensor_copy(out=h1t[:].rearrange("p k t -> p (k t)"), in_=tb[:, :]),
                       r=[self.R_bank[bi]], w=[R_h1])
                sp.dma(self.h1T[:, :, 2 + b * 128:2 + (b + 1) * 128], h1t[:], R_h1, r=[R_h1], w=[self.R_h1T[b]])

            def s5(b):
                h1t, R_h1 = h1[b % 2]
                bi = dbank[b % 2]
                pe.mm_group([(lambda e, k=k: e.matmul(self.bank[bi][0:96, 0:128], lhsT=wdt[:, k, :], rhs=h1t[:, k, :],
                                                      start=(k == 0), stop=(k == 7))) for k in range(8)],
                            r=[R_wdt, R_h1], w=[self.R_bank[bi]])

            def s6(b):
                bi = dbank[b % 2]
                act.op(lambda e: e.activation(out=self.X[0][0:96, b * 128:(b + 1) * 128], in_=self.bank[bi][0:96, 0:128],
                                              func=AF.Exp, bias=self.c_("dtb")[0:96, :]),
                       r=[self.R_bank[bi], self.R_cst], w=[self.R_X[0]])

            stages = [s0, s1, s2, s3, s4, s5, s6]
            for t in range(nb + len(stages) - 1):
                for si in range(len(stages) - 1, -1, -1):
                    b = t - si
                    if 0 <= b < nb:
                        stages[si](b)

    def phase_d(self):
        nc, S, nb = self.nc, self.S, self.nb
        pe, act, dve, pool, sp = self.pe, self.act, self.dve, self.pool, self.sp
        X, RX = self.X, self.R_X
        P = slice(0, 96)
        sgn = self.c_("sel", 0, 1)[P]
        bsel = self.c_("sel", 1, 2)[P]
        fsel = self.c_("sel", 2, 3)[P]
        with ExitStack() as es:
            negA, R_negA = self.tile(es, "negA", [128, 1], F32)
            CSE, R_CSE = self.tile(es, "CSE", [128, nb], F32)
            c1, R_c1 = self.tile(es, "c1", [128, nb], F32)
            c2, R_c2 = self.tile(es, "c2", [128, nb], F32)
            DEC, R_DEC = self.tile(es, "DEC", [128, nb], F32)
            Dg, R_Dg = self.tile(es, "Dg", [16, nb, 16], F32)
            ones, R_ones = self.tile(es, "onesD", [16, 128], F32)
            act.op(lambda e: e.activation(out=X[0][P], in_=X[0][P], func=AF.Ln, bias=1.0), r=[], w=[RX[0]])
            act.op(lambda e: e.activation(out=X[1][P], in_=X[0][P], func=AF.Ln), r=[RX[0]], w=[RX[1]])
            act.op(lambda e: e.activation(out=negA[P], in_=self.c_("alog")[P], func=AF.Exp), r=[self.R_cst], w=[R_negA])
            dve.op(lambda e: e.tensor_scalar(out=negA[P], in0=negA[P], scalar1=-1.0, scalar2=None, op0=ALU.mult),
                   r=[], w=[R_negA])
            dve.op(lambda e: e.tensor_scalar(out=X[2][P], in0=X[0][P], scalar1=negA[P], scalar2=None, op0=ALU.mult),
                   r=[RX[0], R_negA], w=[RX[2]])
            dve.op(lambda e: e.memset(self.PHI3[P], 1.0), w=[self.R_PHI3])
            dve.op(lambda e: e.memset(self.PHI3[P].rearrange("p (c j) -> p c j", j=128)[:, :, 0:1], 0.0),
                   w=[self.R_PHI3])
            dve.op(lambda e: e.tensor_tensor_scan(out=X[3][P], data0=self.PHI3[P], data1=X[2][P], initial=0.0,
                                                  op0=ALU.mult, op1=ALU.add), r=[self.R_PHI3, RX[2]], w=[RX[3]])
            dve.op(lambda e: e.tensor_copy(out=CSE[P], in_=X[3][P].rearrange("p (c j) -> p c j", j=128)[:, :, 127]),
                   r=[RX[3]], w=[R_CSE])
            dve.op(lambda e: e.tensor_scalar(out=X[2][P], in0=X[2][P], scalar1=bsel, scalar2=None, op0=ALU.mult),
                   r=[self.R_cst], w=[RX[2]])
            dve.op(lambda e: e.scalar_tensor_tensor(out=X[3][P], in0=X[3][P], scalar=sgn, in1=X[2][P],
                                                    op0=ALU.mult, op1=ALU.add), r=[RX[2], self.R_cst], w=[RX[3]])
            dve.op(lambda e: e.tensor_tensor(out=X[1][P], in0=X[1][P], in1=X[3][P], op=ALU.subtract),
                   r=[RX[3]], w=[RX[1]])
            dve.op(lambda e: e.tensor_scalar(out=c1[P], in0=CSE[P], scalar1=bsel, scalar2=None, op0=ALU.mult),
                   r=[R_CSE], w=[R_c1])
            dve.op(lambda e: e.tensor_scalar(out=c2[P], in0=CSE[P], scalar1=fsel, scalar2=None, op0=ALU.mult),
                   r=[R_CSE], w=[R_c2])
            v3 = lambda t: t[P].rearrange("p (c j) -> p c j", j=128)
            bc = lambda t: t[P].unsqueeze(2).to_broadcast([96, nb, 128])
            dve.op(lambda e: e.tensor_tensor(out=v3(X[2]), in0=v3(X[3]), in1=bc(c1), op=ALU.add),
                   r=[RX[3], R_c1], w=[RX[2]])
            act.op(lambda e: e.activation(out=X[2][P], in_=X[2][P], func=AF.Exp), r=[], w=[RX[2]])
            dve.op(lambda e: e.tensor_tensor(out=v3(X[0]), in0=v3(X[1]), in1=bc(c2), op=ALU.add),
                   r=[RX[1], R_c2], w=[RX[0]])
            act.op(lambda e: e.activation(out=X[0][P], in_=X[0][P], func=AF.Exp), r=[], w=[RX[0]])
            act.op(lambda e: e.activation(out=DEC[P], in_=CSE[P], func=AF.Exp), r=[R_CSE], w=[R_DEC])
            identf = self.c_("identf")
            for g0 in range(0, nb, 16):
                gn = min(16, nb - g0)
                bk = self.bank[0][:].rearrange("p (b c) -> p b c", c=32)
                for bi in range(gn):
                    b = g0 + bi
                    pe.op(lambda e: e.transpose(out=bk[:, bi, 0:16], in_=X[2][0:16, b * 128:(b + 1) * 128],
                                                identity=identf[0:16, 0:16]), r=[RX[2], self.R_cst], w=[self.R_bank[0]])
                    pe.op(lambda e: e.transpose(out=bk[:, bi, 16:32], in_=X[0][0:16, b * 128:(b + 1) * 128],
                                                identity=identf[0:16, 0:16]), r=[RX[0], self.R_cst], w=[self.R_bank[0]])
                dve.op(lambda e: e.tensor_copy(out=self.TM[:, g0:g0 + gn, :], in_=bk[:, 0:gn, :]),
                       r=[self.R_bank[0]], w=[self.R_TM])
            dve.op(lambda e: e.memset(ones[:], 1.0), w=[R_ones])
            dve.op(lambda e: e.tensor_tensor(out=Dg[:], in0=DEC[0:16].unsqueeze(2).to_broadcast([16, nb, 16]),
                                             in1=identf[0:16, 0:16].unsqueeze(1).to_broadcast([16, nb, 16]),
                                             op=ALU.mult), r=[R_DEC, self.R_cst], w=[R_Dg])
            for c0 in range(0, nb, 32):
                cn = min(32, nb - c0)
                pe.op(lambda e: e.matmul(self.bank[1][:, 0:cn * 16], lhsT=ones[:],
                                         rhs=Dg[:, c0:c0 + cn, :].rearrange("p b h -> p (b h)"), start=True, stop=True),
                      r=[R_ones, R_Dg], w=[self.R_bank[1]])
                dve.op(lambda e: e.tensor_copy(out=self.DECB[:, c0:c0 + cn, :].rearrange("p b h -> p (b h)"),
                                               in_=self.bank[1][:, 0:cn * 16]), r=[self.R_bank[1]], w=[self.R_DECB])
            if self.dbg:
                sp.dma(self.dbgD[:, 0:nb * 32], self.TM[:].rearrange("p b c -> p (b c)"), self.R_TM, r=[self.R_TM])
                sp.dma(self.dbgD[:, nb * 32:nb * 48], self.DECB[:].rearrange("p b c -> p (b c)"), self.R_DECB,
                       r=[self.R_DECB])
            for src, R_src, scr, R_scr, dst, R_dst in ((X[3], RX[3], X[2], RX[2], self.PHI3, self.R_PHI3),
                                                       (X[1], RX[1], X[0], RX[0], self.PSI3, self.R_PSI3)):
                dve.op(lambda e: e.tensor_copy(out=dst[0:96], in_=src[0:96]), r=[R_src], w=[R_dst])
                for (a, b) in ((32, 64), (64, 96)):
                    dve.op(lambda e: e.tensor_tensor(out=scr[a:b], in0=src[a:b], in1=dst[a:b], op=ALU.subtract),
                           r=[R_src, R_dst], w=[R_scr])
                for (a, b) in ((32, 64), (64, 96)):
                    dve.op(lambda e: e.tensor_copy(out=dst[a:b], in_=scr[a:b]), r=[R_scr], w=[R_dst])
                dve.op(lambda e: e.tensor_tensor(out=scr[64:96], in0=scr[64:96], in1=dst[64:96], op=ALU.subtract),
                       r=[R_dst], w=[R_scr])
                dve.op(lambda e: e.tensor_copy(out=dst[64:96], in_=scr[64:96]), r=[R_scr], w=[R_dst])

    def tok_transposes(self, xbt, R_xb, tokt, R_tok, bankid=3):
        tb = self.bank[bankid][:].bitcast(BF16)
        identb = self.cbf[:, 0:128]
        for j in range(6):
            self.pe.op(lambda e: e.transpose(out=tb[:, j * 128:(j + 1) * 128], in_=xbt[:, j, :], identity=identb),
                       r=[R_xb, self.R_cbf], w=[self.R_bank[bankid]])
        self.dve.op(lambda e: e.tensor_copy(out=tokt[:], in_=tb[:, 0:768]), r=[self.R_bank[bankid]], w=[R_tok])

    def state_update(self, tokt, R_tok, xdec, R_xdec, H, R_H, b, d, bankid):
        dve, pe, pool = self.dve, self.pe, self.pool
        sd = self.TM[:, b, 16 + 8 * d:24 + 8 * d].unsqueeze(2).to_broadcast([128, 8, 64])
        dve.op(lambda e: e.tensor_tensor(out=xdec[:].rearrange("p (h q) -> p h q", q=64),
                                         in0=tokt[:, 0:512].rearrange("p (h q) -> p h q", q=64), in1=sd, op=ALU.mult),
               r=[R_tok, self.R_TM], w=[R_xdec])
        for g in range(2):
            pe.op(lambda e: e.matmul(self.bank[bankid][:, g * 256:(g + 1) * 256],
                                     lhsT=tokt[:, 512 + g * 128:512 + (g + 1) * 128],
                                     rhs=xdec[:, g * 256:(g + 1) * 256], start=(g == 0), stop=(g == 1)),
                  r=[R_tok, R_xdec], w=[self.R_bank[bankid]])
        dec = self.DECB[:, b, 8 * d:8 * d + 8].unsqueeze(2).to_broadcast([128, 8, 64])
        dve.op(lambda e: e.tensor_tensor(out=H[:].rearrange("p (h q) -> p h q", q=64),
                                         in0=H[:].rearrange("p (h q) -> p h q", q=64), in1=dec, op=ALU.mult),
               r=[self.R_DECB], w=[R_H])
        dve.op(lambda e: e.tensor_tensor(out=H[:], in0=H[:], in1=self.bank[bankid][:, :], op=ALU.add),
               r=[self.R_bank[bankid]], w=[R_H])

    def conv_jobs(self):
        jobs = []
        for k in range(8):
            rows = slice(k * 128, (k + 1) * 128)
            jobs.append((self.w_in[rows, 0:512], self.wq_s[:, k, 0:512], "qperm", ("g1", k), 512))
            jobs.append((self.w_in[rows, 512:768], self.wq_s[:, k, 512:768], "scale", ("g1", k), 256))
            jobs.append((self.w_in[rows, 768:1280], self.wq_s[:, k, 768:1280], "scale", ("g1h", k), 512))
            jobs.append((self.w_out[rows, :], self.wo_s[:, k, :], "scale", ("ga", k) if k < 4 else ("gs", k - 4), 1024))
        for k in range(8):
            rows = slice(k * 128, (k + 1) * 128)
            for c0 in range(0, 2 * DFF, 1408):
                jobs.append((self.w_up[rows, c0:c0 + 1408], self.wu_s[:, k, c0:c0 + 1408], "scale", ("g2", k), 1408))
        for k in range(DFF // 128):
            jobs.append((self.w_down[k * 128:(k + 1) * 128, :], self.wd_s[:, k, :], "copy", None, 1024))
        return jobs

    def phase_ab(self):
        nc, S, nb = self.nc, self.S, self.nb
        pe, act, dve, pool, sp = self.pe, self.act, self.dve, self.pool, self.sp
        RB, bank = self.R_bank, self.bank
        tiles = [(b0, min(2, nb - b0)) for b0 in range(0, nb, 2)]
        nt = len(tiles)
        with ExitStack() as es:
            wx, R_wx = self.tile(es, "wxbc", [128, 8, 1024], BF16)
            g1h, R_g1h = self.tile(es, "g1h", [128, 8], F32)
            dve.op(lambda e: e.tensor_scalar(out=g1h[:], in0=self.c_("g1"), scalar1=0.5, scalar2=None, op0=ALU.mult),
                   r=[self.R_cst], w=[R_g1h])
            wst = [self.tile(es, "wxst%d" % i, [128, 1024], F32) for i in range(2)]
            for k in range(8):
                t, r = wst[k % 2]
                sp.dma(t[:], self.w_in[k * 128:(k + 1) * 128, 1280:2304], r, w=[r])
                if k % 2 == 0:
                    dve.op(lambda e: e.tensor_scalar(out=wx[:, k, :], in0=t[:], scalar1=self.c_("g1", k, k + 1),
                                                     scalar2=None, op0=ALU.mult), r=[r, self.R_cst], w=[R_wx])
                else:
                    act.op(lambda e: e.activation(out=wx[:, k, :], in_=t[:], func=AF.Copy,
                                                  scale=self.c_("g1", k, k + 1)), r=[r, self.R_cst], w=[R_wx])
            scw = self.c_("scw")
            scb = self.c_("scb")
            hw = [self.tile(es, "hw%d" % i, [128, 8, 260], BF16) for i in range(3)]
            acc = [self.tile(es, "accAB%d" % i, [128, 256], F32) for i in range(4)]
            xb2 = [self.tile(es, "xb2_%d" % i, [128, 8, 256], BF16) for i in range(2)]
            tok = [self.tile(es, "tokAB%d" % i, [128, 768], BF16) for i in range(2)]
            xdec = [self.tile(es, "xdecAB%d" % i, [128, 512], BF16) for i in range(2)]
            H, R_H = self.tile(es, "HfAB", [128, 512], F32)
            pv = [self.tile(es, "pvAB%d" % i, [128, 512], BF16) for i in range(2)]
            stg = [self.tile(es, "cvs%d" % i, [128, 1408], F32) for i in range(3)]
            cvt = [self.tile(es, "cvo%d" % i, [128, 1408], BF16) for i in range(2)]
            dve.op(lambda e: e.memset(H[:], 0.0), w=[R_H])
            identb = self.cbf[:, 0:128]
            xpb = [0, 1, 2, 3]
            trb = [4, 5]
            stb = [6, 7]
            jobs = self.conv_jobs()
            gtab = {"g1": self.c_("g1"), "g2": self.c_("g2"), "ga": self.c_("ga"), "gs": self.c_("gs"), "g1h": g1h[:]}
            gres = {"g1": self.R_cst, "g2": self.R_cst, "ga": self.R_cst, "gs": self.R_cst, "g1h": R_g1h}
            jstate = {"loaded": 0, "done": 0}

            def job_load(i):
                if i >= len(jobs):
                    return
                src, dst, kind, g, n = jobs[i]
                t, r = stg[i % 3]
                act.dma(t[:, 0:n], src, r, w=[r])

            def job_run(i):
                if i >= len(jobs):
                    return
                src, dst, kind, g, n = jobs[i]
                t, r = stg[i % 3]
                o, ro = cvt[i % 2]
                if kind == "copy":
                    act.op(lambda e: e.activation(out=o[:, 0:n], in_=t[:, 0:n], func=AF.Copy), r=[r], w=[ro])
                else:
                    gcol = gtab[g[0]][:, g[1]:g[1] + 1]
                    if kind == "qperm":
                        for hh in range(2):
                            act.op(lambda e: e.activation(
                                out=o[:, 0:512].rearrange("p (i hh d) -> p hh i d", i=4, hh=2)[:, hh],
                                in_=t[:, hh * 256:(hh + 1) * 256].rearrange("p (i d) -> p i d", i=4),
                                func=AF.Copy, scale=gcol), r=[r, gres[g[0]]], w=[ro])
                    else:
                        act.op(lambda e: e.activation(out=o[:, 0:n], in_=t[:, 0:n], func=AF.Copy, scale=gcol),
                               r=[r, gres[g[0]]], w=[ro])
                act.dma(dst, o[:, 0:n], ro, r=[ro])

            def do_jobs(cnt):
                for _ in range(cnt):
                    i = jstate["done"]
                    if i >= len(jobs):
                        return
                    while jstate["loaded"] < min(len(jobs), i + 3):
                        job_load(jstate["loaded"])
                        jstate["loaded"] += 1
                    job_run(i)
                    jstate["done"] += 1

            per_iter = -(-len(jobs) // nt)
            cc = {"n": 0}

            def load(T):
                b0, n = tiles[T]
                W = n * 128
                hwt, R_hw = hw[T % 3]
                rd = [self.R_h1pad] + [self.R_h1T[i] for i in range(b0 - 1, b0 + n + 1) if 0 <= i < nb]
                sp.dma(hwt[:, :, 0:W + 4], self.h1T[:, :, b0 * 128:b0 * 128 + W + 4], R_hw, r=rd, w=[R_hw])

            def chunk_pair(T, jp):
                b0, n = tiles[T]
                W = n * 128
                hwt, R_hw = hw[T % 3]
                xbt, R_xb = xb2[T % 2]
                info = []
                for j in (2 * jp, 2 * jp + 1):
                    c = cc["n"]
                    cc["n"] += 1
                    bi = xpb[c % 4]
                    at, R_a = acc[c % 4]
                    pe.mm_group([(lambda e, k=k: e.matmul(bank[bi][:, 0:W + 4], lhsT=wx[:, k, j * 128:(j + 1) * 128],
                                                          rhs=hwt[:, k, 0:W + 4], start=(k == 0), stop=(k == 7)))
                                 for k in range(8)], r=[R_wx, R_hw], w=[RB[bi]])
                    act.op(lambda e: e.activation(out=at[:, 0:W], in_=bank[bi][:, 0:W], func=AF.Identity,
                                                  scale=scw[:, j * 5:j * 5 + 1], bias=scb[:, j:j + 1]),
                           r=[RB[bi], self.R_cst], w=[R_a])
                    info.append((j, bi, at, R_a))
                for t in range(1, 5):
                    for (j, bi, at, R_a) in info:
                        dve.op(lambda e: e.scalar_tensor_tensor(out=at[:, 0:W], in0=bank[bi][:, t:t + W],
                                                                scalar=scw[:, j * 5 + t:j * 5 + t + 1], in1=at[:, 0:W],
                                                                op0=ALU.mult, op1=ALU.add),
                               r=[RB[bi], self.R_cst], w=[R_a])
                for (j, bi, at, R_a) in info:
                    act.op(lambda e: e.activation(out=xbt[:, j, 0:W], in_=at[:, 0:W], func=AF.Silu), r=[R_a], w=[R_xb])

            def store(T):
                b0, n = tiles[T]
                xbt, R_xb = xb2[T % 2]
                for bi_ in range(n):
                    b = b0 + bi_
                    sp.dma(self.xbcT[b].rearrange("p (k t) -> p k t", k=8), xbt[:, :, bi_ * 128:(bi_ + 1) * 128], R_xb,
                           r=[R_xb], w=[self.R_xbcT[b]])

            def tr(T):
                b0, n = tiles[T]
                xbt, R_xb = xb2[T % 2]
                for bi_ in range(n):
                    b = b0 + bi_
                    tbk = trb[bi_]
                    tb = bank[tbk][:].bitcast(BF16)
                    pe.mm_group([(lambda e, j=j: e.transpose(out=tb[:, j * 128:(j + 1) * 128],
                                                             in_=xbt[:, j, bi_ * 128:(bi_ + 1) * 128], identity=identb))
                                 for j in range(6)], r=[R_xb, self.R_cbf], w=[RB[tbk]])

            def evac(T):
                b0, n = tiles[T]
                for bi_ in range(n):
                    b = b0 + bi_
                    tbk = trb[bi_]
                    tb = bank[tbk][:].bitcast(BF16)
                    tokt, R_tok = tok[b % 2]
                    xd, R_xd = xdec[b % 2]
                    dve.op(lambda e: e.tensor_copy(out=tokt[:], in_=tb[:, 0:768]), r=[RB[tbk]], w=[R_tok])
                    sd = self.TM[:, b, 16:24].unsqueeze(2).to_broadcast([128, 8, 64])
                    dve.op(lambda e: e.tensor_tensor(out=xd[:].rearrange("p (h q) -> p h q", q=64),
                                                     in0=tokt[:, 0:512].rearrange("p (h q) -> p h q", q=64), in1=sd,
                                                     op=ALU.mult), r=[R_tok, self.R_TM], w=[R_xd])

            def states(T):
                b0, n = tiles[T]
                for bi_ in range(n):
                    b = b0 + bi_
                    tokt, R_tok = tok[b % 2]
                    xd, R_xd = xdec[b % 2]
                    sb = stb[bi_]
                    pe.mm_group([(lambda e, g=g: e.matmul(bank[sb][:, g * 256:(g + 1) * 256],
                                                          lhsT=tokt[:, 512 + g * 128:512 + (g + 1) * 128],
                                                          rhs=xd[:, g * 256:(g + 1) * 256], start=(g == 0), stop=(g == 1)))
                                 for g in range(2)], r=[R_tok, R_xd], w=[RB[sb]])

            def scan(T, bi_):
                b0, n = tiles[T]
                if bi_ >= n:
                    return
                b = b0 + bi_
                sb = stb[bi_]
                pvt, R_pv = pv[b % 2]
                act.op(lambda e: e.activation(out=pvt[:], in_=H[:], func=AF.Copy), r=[R_H], w=[R_pv])
                sp.dma(self.prevf[b], pvt[:], R_pv, r=[R_pv], w=[self.R_prevf[b]])
                dec = self.DECB[:, b, 0:8].unsqueeze(2).to_broadcast([128, 8, 64])
                dve.op(lambda e: e.tensor_tensor(out=H[:].rearrange("p (h q) -> p h q", q=64),
                                                 in0=H[:].rearrange("p (h q) -> p h q", q=64), in1=dec, op=ALU.mult),
                       r=[self.R_DECB], w=[R_H])
                dve.op(lambda e: e.tensor_tensor(out=H[:], in0=H[:], in1=bank[sb][:, :], op=ALU.add),
                       r=[RB[sb]], w=[R_H])

            load(0)
            if nt > 1:
                load(1)
            for T in range(nt + 1):
                P = T - 1
                if T + 2 < nt:
                    load(T + 2)
                if T < nt:
                    chunk_pair(T, 0)
                    chunk_pair(T, 1)
                if P >= 0:
                    tr(P)
                if T < nt:
                    chunk_pair(T, 2)
                if P >= 0:
                    evac(P)
                    states(P)
                if P >= 0:
                    scan(P, 0)
                if T < nt:
                    chunk_pair(T, 3)
                if P >= 0:
                    scan(P, 1)
                do_jobs(per_iter)
                if T < nt:
                    store(T)
            do_jobs(len(jobs))

    def phase_c(self):
        nc, S, nb = self.nc, self.S, self.nb
        pe, act, dve, pool, sp = self.pe, self.act, self.dve, self.pool, self.sp
        RB = self.R_bank
        bank = self.bank
        identb = self.cbf[:, 0:128]
        mle = self.cbf[:, _b["mle"][0]:_b["mle"][1]]
        mge = self.cbf[:, _b["mge"][0]:_b["mge"][1]]
        ind = self.cbf[:, _b["ind"][0]:_b["ind"][1]].rearrange("p (h t) -> p h t", t=128)
        bkc = [0]
        held = set()

        def ab(hold=False):
            for _ in range(8):
                i = bkc[0]
                bkc[0] = (i + 1) % 8
                if i not in held:
                    if hold:
                        held.add(i)
                    return i
            raise RuntimeError("all PSUM banks held")

        def rel(*bs):
            for i in bs:
                held.discard(i)

        with ExitStack() as es:
            wq, R_wq = self.tile(es, "wqkvz", [128, 8, 1280], BF16)
            wo, R_wo = self.tile(es, "wout", [128, 8, 1024], BF16)
            rope, R_rope = self.tile(es, "rope", [128, nb, 16], F32)
            esink, R_esink = self.tile(es, "esink", [128, 8], F32)
            Dm, R_Dm = self.tile(es, "Dm", [128, 8, 128], BF16)
            sp.dma(rope[:].rearrange("p b c -> p (b c)"), self.rope_d, R_rope, w=[R_rope])
            act.op(lambda e: e.activation(out=esink[:], in_=self.c_("sink"), func=AF.Exp), r=[self.R_cst], w=[R_esink])
            sp.dma(wq[:], self.wq_s, R_wq, w=[R_wq])
            sp.dma(wo[:], self.wo_s, R_wo, w=[R_wo])
            for h in range(8):
                dve.op(lambda e: e.tensor_scalar(out=Dm[:, h, :], in0=identb, scalar1=self.c_("dskip", h, h + 1),
                                                 scalar2=None, op0=ALU.mult), r=[self.R_cbf, self.R_cst], w=[R_Dm])
            T = lambda name, shape, dt, n: [self.tile(es, "%s%d" % (name, i), shape, dt) for i in range(n)]
            hc = T("hc", [128, 8, 128], BF16, 2)
            xb = T("xbC", [128, 8, 128], BF16, 3)
            pf = T("pfC", [128, 512], BF16, 3)
            xr = T("xr", [128, D], F32, 2)
            qk = T("qktm", [128, 640], BF16, 2)
            rtq = T("rtq", [128, 8, 8], F32, 4)
            rtk = T("rtk", [128, 2, 8], F32, 4)
            qT = T("qT", [128, 4, 128], BF16, 3)
            KP = [T("KP%d_" % i, [128, 128], BF16, 2) for i in range(4)]
            V1 = T("V1_", [128, 2, 65], BF16, 4)
            gate = T("gate", [128, 512], F32, 2)
            thz = T("thz", [128, 512], F32, 2)
            tok = T("tokC", [128, 768], BF16, 2)
            PT = T("PT", [128, 512], BF16, 6)
            den = T("den", [128, 8], F32, 2)
            att = T("att", [128, 512], F32, 2)
            junk = self.tile(es, "junkC", [128, 1024], BF16)
            ssn = self.tile(es, "ssn", [128, 16], F32)
            rsn = self.tile(es, "rsn", [128, 16], F32)
            R_ss = [Res() for _ in range(16)]
            R_rs = [Res() for _ in range(16)]
            mixed = T("mixed", [128, 1024], BF16, 3)
            R_mixa = [Res() for _ in range(3)]
            R_mixs = [Res() for _ in range(3)]
            LT = T("LT", [128, 8, 128], F32, 2)
            WT = T("WT", [128, 8, 128], BF16, 2)
            ytmp = T("ytmp", [128, 512], F32, 3)
            xdec = self.tile(es, "xdecC", [128, 512], BF16)
            Hb = self.tile(es, "HbC", [128, 512], F32)
            pvb = T("pvbC", [128, 512], BF16, 2)
            mT = T("mT", [128, 8, 128], BF16, 2)
            x1 = T("x1_", [128, D], F32, 2)
            h2 = T("h2_", [128, D], BF16, 2)
            h2T = T("h2T", [128, 8, 128], BF16, 2)
            dve.op(lambda e: e.memset(Hb[0][:], 0.0), w=[Hb[1]])
            for i in range(4):
                for h in range(2):
                    dve.op(lambda e: e.memset(KP[i][h][0][:], 0.0), w=[KP[i][h][1]])
                dve.op(lambda e: e.memset(V1[i][0][:], 1.0), w=[V1[i][1]])
            sp.dma(self.h2T[:, :, 0:1], self.zero[:, 0:8].rearrange("p (k c) -> p k c", k=8), self.R_zero,
                   r=[self.R_zero], w=[self.R_h2pad], slow=True)
            sp.dma(self.h2T[:, :, S + 1:S + 2], self.zero[:, 0:8].rearrange("p (k c) -> p k c", k=8), self.R_zero,
                   r=[self.R_zero], w=[self.R_h2pad], slow=True)
            ctx = {}
            v3 = lambda ap: ap.rearrange("p (h q) -> p h q", q=64)
            bfv = lambda bi: bank[bi][:].bitcast(BF16)

            def F_load(b):
                hct, R_hc = hc[b % 2]
                sp.dma(hct[:], self.h1T[:, :, 2 + b * 128:2 + (b + 1) * 128], R_hc, r=[self.R_h1T[b]], w=[R_hc])
                sp.dma(xb[b % 3][0][:].rearrange("p k t -> p (k t)"), self.xbcT[b], xb[b % 3][1], r=[self.R_xbcT[b]],
                       w=[xb[b % 3][1]])
                sp.dma(pf[b % 3][0][:], self.prevf[b], pf[b % 3][1], r=[self.R_prevf[b]], w=[pf[b % 3][1]])

            def F1(b):
                hct, R_hc = hc[b % 2]
                qb = [ab(True), ab(True), ab(True)]
                ctx[b] = {"qb": qb}
                for (c0, cn, bi) in ((0, 512, qb[0]), (512, 512, qb[1]), (1024, 256, qb[2])):
                    pe.mm_group([(lambda e, k=k: e.matmul(bank[bi][:, 0:cn], lhsT=hct[:, k, :], rhs=wq[:, k, c0:c0 + cn],
                                                          start=(k == 0), stop=(k == 7))) for k in range(8)],
                                r=[R_hc, R_wq], w=[RB[bi]])

            def rope_part(b, bi, nh, o0, rt):
                src = bank[bi][:, 0:nh * 64].rearrange("p (h d) -> p h d", d=64)
                dst = qk[b % 2][0][:, o0:o0 + nh * 64].rearrange("p (h d) -> p h d", d=64)
                R_qk = qk[b % 2][1]
                cb_ = rope[:, b, 0:8].unsqueeze(1).to_broadcast([128, nh, 8])
                sb_ = rope[:, b, 8:16].unsqueeze(1).to_broadcast([128, nh, 8])
                rr = [RB[bi], R_rope]
                dve.op(lambda e: e.tensor_tensor(out=rt[0][0][:], in0=src[:, :, 0:8], in1=cb_, op=ALU.mult), r=rr, w=[rt[0][1]])
                dve.op(lambda e: e.tensor_tensor(out=rt[1][0][:], in0=src[:, :, 8:16], in1=sb_, op=ALU.mult), r=rr, w=[rt[1][1]])
                dve.op(lambda e: e.tensor_tensor(out=rt[2][0][:], in0=src[:, :, 8:16], in1=cb_, op=ALU.mult), r=rr, w=[rt[2][1]])
                dve.op(lambda e: e.tensor_tensor(out=rt[3][0][:], in0=src[:, :, 0:8], in1=sb_, op=ALU.mult), r=rr, w=[rt[3][1]])
                act.op(lambda e: e.activation(out=dst[:, :, 16:64], in_=src[:, :, 16:64], func=AF.Copy), r=[RB[bi]], w=[R_qk])
                dve.op(lambda e: e.tensor_tensor(out=dst[:, :, 0:8], in0=rt[0][0][:], in1=rt[1][0][:], op=ALU.subtract),
                       r=[rt[0][1], rt[1][1]], w=[R_qk])
                dve.op(lambda e: e.tensor_tensor(out=dst[:, :, 8:16], in0=rt[2][0][:], in1=rt[3][0][:], op=ALU.add),
                       r=[rt[2][1], rt[3][1]], w=[R_qk])

            def F2(b):
                qb = ctx[b]["qb"]
                rope_part(b, qb[0], 8, 0, rtq)
                v1t, R_v1 = V1[b % 4]
                act.op(lambda e: e.activation(out=v1t[:, :, 0:64],
                                              in_=bank[qb[1]][:, 128:256].rearrange("p (h d) -> p h d", d=64), func=AF.Copy),
                       r=[RB[qb[1]]], w=[R_v1])

            def F3(b):
                qb = ctx[b]["qb"]
                rope_part(b, qb[1], 2, 512, rtk)
                zs = [bank[qb[1]][:, 256:512], bank[qb[2]][:, 0:256]]
                for zi in range(2):
                    act.op(lambda e: e.activation(out=thz[b % 2][0][:, zi * 256:(zi + 1) * 256], in_=zs[zi], func=AF.Tanh),
                           r=[RB[qb[1 + zi]]], w=[thz[b % 2][1]])
                for zi in range(2):
                    dve.op(lambda e: e.scalar_tensor_tensor(out=gate[b % 2][0][:, zi * 256:(zi + 1) * 256],
                                                            in0=thz[b % 2][0][:, zi * 256:(zi + 1) * 256], scalar=1.0,
                                                            in1=zs[zi], op0=ALU.add, op1=ALU.mult),
                           r=[thz[b % 2][1], RB[qb[1 + zi]]], w=[gate[b % 2][1]])
                rel(*qb)

            def F4(b):
                bi = ab()
                tb = bfv(bi)
                pe.mm_group([(lambda e, j=j: e.transpose(out=tb[:, j * 128:(j + 1) * 128],
                                                         in_=qk[b % 2][0][:, j * 128:(j + 1) * 128], identity=identb))
                             for j in range(5)], r=[qk[b % 2][1], self.R_cbf], w=[RB[bi]])
                dve.op(lambda e: e.tensor_copy(out=qT[b % 3][0][:].rearrange("p c t -> p (c t)"), in_=tb[:, 0:512]),
                       r=[RB[bi]], w=[qT[b % 3][1]])
                for h in range(2):
                    kpt, R_kp = KP[b % 4][h]
                    dve.op(lambda e: e.tensor_copy(out=kpt[h * 64:(h + 1) * 64, :], in_=tb[h * 64:(h + 1) * 64, 512:640]),
                           r=[RB[bi]], w=[R_kp])

            def F5(b):
                bi = ab()
                tb = bfv(bi)
                xbt, R_xb = xb[b % 3]
                pe.mm_group([(lambda e, j=j: e.transpose(out=tb[:, j * 128:(j + 1) * 128], in_=xbt[:, j, :], identity=identb))
                             for j in range(6)], r=[R_xb, self.R_cbf], w=[RB[bi]])
                dve.op(lambda e: e.tensor_copy(out=tok[b % 2][0][:], in_=tb[:, 0:768]), r=[RB[bi]], w=[tok[b % 2][1]])

            def S_exp(c, d):
                xbt, R_xb = xb[c % 3]
                cols = slice(c * 128, (c + 1) * 128)
                msk = mle if d == 0 else mge
                for g in range(2):
                    bi = ab()
                    hd0 = d * 8 + g * 4
                    fns = [(lambda e, hh=hh: e.matmul(bank[bi][:, hh * 128:(hh + 1) * 128], lhsT=ind[0:96, hd0 + hh, :],
                                                      rhs=self.PHI3[0:96, cols], start=(hh == 0), stop=False))
                           for hh in range(4)]
                    fns.append(lambda e: e.matmul(bank[bi][:, :], lhsT=self.PSI3[0:96, cols],
                                                  rhs=ind[0:96, hd0:hd0 + 4, :].rearrange("p h t -> p (h t)"),
                                                  start=False, stop=False))
                    fns.append(lambda e: e.matmul(bank[bi][:, :], lhsT=identb, rhs=msk, start=False, stop=True))
                    pe.mm_group(fns, r=[self.R_cbf, self.R_PHI3, self.R_PSI3], w=[RB[bi]])
                    act.op(lambda e: e.activation(out=LT[d][0][:, g * 4:(g + 1) * 4, :].rearrange("p h t -> p (h t)"),
                                                  in_=bank[bi][:, :], func=AF.Exp), r=[RB[bi]], w=[LT[d][1]])

            def S_wt(c, d):
                cbk = ctx[c]["cbk"]
                cbb = bank[cbk][:, 0:256].rearrange("p (g t) -> p g t", g=2).unsqueeze(2).to_broadcast([128, 2, 4, 128])
                dve.op(lambda e: e.tensor_tensor(out=WT[d][0][:].rearrange("p (g h) t -> p g h t", g=2),
                                                 in0=LT[d][0][:].rearrange("p (g h) t -> p g h t", g=2), in1=cbb,
                                                 op=ALU.mult), r=[LT[d][1], RB[cbk]], w=[WT[d][1]])

            def S1(c):
                xbt, R_xb = xb[c % 3]
                cbk = ab(True)
                ctx[c]["cbk"] = cbk
                pe.mm_group([(lambda e, g=g: e.matmul(bank[cbk][:, g * 128:(g + 1) * 128], lhsT=xbt[:, 4 + g, :],
                                                      rhs=xbt[:, 6 + g, :], start=(g == 0), stop=(g == 1)))
                             for g in range(2)], r=[R_xb], w=[RB[cbk]])
                S_exp(c, 0)

            def S2(c):
                S_wt(c, 0)
                S_exp(c, 1)

            def S3(c):
                S_wt(c, 1)
                rel(ctx[c]["cbk"])
                xbt, R_xb = xb[c % 3]
                act.op(lambda e: e.activation(out=pvb[c % 2][0][:], in_=Hb[0][:], func=AF.Copy), r=[Hb[1]], w=[pvb[c % 2][1]])
                yo = [ab(True), ab(True)]
                ctx[c]["yo"] = yo
                for d in range(2):
                    pv_t, R_pvx = (pf[c % 3] if d == 0 else pvb[c % 2])
                    pe.mm_group([(lambda e, g=g: e.matmul(bank[yo[d]][:, g * 256:(g + 1) * 256], lhsT=xbt[:, 6 + g, :],
                                                          rhs=pv_t[:, g * 256:(g + 1) * 256], start=(g == 0), stop=(g == 1)))
                                 for g in range(2)], r=[R_xb, R_pvx], w=[RB[yo[d]]])

            def S4(c):
                tokt, R_tok = tok[c % 2]
                yd = ab(True)
                ctx[c]["yd"] = yd
                fns = []
                for h in range(8):
                    for (i, lt) in enumerate((WT[0][0][:, h, :], WT[1][0][:, h, :], Dm[:, h, :])):
                        fns.append(lambda e, h=h, i=i, lt=lt: e.matmul(bank[yd][:, h * 64:(h + 1) * 64], lhsT=lt,
                                                                      rhs=tokt[:, h * 64:(h + 1) * 64],
                                                                      start=(h == 0 and i == 0), stop=(h == 7 and i == 2)))
                pe.mm_group(fns, r=[WT[0][1], WT[1][1], R_Dm, R_tok], w=[RB[yd]])
                yo = ctx[c]["yo"]
                eyf = self.TM[:, c, 0:8].unsqueeze(2).to_broadcast([128, 8, 64])
                eyb = self.TM[:, c, 8:16].unsqueeze(2).to_broadcast([128, 8, 64])
                y1, y2, y3 = ytmp
                dve.op(lambda e: e.tensor_tensor(out=v3(y1[0][:]), in0=v3(bank[yo[0]][:, :]), in1=eyf, op=ALU.mult),
                       r=[RB[yo[0]], self.R_TM], w=[y1[1]])
                dve.op(lambda e: e.tensor_tensor(out=v3(y2[0][:]), in0=v3(bank[yo[1]][:, :]), in1=eyb, op=ALU.mult),
                       r=[RB[yo[1]], self.R_TM], w=[y2[1]])
                rel(*yo)

            def S5(c):
                yd = ctx[c]["yd"]
                y1, y2, y3 = ytmp
                dve.op(lambda e: e.tensor_tensor(out=y3[0][:], in0=bank[yd][:, :], in1=y1[0][:], op=ALU.add),
                       r=[RB[yd], y1[1]], w=[y3[1]])
                rel(yd)
                dve.op(lambda e: e.tensor_tensor(out=y3[0][:], in0=y3[0][:], in1=y2[0][:], op=ALU.add),
                       r=[y2[1]], w=[y3[1]])
                dve.op(lambda e: e.tensor_tensor(out=y3[0][:], in0=y3[0][:], in1=gate[c % 2][0][:], op=ALU.mult),
                       r=[gate[c % 2][1]], w=[y3[1]])

            def S6(c):
                y3 = ytmp[2]
                q = (c % 4) * 4
                for g in range(2):
                    act.op(lambda e: e.activation(out=junk[0][:, g * 256:(g + 1) * 256], in_=y3[0][:, g * 256:(g + 1) * 256],
                                                  func=AF.Square, accum_out=ssn[0][:, q + 1 + g:q + 2 + g]),
                           r=[y3[1]], w=[junk[1], R_ss[q + 1 + g]])
                self.rstd(ssn[0][:, q + 1:q + 3], 256, rsn[0][:, q + 1:q + 3], [R_ss[q + 1], R_ss[q + 2]],
                          [R_rs[q + 1], R_rs[q + 2]])
                self.state_update(tok[c % 2][0], tok[c % 2][1], xdec[0], xdec[1], Hb[0], Hb[1], c, 1, bankid=ab())

            def S7(c):
                y3 = ytmp[2]
                q = (c % 4) * 4
                for g in range(2):
                    act.op(lambda e: e.activation(out=mixed[c % 3][0][:, 512 + g * 256:512 + (g + 1) * 256],
                                                  in_=y3[0][:, g * 256:(g + 1) * 256], func=AF.Copy,
                                                  scale=rsn[0][:, q + 1 + g:q + 2 + g]), r=[y3[1], R_rs[q + 1 + g]],
                           w=[R_mixs[c % 3]])

            def A_scores(c, half):
                kbs = [kb for kb in (c - 1, c, c + 1) if 0 <= kb < nb]
                for ki, kb in enumerate(kbs):
                    bi = ab()
                    last = (kb == c)
                    fns = [lambda e: e.matmul(bank[bi][:, :], lhsT=KP[kb % 4][half][0][:, :],
                                              rhs=qT[c % 3][0][:].rearrange("p c t -> p (c t)"), start=True, stop=last)]
                    if not last:
                        m = mge if kb < c else mle
                        fns.append(lambda e: e.matmul(bank[bi][:, :], lhsT=identb, rhs=m, start=False, stop=True))
                    pe.mm_group(fns, r=[KP[kb % 4][half][1], qT[c % 3][1], self.R_cbf], w=[RB[bi]])
                    pt = PT[half * 3 + ki]
                    act.op(lambda e: e.activation(out=pt[0][:], in_=bank[bi][:, :], func=AF.Exp, scale=0.125),
                           r=[RB[bi]], w=[pt[1]])

            def A_pv(c, half):
                kbs = [kb for kb in (c - 1, c, c + 1) if 0 <= kb < nb]
                pb = ab()
                po = bank[pb][:, 0:260].rearrange("p (h d) -> p h d", d=65)
                fns = []
                rd = []
                for i in range(4):
                    for ki, kb in enumerate(kbs):
                        pt = PT[half * 3 + ki]
                        fns.append(lambda e, i=i, ki=ki, kb=kb, pt=pt: e.matmul(
                            po[:, i, :], lhsT=pt[0][:, i * 128:(i + 1) * 128], rhs=V1[kb % 4][0][:, half, :],
                            start=(ki == 0), stop=(ki == len(kbs) - 1)))
                        rd += [pt[1], V1[kb % 4][1]]
                pe.mm_group(fns, r=rd, w=[RB[pb]])
                hs = slice(half * 4, half * 4 + 4)
                dn = den[c % 2]
                at_ = att[c % 2]
                dve.op(lambda e: e.tensor_tensor(out=dn[0][:, hs], in0=po[:, :, 64], in1=esink[:, hs], op=ALU.add),
                       r=[RB[pb], R_esink], w=[dn[1]])
                dve.op(lambda e: e.reciprocal(out=dn[0][:, hs], in_=dn[0][:, hs]), r=[], w=[dn[1]])
                dve.op(lambda e: e.tensor_tensor(
                    out=at_[0][:, half * 256:(half + 1) * 256].rearrange("p (h d) -> p h d", d=64),
                    in0=po[:, :, 0:64], in1=dn[0][:, hs].unsqueeze(2).to_broadcast([128, 4, 64]), op=ALU.mult),
                    r=[RB[pb], dn[1]], w=[at_[1]])

            def A5(c):
                q = (c % 4) * 4
                at_ = att[c % 2]
                act.op(lambda e: e.activation(out=junk[0][:, 0:512], in_=at_[0][:], func=AF.Square, accum_out=ssn[0][:, q:q + 1]),
                       r=[at_[1]], w=[junk[1], R_ss[q]])
                self.rstd(ssn[0][:, q:q + 1], 512, rsn[0][:, q:q + 1], [R_ss[q]], [R_rs[q]])

            def A6(c):
                q = (c % 4) * 4
                at_ = att[c % 2]
                act.op(lambda e: e.activation(out=mixed[c % 3][0][:, 0:512], in_=at_[0][:], func=AF.Copy,
                                              scale=rsn[0][:, q:q + 1]), r=[at_[1], R_rs[q]], w=[R_mixa[c % 3]])

            def O_load(c):
                sp.dma(xr[c % 2][0][:], self.x[c * 128:(c + 1) * 128, :], xr[c % 2][1], w=[xr[c % 2][1]])

            def O1(c):
                bi = ab()
                tb = bfv(bi)
                mx = mixed[c % 3][0]
                pe.mm_group([(lambda e, k=k: e.transpose(out=tb[:, k * 128:(k + 1) * 128], in_=mx[:, k * 128:(k + 1) * 128],
                                                         identity=identb)) for k in range(8)],
                            r=[R_mixa[c % 3], R_mixs[c % 3], self.R_cbf], w=[RB[bi]])
                dve.op(lambda e: e.tensor_copy(out=mT[c % 2][0][:].rearrange("p k t -> p (k t)"), in_=tb[:, :]),
                       r=[RB[bi]], w=[mT[c % 2][1]])

            def O2(c):
                ob = [ab(), ab()]
                for cg in range(2):
                    pe.mm_group([(lambda e, k=k: e.matmul(bank[ob[cg]][:, :], lhsT=mT[c % 2][0][:, k, :],
                                                          rhs=wo[:, k, cg * 512:(cg + 1) * 512], start=(k == 0), stop=(k == 7)))
                                 for k in range(8)], r=[mT[c % 2][1], R_wo], w=[RB[ob[cg]]])
                x1t, R_x1 = x1[c % 2]
                for cg in range(2):
                    dve.op(lambda e: e.tensor_tensor(out=x1t[:, cg * 512:(cg + 1) * 512], in0=bank[ob[cg]][:, :],
                                                     in1=xr[c % 2][0][:, cg * 512:(cg + 1) * 512], op=ALU.add),
                           r=[RB[ob[cg]], xr[c % 2][1]], w=[R_x1])
                sp.dma(self.x1s[c * 128:(c + 1) * 128, :], x1t[:], R_x1, r=[R_x1], w=[self.R_x1[c]])
                q = (c % 4) * 4 + 3
                act.op(lambda e: e.activation(out=junk[0][:], in_=x1t[:], func=AF.Square, accum_out=ssn[0][:, q:q + 1]),
                       r=[R_x1], w=[junk[1], R_ss[q]])
                self.rstd(ssn[0][:, q:q + 1], D, rsn[0][:, q:q + 1], [R_ss[q]], [R_rs[q]])

            def O3(c):
                x1t, R_x1 = x1[c % 2]
                q = (c % 4) * 4 + 3
                act.op(lambda e: e.activation(out=h2[c % 2][0][:], in_=x1t[:], func=AF.Copy, scale=rsn[0][:, q:q + 1]),
                       r=[R_x1, R_rs[q]], w=[h2[c % 2][1]])
                bi = ab()
                tb = bfv(bi)
                pe.mm_group([(lambda e, k=k: e.transpose(out=tb[:, k * 128:(k + 1) * 128],
                                                         in_=h2[c % 2][0][:, k * 128:(k + 1) * 128], identity=identb))
                             for k in range(8)], r=[h2[c % 2][1], self.R_cbf], w=[RB[bi]])
                dve.op(lambda e: e.tensor_copy(out=h2T[c % 2][0][:].rearrange("p k t -> p (k t)"), in_=tb[:, :]),
                       r=[RB[bi]], w=[h2T[c % 2][1]])
                sp.dma(self.h2T[:, :, 1 + c * 128:1 + (c + 1) * 128], h2T[c % 2][0][:], h2T[c % 2][1], r=[h2T[c % 2][1]],
                       w=[self.R_h2T[c]])

            order = list(range(nb - 1, -1, -1))
            chains = {
                "F": [F1, F2, F3, F4, F5],
                "S": [S1, S2, S3, S4, S5, S6, S7],
                "A": [lambda c: A_scores(c, 0), lambda c: A_pv(c, 0), lambda c: A_scores(c, 1), lambda c: A_pv(c, 1), A5, A6],
                "O": [O1, O2, O3],
            }
            lag = {"F": 0, "S": 1, "A": 2, "O": 3}
            F_load(order[0])
            for k in range(nb + 3):
                if k + 1 < nb:
                    F_load(order[k + 1])
                if 0 <= k - 3 < nb:
                    O_load(order[k - 3])
                nsteps = max(len(v) for v in chains.values())
                for i in range(nsteps):
                    for name in ("O", "A", "S", "F"):
                        p = k - lag[name]
                        if 0 <= p < nb and i < len(chains[name]):
                            chains[name][i](order[p])

    def phase_f(self):
        nc, S, nb = self.nc, self.S, self.nb
        pe, act, dve, pool, sp = self.pe, self.act, self.dve, self.pool, self.sp
        RB = self.R_bank
        bank = self.bank
        NJ = DFF // 128
        with ExitStack() as es:
            wu, _ = self.tile(es, "wup", [128, 8, 2 * DFF], BF16)
            wd, _ = self.tile(es, "wdown", [128, NJ, D], BF16)
            gft, R_gf = self.tile(es, "gft", [128, D], F32)
            R_wu = [Res() for _ in range(NJ // 2)]
            R_wd = [Res() for _ in range(2)]
            for r in R_wu + R_wd:
                self.alltok_res.append(r)
            for p in range(NJ // 2):
                for base in (0, DFF):
                    c0 = base + p * 256
                    sp.dma(wu[:, :, c0:c0 + 256], self.wu_s[:, :, c0:c0 + 256], R_wu[p], w=[R_wu[p]])
            for h in range(2):
                sp.dma(wd[:, h * 11:(h + 1) * 11, :], self.wd_s[:, h * 11:(h + 1) * 11, :], R_wd[h], w=[R_wd[h]])
            sp.dma(gft[:], self.gf_d, R_gf, w=[R_gf])
            gf = gft[:]
            fcw = self.c_("fcw")
            fcb = self.c_("fcb")
            NW = FT + 2
            hw = [self.tile(es, "hwF%d" % i, [128, 8, NW], BF16) for i in range(2)]
            gT = self.tile(es, "gT", [128, NJ, FT], BF16)
            ag = [self.tile(es, "agF%d" % i, [128, FT], F32) for i in range(2)]
            av = [self.tile(es, "avF%d" % i, [128, FT], F32) for i in range(2)]
            tg = [self.tile(es, "tgF%d" % i, [128, FT], F32) for i in range(2)]
            x1 = [self.tile(es, "x1F%d" % i, [128, D], F32) for i in range(2)]
            x2 = [self.tile(es, "x2F%d" % i, [128, D], F32) for i in range(2)]
            junk = self.tile(es, "junkF", [128, D], BF16)
            ssf = self.tile(es, "ssf", [128, 2], F32)
            rsf = self.tile(es, "rsf", [128, 2], F32)
            R_ssf = [Res(), Res()]
            R_rsf = [Res(), Res()]
            tiles = []
            t0 = 0
            while t0 < S:
                tiles.append((t0, min(FT, S - t0)))
                t0 += FT
            blkctr = [0]

            def load(ti):
                t0, nt = tiles[ti]
                hwt, R_hw = hw[ti % 2]
                n = nt + 2
                b0, b1 = t0 // 128, min(nb - 1, (t0 + nt) // 128)
                rd = [self.R_h2pad] + [self.R_h2T[i] for i in range(max(0, b0 - 1), min(nb, b1 + 2))]
                sp.dma(hwt[:, :, 0:n], self.h2T[:, :, t0:t0 + n], R_hw, r=rd, w=[R_hw])

            def up(ti):
                t0, nt = tiles[ti]
                hwt, R_hw = hw[ti % 2]
                n = nt + 2
                for j in range(NJ):
                    p = j % 2
                    bg, bv = 2 * p, 2 * p + 1
                    for (bi, c0) in ((bg, j * 128), (bv, DFF + j * 128)):
                        pe.mm_group([(lambda e, k=k: e.matmul(bank[bi][:, 0:n], lhsT=wu[:, k, c0:c0 + 128],
                                                              rhs=hwt[:, k, 0:n], start=(k == 0), stop=(k == 7)))
                                     for k in range(8)], r=[R_wu[j // 2], R_hw], w=[RB[bi]])
                    agt, R_ag = ag[p]
                    avt, R_av = av[p]
                    tgt, R_tg = tg[p]
                    for (bi, acc, R_a, wc, bc_) in ((bg, agt, R_ag, j * 3, j), (bv, avt, R_av, 66 + j * 3, 22 + j)):
                        act.op(lambda e: e.activation(out=acc[:, 0:nt], in_=bank[bi][:, 0:nt], func=AF.Identity,
                                                      scale=fcw[:, wc:wc + 1], bias=fcb[:, bc_:bc_ + 1]),
                               r=[RB[bi], self.R_cst], w=[R_a])
                    for t in (1, 2):
                        for (bi, acc, R_a, wc) in ((bg, agt, R_ag, j * 3), (bv, avt, R_av, 66 + j * 3)):
                            dve.op(lambda e: e.scalar_tensor_tensor(out=acc[:, 0:nt], in0=bank[bi][:, t:t + nt],
                                                                    scalar=fcw[:, wc + t:wc + t + 1], in1=acc[:, 0:nt],
                                                                    op0=ALU.mult, op1=ALU.add),
                                   r=[RB[bi], self.R_cst], w=[R_a])
                    act.op(lambda e: e.activation(out=tgt[:, 0:nt], in_=agt[:, 0:nt], func=AF.Silu), r=[R_ag], w=[R_tg])
                    dve.op(lambda e: e.tensor_tensor(out=gT[0][:, j, 0:nt], in0=tgt[:, 0:nt], in1=avt[:, 0:nt], op=ALU.mult),
                           r=[R_tg, R_av], w=[gT[1]])

            def down(ti):
                t0, nt = tiles[ti]
                o = 0
                while o < nt:
                    m = min(128, nt - o)
                    q = blkctr[0] % 2
                    blkctr[0] += 1
                    r0 = t0 + o
                    x1t, R_x1 = x1[q]
                    x2t, R_x2 = x2[q]
                    bl0, bl1 = r0 // 128, (r0 + m - 1) // 128
                    sp.dma(x1t[0:m, :], self.x1s[r0:r0 + m, :], R_x1, r=[self.R_x1[i] for i in range(bl0, bl1 + 1)], w=[R_x1])
                    for cg in range(2):
                        bi = 4 + 2 * q + cg
                        pe.mm_group([(lambda e, j=j: e.matmul(bank[bi][0:m, :], lhsT=gT[0][:, j, o:o + m],
                                                              rhs=wd[:, j, cg * 512:(cg + 1) * 512], start=(j == 0),
                                                              stop=(j == NJ - 1))) for j in range(NJ)],
                                    r=[gT[1]] + R_wd, w=[RB[bi]])
                    for cg in range(2):
                        bi = 4 + 2 * q + cg
                        dve.op(lambda e: e.tensor_tensor(out=x2t[0:m, cg * 512:(cg + 1) * 512], in0=bank[bi][0:m, :],
                                                         in1=x1t[0:m, cg * 512:(cg + 1) * 512], op=ALU.add),
                               r=[RB[bi], R_x1], w=[R_x2])
                    act.op(lambda e: e.activation(out=junk[0][0:m, :], in_=x2t[0:m, :], func=AF.Square,
                                                  accum_out=ssf[0][0:m, q:q + 1]), r=[R_x2], w=[junk[1], R_ssf[q]])
                    self.rstd(ssf[0][0:m, q:q + 1], D, rsf[0][0:m, q:q + 1], [R_ssf[q]], [R_rsf[q]])
                    dve.op(lambda e: e.scalar_tensor_tensor(out=x2t[0:m, :], in0=x2t[0:m, :], scalar=rsf[0][0:m, q:q + 1],
                                                            in1=gf[0:m, :], op0=ALU.mult, op1=ALU.mult),
                           r=[R_rsf[q], R_gf], w=[R_x2])
                    sp.dma(self.out[r0:r0 + m, :], x2t[0:m, :], R_x2, r=[R_x2], w=[self.R_out])
                    o += m

            load(0)
            for ti in range(len(tiles)):
                if ti + 1 < len(tiles):
                    load(ti + 1)
                up(ti)
                down(ti)


def host_consts(S, p):
    nb = S // 128
    cst = np.zeros((128, CST_COLS), np.float32)

    def put(name, arr):
        a, b = _c[name]
        cst[:, a:b] = np.asarray(arr, np.float32).reshape(128, b - a)

    put("g1", p["norm1_g"][0].reshape(8, 128).T)
    put("g2", p["norm2_g"][0].reshape(8, 128).T)
    put("ga", p["attn_out_g"][0].reshape(4, 128).T)
    put("gs", p["ssd_norm_g"][0].reshape(4, 128).T)
    put("scw", p["ssd_conv_w"][0].reshape(5, 8, 128).transpose(2, 1, 0))
    put("scb", p["ssd_conv_b"][0].reshape(8, 128).T)
    put("fcw", p["ffn_conv_w"][0].reshape(3, 44, 128).transpose(2, 1, 0))
    put("fcb", p["ffn_conv_b"][0].reshape(44, 128).T)
    dtb = np.zeros((128, 1), np.float32)
    alog = np.zeros((128, 1), np.float32)
    sel = np.zeros((128, 3), np.float32)
    both_b = np.concatenate([p["ssd_dt_bias_fwd"][0], p["ssd_dt_bias_bwd"][0]])
    both_a = np.concatenate([p["ssd_a_log_fwd"][0], p["ssd_a_log_bwd"][0]])
    for grp in range(3):
        dtb[grp * 32:grp * 32 + 16, 0] = both_b
        alog[grp * 32:grp * 32 + 16, 0] = both_a
        sel[grp * 32:grp * 32 + 8, 0] = 1.0
        sel[grp * 32 + 8:grp * 32 + 16, 0] = -1.0
        sel[grp * 32 + 8:grp * 32 + 16, 1] = 1.0
        sel[grp * 32:grp * 32 + 8, 2] = 1.0
    put("dtb", dtb)
    put("alog", alog)
    put("sel", sel)
    put("dskip", np.tile(p["ssd_d"][0][None, :], (128, 1)))
    put("sink", np.tile(p["attn_sink"][0][None, :], (128, 1)))
    put("identf", np.eye(128, dtype=np.float32))
    cbf = np.zeros((128, CBF_COLS), np.float32)
    cbf[:, 0:128] = np.eye(128)
    pp = np.arange(128)[:, None]
    ff = np.arange(128)[None, :]
    mle = np.where(pp <= ff, 0.0, NEGM)
    mge = np.where(pp >= ff, 0.0, NEGM)
    cbf[:, 128:640] = np.tile(mle, (1, 4))
    cbf[:, 640:1152] = np.tile(mge, (1, 4))
    ind = np.zeros((128, 16, 128), np.float32)
    for grp in range(3):
        for hd in range(16):
            ind[grp * 32 + hd, hd, :] = 1.0
    cbf[:, 1152:] = ind.reshape(128, 2048)
    cbf = cbf.astype(ml_dtypes.bfloat16)
    pos = np.arange(S, dtype=np.float32)
    inv = (np.float32(500000.0) ** (-np.arange(0, 16, 2, dtype=np.float32) / np.float32(16))).astype(np.float32)
    ang = (pos[:, None] * inv[None, :]).astype(np.float32)
    tab = np.concatenate([np.cos(ang), np.sin(ang)], axis=1).astype(np.float32)
    rope = tab.reshape(nb, 128, 16).transpose(1, 0, 2).reshape(128, nb * 16).copy()
    gfc = np.ascontiguousarray(np.tile(np.asarray(p["final_norm_g"], np.float32)[None, :], (128, 1)))
    return cst, cbf, rope, gfc


_NC_CACHE = {}


def run(inputs, S, n_cores, dbg=False, phases="A1,D,AB,C,F"):
    p = {k: np.asarray(v) for k, v in inputs.items()}
    key = (S, dbg, phases)
    if key not in _NC_CACHE:
        _NC_CACHE[key] = KB(S, dbg=dbg, phases=phases).build()
    nc = _NC_CACHE[key]
    cst, cbf, rope, gfc = host_consts(S, p)
    shared = {
        "w_in": np.ascontiguousarray(p["w_in"][0], np.float32),
        "w_out": np.ascontiguousarray(p["w_out"][0], np.float32),
        "w_up": np.ascontiguousarray(p["w_up"][0], np.float32),
        "w_down": np.ascontiguousarray(p["w_down"][0], np.float32),
        "cst": cst, "cbf": cbf, "rope": rope, "gfc": gfc,
    }
    in_maps = []
    for c in range(n_cores):
        m = dict(shared)
        m["x"] = np.ascontiguousarray(p["x"][c], np.float32)
        in_maps.append(m)
    res = run_bass_kernel_spmd(nc, in_maps, core_ids=list(range(n_cores)))
    return res


def kernel(**inputs):
    res = run(inputs, SEQ, NCORES)
    return np.stack([np.asarray(r["out"], np.float32) for r in res.results], axis=0)
```
